# Optimizing a Trainium2 kernel written in Bass

```python
import math
import jax
import jax.numpy as jnp
from jax import lax
import numpy as np

D_MODEL = 1024
BATCH = 4
SEQ = 4096
DEPTH = 4
DEC_BATCH = 32
DEC_SEQ = 4
PAST_LEN = 8192
PAGE_SIZE = 128

N_A_LAYERS = DEPTH // 2
N_B_LAYERS = DEPTH - N_A_LAYERS
N_HEADS = 16
KV_HEADS = 4
GROUP = N_HEADS // KV_HEADS
HEAD_DIM = 64
CONV_W = 31
D_FF = -(-8 * D_MODEL // (3 * 256)) * 256
CMP_BLOCK = 32
SEL_BLOCK = 64
N_SEL = 16
WINDOW = 512
PHI_HIDDEN = 128
N_BUCKETS = 32
MAX_DISTANCE = 1024
Q_BLOCK = 128
EPS = 1e-6
NEG = -1e30
FORCE = 1e4

kernel_name = 'yoco_conformer_nsa_decoder_step'


def rmsnorm(x, g):
    x32 = x.astype(jnp.float32)
    y = x32 * lax.rsqrt(jnp.mean(x32 * x32, axis=-1, keepdims=True) + EPS) * g.astype(jnp.float32)
    return y.astype(x.dtype)


def layernorm(x, g, b):
    x32 = x.astype(jnp.float32)
    mu = jnp.mean(x32, axis=-1, keepdims=True)
    var = jnp.mean(jnp.square(x32 - mu), axis=-1, keepdims=True)
    y = (x32 - mu) * lax.rsqrt(var + EPS) * g.astype(jnp.float32) + b.astype(jnp.float32)
    return y.astype(x.dtype)


def ada_modulation(c, w, b):
    m = jax.nn.silu(c) @ w + b
    shift, scale, gate = jnp.split(m, 3, axis=-1)
    return shift[:, None, :], scale[:, None, :], gate[:, None, :]


def swiglu(h, w_gate, w_up, w_down):
    return (jax.nn.silu(h @ w_gate) * (h @ w_up)) @ w_down


def rel_bucket(dist):
    n = jnp.maximum(dist, 0)
    exact = N_BUCKETS // 2
    nf = jnp.maximum(n, 1).astype(jnp.float32)
    large = exact + (jnp.log(nf / exact) / math.log(MAX_DISTANCE / exact) * (N_BUCKETS - exact)).astype(jnp.int32)
    large = jnp.minimum(large, N_BUCKETS - 1)
    return jnp.where(n < exact, n, large)


def rel_bias_2d(table, q_pos, k_pos):
    b = table.astype(jnp.float32)[rel_bucket(q_pos[:, None] - k_pos[None, :])]
    return b.reshape(q_pos.shape[0], k_pos.shape[0], KV_HEADS, GROUP).transpose(2, 3, 0, 1)


def masked_softmax(s, mask, axis):
    s = jnp.where(mask, s, NEG)
    return jax.nn.softmax(s, axis=axis) * mask


def conformer_conv(h, prev, w_pw1, b_pw1, w_dw, b_dw, ln_g, ln_b, w_pw2, b_pw2):
    u = h @ w_pw1 + b_pw1
    a, g = jnp.split(u, 2, axis=-1)
    u = a * jax.nn.sigmoid(g)
    if prev is None:
        prev = jnp.zeros((u.shape[0], CONV_W - 1, u.shape[2]), u.dtype)
    full = jnp.concatenate([prev.astype(u.dtype), u], axis=1)
    y = lax.conv_general_dilated(full, w_dw[:, None, :].astype(u.dtype), (1,), 'VALID',
                                 dimension_numbers=('NWC', 'WIO', 'NWC'),
                                 feature_group_count=u.shape[-1]) + b_dw
    y = jax.nn.silu(layernorm(y, ln_g, ln_b))
    return y @ w_pw2 + b_pw2, full[:, -(CONV_W - 1):]


def shared_kv_rows(x, g_kv, w_kv):
    B, T, _ = x.shape
    return (rmsnorm(x, g_kv) @ w_kv).reshape(B, T, 3, 2, KV_HEADS, HEAD_DIM)


def gather_pages(cache, page_table):
    g = cache[page_table]
    return g.reshape(page_table.shape[0], page_table.shape[1] * PAGE_SIZE, *cache.shape[2:])


def compress(kv_full, w_phi1, b_phi1, w_phi2, b_phi2, pe_cmp):
    B, Tk = kv_full.shape[:2]
    n = Tk // CMP_BLOCK
    blk = kv_full[:, :n * CMP_BLOCK].reshape(B, n, CMP_BLOCK, 2, KV_HEADS, HEAD_DIM)
    blk = blk + pe_cmp.transpose(1, 0, 2)[:, :, None, :]
    blk = blk.transpose(0, 1, 3, 4, 2, 5).reshape(B, n, 2, KV_HEADS, CMP_BLOCK * HEAD_DIM)
    hid = jax.nn.silu(jnp.einsum('bnegf,efh->bnegh', blk, w_phi1) + b_phi1[:, None, :])
    out = jnp.einsum('bnegh,ehd->bnegd', hid, w_phi2) + b_phi2[:, None, :]
    end = jnp.arange(n) * CMP_BLOCK + CMP_BLOCK - 1
    return out[:, :, 0], out[:, :, 1], end


def selection_blocks(kv_full):
    B, Tk = kv_full.shape[:2]
    n_blk = -(-Tk // SEL_BLOCK)
    kv = jnp.pad(kv_full, ((0, 0), (0, n_blk * SEL_BLOCK - Tk), (0, 0), (0, 0), (0, 0)))
    kv = kv.reshape(B, n_blk, SEL_BLOCK, 2, KV_HEADS, HEAD_DIM).transpose(3, 0, 4, 1, 2, 5)
    return kv[0], kv[1]


def nsa_chunk(q, gates, q_pos, cmp_k, cmp_v, cmp_end, blk_k, blk_v, win_k, win_v, win_pos, table):
    B, Q = q.shape[:2]
    s_c = jnp.einsum('bqgrd,bngd->bgrqn', q, cmp_k).astype(jnp.float32) + rel_bias_2d(table, q_pos, cmp_end)
    p_c = masked_softmax(s_c, cmp_end[None, :] <= q_pos[:, None], -1)
    o_c = jnp.einsum('bgrqn,bngd->bqgrd', p_c.astype(cmp_v.dtype), cmp_v)
    n_cmp = cmp_end.shape[0]
    n_blk = blk_k.shape[2]
    ratio = SEL_BLOCK // CMP_BLOCK
    imp = jnp.pad(p_c.sum(axis=2), ((0, 0), (0, 0), (0, 0), (0, n_blk * ratio - n_cmp)))
    imp = imp.reshape(B, KV_HEADS, Q, n_blk, ratio).sum(-1)
    j = jnp.arange(n_blk)[None, :]
    jq = (q_pos // SEL_BLOCK)[:, None]
    forced = (j == 0) | (j == jq) | (j == jq - 1)
    score = jnp.where(j <= jq, jnp.where(forced, FORCE, imp), -1.0)
    top_s, idx = lax.top_k(score, min(N_SEL, n_blk))
    bi = jnp.arange(B)[:, None, None, None]
    gi = jnp.arange(KV_HEADS)[None, :, None, None]
    ks = blk_k[bi, gi, idx]
    vs = blk_v[bi, gi, idx]
    kpos = idx[..., None] * SEL_BLOCK + jnp.arange(SEL_BLOCK)
    qp = q_pos[None, None, :, None, None]
    smask = (top_s >= 0)[..., None] & (kpos <= qp)
    table_g = table.astype(jnp.float32).reshape(N_BUCKETS, KV_HEADS, GROUP).transpose(1, 0, 2)
    bias_s = table_g[jnp.arange(KV_HEADS)[None, :, None, None, None], rel_bucket(qp - kpos)]
    s_s = jnp.einsum('bqgrd,bgqksd->bgqksr', q, ks).astype(jnp.float32) + bias_s
    nk = ks.shape[3] * SEL_BLOCK
    p_s = masked_softmax(s_s.reshape(B, KV_HEADS, Q, nk, GROUP), smask.reshape(B, KV_HEADS, Q, nk)[..., None], 3)
    o_s = jnp.einsum('bgqnr,bgqnd->bqgrd', p_s.astype(vs.dtype), vs.reshape(B, KV_HEADS, Q, nk, HEAD_DIM))
    d = q_pos[:, None] - win_pos[None, :]
    wmask = (d >= 0) & (d < WINDOW) & (win_pos[None, :] >= 0)
    s_w = jnp.einsum('bqgrd,bkgd->bgrqk', q, win_k).astype(jnp.float32) + rel_bias_2d(table, q_pos, win_pos)
    p_w = masked_softmax(s_w, wmask, -1)
    o_w = jnp.einsum('bgrqk,bkgd->bqgrd', p_w.astype(win_v.dtype), win_v)
    return gates[..., 0:1] * o_c + gates[..., 1:2] * o_s + gates[..., 2:3] * o_w


def nsa_mixer(h, w_qg, w_o, table, cmp_k, cmp_v, cmp_end, blk_k, blk_v, win_all, q_pos0, is_prompt):
    B, T, _ = h.shape
    qg = h @ w_qg
    q = qg[..., :N_HEADS * HEAD_DIM].reshape(B, T, KV_HEADS, GROUP, HEAD_DIM) * (HEAD_DIM ** -0.5)
    gates = jax.nn.sigmoid(qg[..., N_HEADS * HEAD_DIM:]).reshape(B, T, KV_HEADS, GROUP, 3)
    if is_prompt:
        nb = T // Q_BLOCK
        wpad = jnp.pad(win_all, ((0, 0), (WINDOW, 0), (0, 0), (0, 0), (0, 0)))

        def body(args):
            qc, gc, i = args
            start = i * Q_BLOCK
            q_pos = start + jnp.arange(Q_BLOCK)
            wkv = lax.dynamic_slice_in_dim(wpad, start, Q_BLOCK + WINDOW, axis=1)
            w_pos = start - WINDOW + jnp.arange(Q_BLOCK + WINDOW)
            return nsa_chunk(qc, gc, q_pos, cmp_k, cmp_v, cmp_end, blk_k, blk_v,
                             wkv[:, :, 0], wkv[:, :, 1], w_pos, table)

        qc = q.reshape(B, nb, Q_BLOCK, KV_HEADS, GROUP, HEAD_DIM).swapaxes(0, 1)
        gc = gates.reshape(B, nb, Q_BLOCK, KV_HEADS, GROUP, 3).swapaxes(0, 1)
        o = lax.map(body, (qc, gc, jnp.arange(nb)))
        o = o.swapaxes(0, 1).reshape(B, T, N_HEADS * HEAD_DIM)
    else:
        q_pos = q_pos0 + jnp.arange(T)
        w_pos = q_pos0 + T - win_all.shape[1] + jnp.arange(win_all.shape[1])
        o = nsa_chunk(q, gates, q_pos, cmp_k, cmp_v, cmp_end, blk_k, blk_v,
                      win_all[:, :, 0], win_all[:, :, 1], w_pos, table)
        o = o.reshape(B, T, N_HEADS * HEAD_DIM)
    return o @ w_o


def trunk(x, c, is_prompt, cache_kv_cmp, cache_kv_sel, cache_kv_win, state_conv, page_table, P):
    B, T, _ = x.shape
    conv_states = []
    for l in range(DEPTH):
        if l == N_A_LAYERS:
            rows = shared_kv_rows(x, P['g_kv'], P['w_kv'])
            if is_prompt:
                q_pos0 = 0
                cmp_full = rows[:, :, 0]
                sel_full = rows[:, :, 1]
                win_all = rows[:, :, 2]
                win_state = win_all[:, -min(WINDOW, T):]
            else:
                q_pos0 = page_table.shape[1] * PAGE_SIZE
                cmp_full = jnp.concatenate([gather_pages(cache_kv_cmp, page_table), rows[:, :, 0]], axis=1)
                sel_full = jnp.concatenate([gather_pages(cache_kv_sel, page_table), rows[:, :, 1]], axis=1)
                win_all = jnp.concatenate([cache_kv_win, rows[:, :, 2]], axis=1)
                win_state = win_all[:, -cache_kv_win.shape[1]:]
            cmp_k, cmp_v, cmp_end = compress(cmp_full, P['w_phi1'], P['b_phi1'], P['w_phi2'], P['b_phi2'], P['pe_cmp'])
            blk_k, blk_v = selection_blocks(sel_full)
        shift, scale, gate = ada_modulation(c, P['w_ada'][l, 0], P['b_ada'][l, 0])
        h = rmsnorm(x, P['norm_g'][l, 0]) * (1 + scale) + shift
        if l < N_A_LAYERS:
            prev = None if is_prompt else state_conv[l]
            out, st = conformer_conv(h, prev, P['w_pw1'][l], P['b_pw1'][l], P['w_dw'][l], P['b_dw'][l],
                                     P['ln_g'][l], P['ln_b'][l], P['w_pw2'][l], P['b_pw2'][l])
            conv_states.append(st)
        else:
            lb = l - N_A_LAYERS
            out = nsa_mixer(h, P['w_qg'][lb], P['w_o'][lb], P['rel_table'], cmp_k, cmp_v, cmp_end,
                            blk_k, blk_v, win_all, q_pos0, is_prompt)
        x = x + gate * out
        shift, scale, gate = ada_modulation(c, P['w_ada'][l, 1], P['b_ada'][l, 1])
        h = rmsnorm(x, P['norm_g'][l, 1]) * (1 + scale) + shift
        x = x + gate * swiglu(h, P['w_gate'][l], P['w_up'][l], P['w_down'][l])
    y = rmsnorm(x, P['final_g'])
    return y, rows[:, :, 0], rows[:, :, 1], win_state, jnp.stack(conv_states)


def setup_inputs(seed: int = 0) -> dict:
    key = jax.random.key(seed)
    ks = jax.random.split(key, 48)
    it = iter(range(48))

    def nrm(shape, scale):
        return jax.random.normal(ks[next(it)], shape, jnp.float32) * scale

    n_pages = PAST_LEN // PAGE_SIZE
    n_pool = (DEC_BATCH * n_pages * 5 + 3) // 4
    win_buf = min(WINDOW, PAST_LEN)
    qg_width = N_HEADS * HEAD_DIM + 3 * N_HEADS
    kv_width = 3 * 2 * KV_HEADS * HEAD_DIM
    perm = jax.random.permutation(ks[next(it)], n_pool)
    page_table = perm[:DEC_BATCH * n_pages].reshape(DEC_BATCH, n_pages).astype(jnp.int32)
    return {
        'x_prompt': nrm((BATCH, SEQ, D_MODEL), 1.0),
        'x_sample': nrm((DEC_BATCH, DEC_SEQ, D_MODEL), 1.0),
        'c_prompt': nrm((BATCH, D_MODEL), 1.0),
        'c_sample': nrm((DEC_BATCH, D_MODEL), 1.0),
        'cache_kv_cmp': nrm((n_pool, PAGE_SIZE, 2, KV_HEADS, HEAD_DIM), 1.0),
        'cache_kv_sel': nrm((n_pool, PAGE_SIZE, 2, KV_HEADS, HEAD_DIM), 1.0),
        'cache_kv_win': nrm((DEC_BATCH, win_buf, 2, KV_HEADS, HEAD_DIM), 1.0),
        'state_conv': nrm((N_A_LAYERS, DEC_BATCH, CONV_W - 1, D_MODEL), 0.5),
        'page_table': page_table,
        'w_ada': nrm((DEPTH, 2, D_MODEL, 3 * D_MODEL), 0.5 * D_MODEL ** -0.5),
        'b_ada': nrm((DEPTH, 2, 3 * D_MODEL), 0.02),
        'norm_g': 1.0 + nrm((DEPTH, 2, D_MODEL), 0.05),
        'w_pw1': nrm((N_A_LAYERS, D_MODEL, 2 * D_MODEL), D_MODEL ** -0.5),
        'b_pw1': nrm((N_A_LAYERS, 2 * D_MODEL), 0.02),
        'w_dw': nrm((N_A_LAYERS, CONV_W, D_MODEL), CONV_W ** -0.5),
        'b_dw': nrm((N_A_LAYERS, D_MODEL), 0.02),
        'ln_g': 1.0 + nrm((N_A_LAYERS, D_MODEL), 0.05),
        'ln_b': nrm((N_A_LAYERS, D_MODEL), 0.02),
        'w_pw2': nrm((N_A_LAYERS, D_MODEL, D_MODEL), D_MODEL ** -0.5),
        'b_pw2': nrm((N_A_LAYERS, D_MODEL), 0.02),
        'g_kv': 1.0 + nrm((D_MODEL,), 0.05),
        'w_kv': nrm((D_MODEL, kv_width), D_MODEL ** -0.5),
        'w_phi1': nrm((2, CMP_BLOCK * HEAD_DIM, PHI_HIDDEN), (CMP_BLOCK * HEAD_DIM) ** -0.5),
        'b_phi1': nrm((2, PHI_HIDDEN), 0.02),
        'w_phi2': nrm((2, PHI_HIDDEN, HEAD_DIM), PHI_HIDDEN ** -0.5),
        'b_phi2': nrm((2, HEAD_DIM), 0.02),
        'pe_cmp': nrm((2, CMP_BLOCK, HEAD_DIM), 0.1),
        'w_qg': nrm((N_B_LAYERS, D_MODEL, qg_width), D_MODEL ** -0.5),
        'w_o': nrm((N_B_LAYERS, N_HEADS * HEAD_DIM, D_MODEL), (N_HEADS * HEAD_DIM) ** -0.5),
        'rel_table': nrm((N_BUCKETS, N_HEADS), 0.5),
        'w_gate': nrm((DEPTH, D_MODEL, D_FF), D_MODEL ** -0.5),
        'w_up': nrm((DEPTH, D_MODEL, D_FF), D_MODEL ** -0.5),
        'w_down': nrm((DEPTH, D_FF, D_MODEL), D_FF ** -0.5),
        'final_g': 1.0 + nrm((D_MODEL,), 0.05),
    }


def reference(x_prompt, x_sample, c_prompt, c_sample, cache_kv_cmp, cache_kv_sel, cache_kv_win, state_conv,
              page_table, w_ada, b_ada, norm_g, w_pw1, b_pw1, w_dw, b_dw, ln_g, ln_b, w_pw2, b_pw2,
              g_kv, w_kv, w_phi1, b_phi1, w_phi2, b_phi2, pe_cmp, w_qg, w_o, rel_table,
              w_gate, w_up, w_down, final_g):
    P = dict(w_ada=w_ada, b_ada=b_ada, norm_g=norm_g, w_pw1=w_pw1, b_pw1=b_pw1, w_dw=w_dw, b_dw=b_dw,
             ln_g=ln_g, ln_b=ln_b, w_pw2=w_pw2, b_pw2=b_pw2, g_kv=g_kv, w_kv=w_kv, w_phi1=w_phi1,
             b_phi1=b_phi1, w_phi2=w_phi2, b_phi2=b_phi2, pe_cmp=pe_cmp, w_qg=w_qg, w_o=w_o,
             rel_table=rel_table, w_gate=w_gate, w_up=w_up, w_down=w_down, final_g=final_g)
    y_prompt, cmp_p, sel_p, win_p, conv_p = trunk(x_prompt, c_prompt, True, None, None, None, None, None, P)
    y_sample, cmp_s, sel_s, win_s, conv_s = trunk(x_sample, c_sample, False, cache_kv_cmp, cache_kv_sel,
                                                  cache_kv_win, state_conv, page_table, P)
    return (y_prompt, y_sample, cmp_p, cmp_s, sel_p, sel_s, win_p, win_s, conv_p, conv_s)
```

```python
import numpy as np
from contextlib import ExitStack
import concourse.bass as bass
import concourse.mybir as mybir
from concourse.bass_utils import run_bass_kernel_spmd

F32 = mybir.dt.float32
BF16 = mybir.dt.bfloat16
I32 = mybir.dt.int32
U32 = mybir.dt.uint32
AF = mybir.ActivationFunctionType
ALU = mybir.AluOpType
AX = mybir.AxisListType

EPOCH = 30000
N_DMA_SEMS = 40
STRICT = True

D = 1024
KC = 8
DFF = 2816
FC = 22
TV = 4096
NG = 8
GN = 512
NS = 4
ST = 4
EPS = 1e-6
DEBUG = False
NEGV = -30000.0
QH0 = 2048


def _bucket_thresholds():
    import math
    th = []
    nb, ex, md = 32, 16, 1024
    def bucket(n):
        if n < ex:
            return n
        v = np.float32(np.log(np.float32(n) / np.float32(ex))) / np.float32(math.log(md / ex)) * np.float32(nb - ex)
        return min(ex + int(v), nb - 1)
    for b in range(1, 32):
        n = 0
        while bucket(n) < b:
            n += 1
        th.append(n)
    return th


TH = _bucket_thresholds()

R_BADA, R_NG, R_BPW1, R_WDW, R_BDW, R_LNG, R_LNB, R_BPW2, R_GKV, R_FG, NVEC = 0, 24, 32, 36, 98, 100, 102, 104, 106, 107, 108


class Buf:
    __slots__ = ("name", "w", "r")

    def __init__(self, name):
        self.name = name
        self.w = None
        self.r = {}


class Emitter:
    ENGS = ("pe", "act", "dve", "pool", "sp")

    def __init__(self, nc, stack):
        self.nc = nc
        self.stack = stack
        self.prog = {e: [] for e in self.ENGS}
        self.cnt = {e: 0 for e in self.ENGS}
        self.esems = {e: [] for e in self.ENGS}
        self.waited = {e: {} for e in self.ENGS}
        self.dma_sems = [stack.enter_context(nc.semaphore("dq%d" % i)) for i in range(N_DMA_SEMS)]
        self.dma_tot = [0] * N_DMA_SEMS
        self.dma_rr = 0
        self.nbuf = 0

    def buf(self, name=None):
        self.nbuf += 1
        return Buf(name or "b%d" % self.nbuf)

    def _esem(self, e, epoch):
        while len(self.esems[e]) <= epoch:
            self.esems[e].append(self.stack.enter_context(self.nc.semaphore("s_%s%d" % (e, len(self.esems[e])))))
        return self.esems[e][epoch]

    def _tok_sem(self, tok):
        if tok[0] == "dma":
            return self.dma_sems[tok[1]], tok[2], ("dma", tok[1])
        e, idx = tok
        return self._esem(e, idx // EPOCH), idx % EPOCH + 1, (e, idx // EPOCH)

    def _collect(self, e, reads, writes, is_dma):
        toks = set()
        same_ok = set()
        for b in reads:
            for w in (b.w or ()):
                toks.add(w)
                if e == "pe":
                    same_ok.add(w)
        for b in writes:
            for w in (b.w or ()):
                if w not in toks:
                    same_ok.add(w)
                toks.add(w)
            for t in b.r.values():
                if t not in toks:
                    same_ok.add(t)
                toks.add(t)
        waits = []
        for t in toks:
            if t[0] != "dma" and t[0] == e and not is_dma and (e == "pe" or (STRICT is False and t in same_ok)):
                continue
            sem, val, key = self._tok_sem(t)
            if self.waited[e].get(key, 0) >= val:
                continue
            self.waited[e][key] = val
            waits.append((sem, val))
        return waits

    def op(self, e, fn, reads=(), writes=()):
        waits = self._collect(e, reads, writes, False)
        idx = self.cnt[e]
        self.cnt[e] += 1
        sem = self._esem(e, idx // EPOCH)
        self.prog[e].append((waits, fn, sem, 1))
        tok = (e, idx)
        for b in reads:
            b.r[e] = tok
        for b in writes:
            b.w = (tok,)
            b.r = {}
        return tok

    def dma(self, e, fn, reads=(), writes=()):
        s = self.dma_rr
        self.dma_rr = (self.dma_rr + 1) % N_DMA_SEMS
        waits = self._collect(e, reads, writes, True)
        prev = self.dma_tot[s]
        if prev > 0 and self.waited[e].get(("dma", s), 0) < prev:
            self.waited[e][("dma", s)] = prev
            waits.append((self.dma_sems[s], prev))
        self.dma_tot[s] = prev + 16
        tok = ("dma", s, prev + 16)
        self.prog[e].append((waits, fn, self.dma_sems[s], 16))
        for b in reads:
            b.r["dma%d" % s] = tok
        for b in writes:
            if b.w and not b.r and all(t[0] == "dma" for t in b.w):
                b.w = b.w + (tok,)
            else:
                b.w = (tok,)
            b.r = {}
        return tok

    def barrier(self):
        toks = []
        for e2 in self.ENGS:
            if self.cnt[e2] > 0:
                toks.append((e2, self.cnt[e2] - 1))
        for e in self.ENGS:
            waits = []
            for t in toks:
                if t[0] == e:
                    continue
                sem, val, key = self._tok_sem(t)
                if self.waited[e].get(key, 0) < val:
                    self.waited[e][key] = val
                    waits.append((sem, val))
            for s in range(N_DMA_SEMS):
                if self.dma_tot[s] > 0 and self.waited[e].get(("dma", s), 0) < self.dma_tot[s]:
                    self.waited[e][("dma", s)] = self.dma_tot[s]
                    waits.append((self.dma_sems[s], self.dma_tot[s]))
            self.prog[e].append((waits, None, None, 0))

    def finish(self):
        waits = []
        for s in range(N_DMA_SEMS):
            if self.dma_tot[s] > 0:
                waits.append((self.dma_sems[s], self.dma_tot[s]))
        self.prog["sp"].append((waits, None, None, 0))
        engmap = {"pe": "tensor", "act": "scalar", "dve": "vector", "pool": "gpsimd", "sp": "sync"}
        with self.nc.Block() as block:
            for e in self.ENGS:
                prog = self.prog[e]

                def body(eng, prog=prog):
                    for waits, fn, sem, inc in prog:
                        for (ws, wv) in waits:
                            eng.wait_ge(ws, wv)
                        if fn is not None:
                            fn(eng).then_inc(sem, inc)

                getattr(block, engmap[e])(body)


class Grp:
    def __init__(self, S, n, seqs):
        self.S, self.n, self.N, self.seqs = S, n, S * n, seqs


def build(stage=9):
    nc = bass.Bass("TRN2", target_bir_lowering=False)

    def din(name, shape, dt=F32):
        return nc.dram_tensor(name, list(shape), dt, kind="ExternalInput").ap()

    def dout(name, shape, dt=F32):
        return nc.dram_tensor(name, list(shape), dt, kind="ExternalOutput").ap()

    xp = din("xp", [TV, D])
    xs = din("xs", [NS * ST, D])
    cvec = din("cvec", [1 + NS, D])
    vecs = din("vecs", [NVEC, D])
    vec2 = din("vec2", [67, 128])
    validt = din("validt", [1, TV])
    vbc_d = din("vbc", [1, 128])
    vbw_d = din("vbw", [1, TV])
    vlim_d = din("vlim", [1, 64])
    f0_d = din("f0", [1, 64])
    stconv = din("stconv", [2, NS, 30, D])
    w_ada = din("w_ada", [4, 2, D, 3 * D])
    w_pw1 = din("w_pw1", [2, D, 2 * D])
    w_pw2 = din("w_pw2", [2, D, D])
    w_kv = din("w_kv", [D, 1536])
    w_gate = din("w_gate", [4, D, DFF])
    w_up = din("w_up", [4, D, DFF])
    w_down = din("w_down", [4, DFF, D])
    w_phi1 = din("w_phi1", [2, 2048, 128])
    w_phi2 = din("w_phi2", [2, 128, 64])
    b_phi2 = din("b_phi2", [2, 64])
    rel_table = din("rel_table", [32, 16])
    w_qg = din("w_qg", [2, D, 1072])
    w_o = din("w_o", [2, D, D])
    cache_cmp = din("cache_cmp", [2560 * 128, 512])
    cache_sel = din("cache_sel", [2560 * 128, 512])
    cache_win = din("cache_win", [NS, 512, 512])
    ptab = din("ptab", [NS, 64], I32)

    rows_p = dout("rows_p", [2048, 1536])
    rows_s = dout("rows_s", [NS * ST, 1536])
    conv_p = dout("conv_p", [2, 30, D])
    conv_s = dout("conv_s", [2, NS, 30, D])
    y_p = dout("y_p", [2048, D])
    y_s = dout("y_s", [NS * ST, D])
    win_s = dout("win_s", [NS, 512, 512])

    XS = nc.dram_tensor("XS", [128, KC, 2048], F32, kind="Internal").ap()
    GSH = nc.dram_tensor("GSH", [16, 1920], F32, kind="Internal")
    GSL = nc.dram_tensor("GSL", [16, 1920], F32, kind="Internal")
    WS = nc.dram_tensor("WS", [16, 128, 1664], F32, kind="Internal").ap()

    with ExitStack() as st:
        em = Emitter(nc, st)

        def sb(name, shape, dt=F32):
            return st.enter_context(nc.sbuf_tensor(name, list(shape), dt))

        def dbg(name, ap, bufs):
            if not DEBUG:
                return
            o = nc.dram_tensor("dbg_" + name, list(ap.shape), ap.dtype, kind="ExternalOutput").ap()
            em.dma("sp", lambda e: e.dma_start(out=o, in_=ap), reads=bufs)

        ident = sb("ident", [128, 128])
        identb = sb("identb", [128, 128], BF16)
        antib = sb("antib", [128, 128], BF16)
        onesb = sb("onesb", [128, 128], BF16)
        epsc = sb("epsc", [128, 1])
        scr0 = sb("scr0", [128, 128])
        b_const = em.buf("const")
        b_scr0 = em.buf()
        em.op("pool", lambda e: e.iota(scr0[:], [[1, 128]], base=0, channel_multiplier=-1, allow_small_or_imprecise_dtypes=True), writes=[b_scr0])
        em.op("dve", lambda e: e.tensor_scalar(out=ident[:], in0=scr0[:], scalar1=0.0, scalar2=None, op0=ALU.is_equal), reads=[b_scr0], writes=[b_const])
        em.op("dve", lambda e: e.tensor_scalar(out=identb[:], in0=scr0[:], scalar1=0.0, scalar2=None, op0=ALU.is_equal), reads=[b_scr0], writes=[b_const])
        em.op("dve", lambda e: e.memset(onesb[:], 1.0), writes=[b_const])
        em.op("dve", lambda e: e.memset(epsc[:], EPS), writes=[b_const])
        em.op("pool", lambda e: e.iota(scr0[:], [[1, 128]], base=-127, channel_multiplier=1, allow_small_or_imprecise_dtypes=True), reads=[b_const], writes=[b_scr0])
        em.op("dve", lambda e: e.tensor_scalar(out=antib[:], in0=scr0[:], scalar1=0.0, scalar2=None, op0=ALU.is_equal), reads=[b_scr0], writes=[b_const])

        NPS = 8
        pst = [st.enter_context(nc.psum_tensor("ps%d" % i, [128, 512], F32)) for i in range(NPS)]
        psb = [em.buf("ps%d" % i) for i in range(NPS)]
        ps_rr = [0]
        ps_nrot = [NPS]

        def ps_next():
            i = ps_rr[0] % ps_nrot[0]
            ps_rr[0] = (i + 1) % ps_nrot[0]
            return pst[i], psb[i]

        def rot(name, shape, dt, n):
            tiles = [sb("%s%d" % (name, i), shape, dt) for i in range(n)]
            bufs = [em.buf("%s%d" % (name, i)) for i in range(n)]
            c = [0]

            def nxt():
                i = c[0]
                c[0] = (i + 1) % n
                return tiles[i], bufs[i]
            return nxt

        bigt = rot("bigt", [128, 1536], F32, 2)
        tmpA = rot("tmpA", [128, GN], F32, 4)
        sqt = rot("sqt", [128, GN], BF16, 2)

        TABf = sb("TABf", [32, 16])
        TABh = sb("TABh", [32, 16], BF16)
        TABl = sb("TABl", [32, 16], BF16)
        CH = sb("CH", [128, 16])
        BCALL = sb("BCALL", [128, 16, 32])
        VBC = sb("VBC", [1, 128], BF16)
        VBW = sb("VBW", [1, 1152], BF16)
        VLIM = sb("VLIM", [128, 64])
        F0 = sb("F0", [128, 64])
        NEGM = sb("NEGM", [128, 16])
        NEGMC = sb("NEGMC", [128, 16])
        b_tab = em.buf("tab")
        b_bias = em.buf("bias")
        em.dma("sp", lambda e: e.dma_start(out=TABf[:], in_=rel_table), writes=[b_tab])
        em.dma("sp", lambda e: e.dma_start(out=CH[:], in_=rel_table[31:32, :].broadcast_to([128, 16])), writes=[b_bias])
        em.dma("pool", lambda e: e.dma_start(out=VBC[:], in_=vbc_d), writes=[b_bias])
        em.dma("pool", lambda e: e.dma_start(out=VBW[:], in_=vbw_d[:, 1536:1536 + 1152]), writes=[b_bias])
        em.dma("sp", lambda e: e.dma_start(out=VLIM[:], in_=vlim_d.broadcast_to([128, 64])), writes=[b_bias])
        em.dma("sp", lambda e: e.dma_start(out=F0[:], in_=f0_d.broadcast_to([128, 64])), writes=[b_bias])
        em.op("dve", lambda e: e.memset(NEGM[:], 0.0), writes=[b_bias])
        em.op("dve", lambda e: e.tensor_copy(out=NEGMC[:], in_=CH[:]), reads=[b_bias], writes=[b_bias])
        em.op("dve", lambda e: e.tensor_copy(out=TABh[:], in_=TABf[:]), reads=[b_tab], writes=[b_tab])
        em.op("dve", lambda e: e.tensor_tensor(out=TABf[:], in0=TABf[:], in1=TABh[:], op=ALU.subtract), reads=[b_tab], writes=[b_tab])
        em.op("dve", lambda e: e.tensor_copy(out=TABl[:], in_=TABf[:]), reads=[b_tab], writes=[b_tab])
        GL_ = 1920
        CW = 384
        with ExitStack() as st3:
            def sb3(name, shape, dt=F32):
                return st3.enter_context(nc.sbuf_tensor(name, list(shape), dt))
            Drow = sb3("Drow", [32, CW])
            BKT = sb3("BKT", [32, CW])
            OH = sb3("OH", [32, CW], BF16)
            PIDX = sb3("PIDX", [32, 1])
            GT = sb3("GT", [16, CW])
            GTh = sb3("GTh", [16, CW], BF16)
            GTh32 = sb3("GTh32", [16, CW])
            VM = sb3("VM", [16, CW])
            NA = sb3("NA", [16, CW])
            bhk = sb3("bhk", [128, 2, 1664], BF16)
            wst = sb3("wst", [128, 1664])
            b_g = em.buf("gtab")
            b_bhk = em.buf()
            b_wst = em.buf()
            b_gs = em.buf("gs")
            em.op("pool", lambda e: e.iota(PIDX[:], [[1, 1]], base=0, channel_multiplier=1, allow_small_or_imprecise_dtypes=True), writes=[b_g])
            for ci in range(5):
                if ci < 3:
                    i0_, d0, vlo, vhi = ci * CW, 1023 - ci * CW, 0, 1023
                else:
                    i0_, d0, vlo, vhi = (ci - 3) * CW, 639 - (ci - 3) * CW, 128, 639
                col0 = ci * CW
                em.op("pool", lambda e, d0=d0: e.iota(Drow[:], [[-1, CW]], base=d0, channel_multiplier=0, allow_small_or_imprecise_dtypes=True), reads=[b_g], writes=[b_g])
                for i_, th in enumerate(TH):
                    if i_ == 0:
                        em.op("dve", lambda e, th=th: e.tensor_scalar(out=BKT[:], in0=Drow[:], scalar1=float(th), scalar2=None, op0=ALU.is_ge), reads=[b_g], writes=[b_g])
                    else:
                        em.op("dve", lambda e, th=th: e.scalar_tensor_tensor(out=BKT[:], in0=Drow[:], scalar=float(th), in1=BKT[:], op0=ALU.is_ge, op1=ALU.add), reads=[b_g], writes=[b_g])
                em.op("dve", lambda e: e.tensor_scalar(out=OH[:], in0=BKT[:], scalar1=PIDX[:, 0:1], scalar2=None, op0=ALU.is_equal), reads=[b_g], writes=[b_g])
                a_ = max(vlo, i0_) - i0_
                b_ = min(vhi + 1, i0_ + CW) - i0_
                em.op("dve", lambda e: e.memset(VM[:], 0.0), reads=[b_g], writes=[b_g])
                if b_ > a_:
                    em.op("dve", lambda e, a_=a_, b_=b_: e.memset(VM[:, a_:b_], 1.0), reads=[b_g], writes=[b_g])
                em.op("dve", lambda e: e.tensor_scalar(out=NA[:], in0=VM[:], scalar1=-NEGV, scalar2=NEGV, op0=ALU.mult, op1=ALU.add), reads=[b_g], writes=[b_g])
                pt, pb = ps_next()
                em.op("pe", lambda e, pt=pt: e.matmul(pt[0:16, 0:CW], lhsT=TABh[:], rhs=OH[:], start=True, stop=False), reads=[b_tab, b_g], writes=[pb])
                em.op("pe", lambda e, pt=pt: e.matmul(pt[0:16, 0:CW], lhsT=TABl[:], rhs=OH[:], start=False, stop=True), reads=[b_tab, b_g], writes=[pb])
                em.op("dve", lambda e, pt=pt: e.tensor_tensor(out=GT[:], in0=pt[0:16, 0:CW], in1=VM[:], op=ALU.mult), reads=[pb, b_g], writes=[b_g])
                em.op("dve", lambda e: e.tensor_tensor(out=GT[:], in0=GT[:], in1=NA[:], op=ALU.add), reads=[b_g], writes=[b_g])
                em.op("dve", lambda e: e.tensor_copy(out=GTh[:], in_=GT[:]), reads=[b_g], writes=[b_g])
                em.op("dve", lambda e: e.tensor_copy(out=GTh32[:], in_=GTh[:]), reads=[b_g], writes=[b_g])
                em.op("dve", lambda e: e.tensor_tensor(out=GT[:], in0=GT[:], in1=GTh32[:], op=ALU.subtract), reads=[b_g], writes=[b_g])
                em.dma("sp", lambda e, col0=col0: e.dma_start(out=GSH.ap()[:, col0:col0 + CW], in_=GTh32[:]), reads=[b_g], writes=[b_gs])
                em.dma("sp", lambda e, col0=col0: e.dma_start(out=GSL.ap()[:, col0:col0 + CW], in_=GT[:]), reads=[b_g], writes=[b_gs])
            b_ws = em.buf("ws")
            for h in range(16):
                for hl, GS_ in enumerate((GSH, GSL)):
                    em.dma("pool", lambda e, hl=hl, GS_=GS_, h=h: e.dma_start(out=bhk[:, hl, 0:1024], in_=bass.AP(GS_, h * GL_, [[1, 128], [1, 1024]])), reads=[b_gs], writes=[b_bhk])
                    em.dma("pool", lambda e, hl=hl, GS_=GS_, h=h: e.dma_start(out=bhk[:, hl, 1024:1664], in_=bass.AP(GS_, h * GL_ + 1152, [[1, 128], [1, 640]])), reads=[b_gs], writes=[b_bhk])
                for c0 in range(0, 1664, 512):
                    w_ = min(512, 1664 - c0)
                    pt, pb = ps_next()
                    em.op("pe", lambda e, c0=c0, w_=w_, pt=pt: e.matmul(pt[:, 0:w_], lhsT=antib[:], rhs=bhk[:, 0, c0:c0 + w_], start=True, stop=False), reads=[b_bhk, b_const], writes=[pb])
                    em.op("pe", lambda e, c0=c0, w_=w_, pt=pt: e.matmul(pt[:, 0:w_], lhsT=antib[:], rhs=bhk[:, 1, c0:c0 + w_], start=False, stop=True), reads=[b_bhk, b_const], writes=[pb])
                    em.op("act", lambda e, c0=c0, w_=w_, pt=pt: e.copy(out=wst[:, c0:c0 + w_], in_=pt[:, 0:w_]), reads=[pb], writes=[b_wst])
                em.op("dve", lambda e, h=h: e.tensor_copy(out=BCALL[:, h, :], in_=wst[:, 31:1024:32]), reads=[b_wst], writes=[b_bias])
                em.dma("sp", lambda e, h=h: e.dma_start(out=WS[h], in_=wst[:]), reads=[b_wst], writes=[b_ws])
            em.barrier()

        VEC = sb("VEC", [128, KC, NVEC])
        b_vec = em.buf("vec")
        vrows, b_vrows = bigt()
        em.dma("sp", lambda e: e.dma_start(out=vrows[0:NVEC, 0:D], in_=vecs), writes=[b_vrows])
        for kc in range(KC):
            pt, pb = ps_next()
            em.op("pe", lambda e, kc=kc, pt=pt: e.transpose(pt[:, 0:NVEC], vrows[0:NVEC, kc * 128:(kc + 1) * 128], ident[0:NVEC, 0:NVEC]), reads=[b_vrows, b_const], writes=[pb])
            em.op("dve", lambda e, kc=kc, pt=pt: e.tensor_copy(out=VEC[:, kc, :], in_=pt[:, 0:NVEC]), reads=[pb], writes=[b_vec])

        def vcol(r, kc):
            return VEC[:, kc, r:r + 1]

        VEC2 = sb("VEC2", [128, 67])
        v2rows, b_v2rows = bigt()
        em.dma("sp", lambda e: e.dma_start(out=v2rows[0:67, 0:128], in_=vec2), writes=[b_v2rows])
        pt, pb = ps_next()
        em.op("pe", lambda e, pt=pt: e.transpose(pt[:, 0:67], v2rows[0:67, 0:128], ident[0:67, 0:67]), reads=[b_v2rows, b_const], writes=[pb])
        em.op("dve", lambda e, pt=pt: e.tensor_copy(out=VEC2[:], in_=pt[:, 0:67]), reads=[pb], writes=[b_vec])

        NSEQ = 1 + NS
        crow, b_crow = bigt()
        em.dma("sp", lambda e: e.dma_start(out=crow[0:NSEQ, 0:D], in_=cvec), writes=[b_crow])
        em.op("act", lambda e: e.activation(out=crow[0:NSEQ, 0:D], in_=crow[0:NSEQ, 0:D], func=AF.Silu), reads=[b_crow], writes=[b_crow])
        scT = sb("scT", [128, KC, NSEQ], BF16)
        b_scT = em.buf()
        for kc in range(KC):
            pt, pb = ps_next()
            em.op("pe", lambda e, kc=kc, pt=pt: e.transpose(pt[:, 0:NSEQ], crow[0:NSEQ, kc * 128:(kc + 1) * 128], ident[0:NSEQ, 0:NSEQ]), reads=[b_crow, b_const], writes=[pb])
            em.op("dve", lambda e, kc=kc, pt=pt: e.tensor_copy(out=scT[:, kc, :], in_=pt[:, 0:NSEQ]), reads=[pb], writes=[b_scT])
        modA = [sb("modA%d" % i, [128, KC, NSEQ]) for i in range(8)]
        modB = [sb("modB%d" % i, [128, KC, NSEQ]) for i in range(8)]
        modG = [sb("modG%d" % i, [128, KC, NSEQ]) for i in range(8)]
        b_mod = [em.buf("mod%d" % i) for i in range(8)]

        WSLOT = 2 * KC * 256
        NW = 3
        wsl = [sb("wsl%d" % i, [128, WSLOT], BF16) for i in range(NW)]
        wsb = [em.buf("wsl%d" % i) for i in range(NW)]
        w_rr = [0]

        def wload(parts):
            i = w_rr[0]
            w_rr[0] = (i + 1) % NW
            for (off, kcn, cols, src) in parts:
                dst = wsl[i][:, off:off + kcn * cols].rearrange("p (k c) -> p k c", k=kcn)
                em.dma("pool", lambda e, dst=dst, src=src: e.dma_start(out=dst, in_=src), writes=[wsb[i]])
            return wsl[i], wsb[i]

        def wview(t, off, kcn, cols):
            return t[:, off:off + kcn * cols].rearrange("p (k c) -> p k c", k=kcn)

        n_li = 8
        blk = 0
        for li in range(n_li):
            l, i = li // 2, li % 2
            for cb in range(6):
                src = w_ada[l, i].rearrange("(kc p) c -> p kc c", p=128)[:, :, cb * 512:(cb + 1) * 512]
                wt, wb = wload([(0, KC, 512, src)])
                wv = wview(wt, 0, KC, 512)
                pt, pb = ps_next()
                for c4 in range(4):
                    for kc in range(KC):
                        em.op("pe", lambda e, c4=c4, kc=kc, pt=pt, wv=wv: e.matmul(pt[:, c4 * 8:c4 * 8 + NSEQ], lhsT=wv[:, kc, c4 * 128:(c4 + 1) * 128], rhs=scT[:, kc, :], start=(kc == 0), stop=(kc == KC - 1)),
                              reads=[wb, b_scT], writes=[pb])
                for c4 in range(4):
                    ch = cb * 4 + c4
                    which, kc = ch // 8, ch % 8
                    brow = R_BADA + li * 3 + which
                    src_ps = pt[:, c4 * 8:c4 * 8 + NSEQ]
                    if which == 0:
                        em.op("dve", lambda e, kc=kc, li=li, src_ps=src_ps, brow=brow: e.tensor_scalar(out=modB[li][:, kc, :], in0=src_ps, scalar1=vcol(brow, kc), scalar2=None, op0=ALU.add), reads=[pb, b_vec], writes=[b_mod[li]])
                    elif which == 1:
                        em.op("dve", lambda e, kc=kc, li=li, src_ps=src_ps, brow=brow: e.tensor_scalar(out=modA[li][:, kc, :], in0=src_ps, scalar1=vcol(brow, kc), scalar2=1.0, op0=ALU.add, op1=ALU.add), reads=[pb, b_vec], writes=[b_mod[li]])
                        em.op("dve", lambda e, kc=kc, li=li: e.tensor_scalar(out=modA[li][:, kc, :], in0=modA[li][:, kc, :], scalar1=vcol(R_NG + li, kc), scalar2=None, op0=ALU.mult), reads=[b_mod[li], b_vec], writes=[b_mod[li]])
                    else:
                        em.op("dve", lambda e, kc=kc, li=li, src_ps=src_ps, brow=brow: e.tensor_scalar(out=modG[li][:, kc, :], in0=src_ps, scalar1=vcol(brow, kc), scalar2=None, op0=ALU.add), reads=[pb, b_vec], writes=[b_mod[li]])

        X = sb("X", [128, KC, GN])
        H = sb("H", [128, KC, GN], BF16)
        ZS = sb("ZS", [128, KC, GN], BF16)
        BIG = sb("BIG", [128, KC * (GN + 30) + KC * GN])
        UB = BIG[:, 0:KC * (GN + 30)].rearrange("p (k t) -> p k t", k=KC)
        Y = BIG[:, KC * (GN + 30):].rearrange("p (k t) -> p k t", k=KC)
        BIGb = BIG[:].bitcast(BF16)
        HID = BIGb[:, 0:FC * GN].rearrange("p (k t) -> p k t", k=FC)
        QT = BIGb[:, 0:16 * GN].rearrange("p (k t) -> p k t", k=16)
        RSTD = sb("RSTD", [128, GN])
        XSS = sb("XSS", [128, KC, NS * ST])
        b_XSS = em.buf("XSS")
        b_XS = em.buf("XS")
        b_X, b_H, b_ZS, b_BIG, b_RSTD, b_MEAN, b_VALID = [em.buf(n) for n in ("X", "H", "ZS", "BIG", "RSTD", "MEAN", "VALID")]
        b_UB = b_Y = b_HID = b_QT = b_BIG
        b_hist = [em.buf("hist0"), em.buf("hist1")]

        KT = sb("KT", [128, 4, TV], BF16)
        SV = sb("SV", [128, 32, 256], BF16)
        WV = sb("WV", [128, 20, 256], BF16)
        CKT = sb("CKT", [64, 4, 128], BF16)
        CV = sb("CV", [128, 4, 64], BF16)
        W2 = sb("W2", [128, 2, 64], BF16)
        B2V = sb("B2V", [128, 64])
        stA = ExitStack()
        st.enter_context(stA)

        def sbA(name, shape, dt=F32):
            return stA.enter_context(nc.sbuf_tensor(name, list(shape), dt))
        VALID = sbA("VALID", [128, GN])
        MEAN = sbA("MEAN", [128, GN])
        hist = [sbA("hist%d" % l, [128, KC, 30]) for l in range(2)]
        HIDC = sbA("HIDC", [128, 2, 4, 128], BF16)
        CT = sbA("CT", [128, 2, 2, GN], BF16)
        b_KT, b_SV, b_WV, b_HIDC, b_CT, b_CK, b_W1T = [em.buf(n) for n in ("KT", "SV", "WV", "HIDC", "CT", "CK", "W1T")]
        for e_ in range(2):
            em.dma("pool", lambda e, e_=e_: e.dma_start(out=W2[:, e_, :], in_=w_phi2[e_]), writes=[b_W1T])
        em.dma("sp", lambda e: e.dma_start(out=B2V[:], in_=b_phi2[1:2, :].broadcast_to([128, 64])), writes=[b_W1T])

        def load_xT(G, src_rows):
            N = G.N
            for t0 in range(0, N, 128):
                tn = min(128, N - t0)
                xt, xb = bigt()
                em.dma("sp", lambda e, xt=xt, t0=t0, tn=tn: e.dma_start(out=xt[0:tn, 0:D], in_=src_rows[t0:t0 + tn, :]), writes=[xb])
                for k0 in range(0, KC, 4):
                    pt, pb = ps_next()
                    for kk in range(4):
                        kc = k0 + kk
                        em.op("pe", lambda e, xt=xt, kc=kc, kk=kk, tn=tn, pt=pt: e.transpose(pt[:, kk * 128:kk * 128 + tn], xt[0:tn, kc * 128:(kc + 1) * 128], ident[0:tn, 0:tn]), reads=[xb, b_const], writes=[pb])
                    em.op("act", lambda e, k0=k0, t0=t0, tn=tn, pt=pt: e.copy(out=X[:, k0:k0 + 4, t0:t0 + tn], in_=pt[:].rearrange("p (k t) -> p k t", k=4)[:, :, 0:tn]), reads=[pb], writes=[b_X])

        def rms_stats(G):
            N = G.N
            pt, pb = ps_next()
            for kc in range(KC):
                sq, sqb = sqt()
                em.op("act", lambda e, kc=kc, sq=sq: e.activation(out=sq[:, 0:N], in_=X[:, kc, 0:N], func=AF.Square), reads=[b_X], writes=[sqb])
                em.op("pe", lambda e, kc=kc, pt=pt, sq=sq: e.matmul(pt[:, 0:N], lhsT=onesb[:], rhs=sq[:, 0:N], start=(kc == 0), stop=(kc == KC - 1)), reads=[sqb, b_const], writes=[pb])
            em.op("act", lambda e, pt=pt: e.activation(out=RSTD[:, 0:N], in_=pt[:, 0:N], func=AF.Sqrt, bias=epsc[:, 0:1], scale=1.0 / D), reads=[pb, b_const], writes=[b_RSTD])
            em.op("dve", lambda e: e.reciprocal(out=RSTD[:, 0:N], in_=RSTD[:, 0:N]), reads=[b_RSTD], writes=[b_RSTD])

        def norm_mod(G, Acol, Bcol, abufs, out=None, out_buf=None):
            rms_stats(G)
            if out is None:
                out, out_buf = H, b_H
            for kc in range(KC):
                for si, seq in enumerate(G.seqs):
                    c0, c1 = si * G.n, (si + 1) * G.n
                    if Bcol is None:
                        em.op("dve", lambda e, kc=kc, c0=c0, c1=c1, seq=seq: e.scalar_tensor_tensor(out=out[:, kc, c0:c1], in0=X[:, kc, c0:c1], scalar=Acol(kc, seq), in1=RSTD[:, c0:c1], op0=ALU.mult, op1=ALU.mult),
                              reads=[b_X, b_RSTD] + abufs, writes=[out_buf])
                    else:
                        tt, tb = tmpA()
                        em.op("dve", lambda e, kc=kc, c0=c0, c1=c1, seq=seq, tt=tt: e.scalar_tensor_tensor(out=tt[:, c0:c1], in0=X[:, kc, c0:c1], scalar=Acol(kc, seq), in1=RSTD[:, c0:c1], op0=ALU.mult, op1=ALU.mult),
                              reads=[b_X, b_RSTD] + abufs, writes=[tb])
                        em.op("act", lambda e, kc=kc, c0=c0, c1=c1, seq=seq, tt=tt: e.activation(out=out[:, kc, c0:c1], in_=tt[:, c0:c1], func=AF.Identity, bias=Bcol(kc, seq), scale=1.0),
                              reads=[tb] + abufs, writes=[out_buf])

        def ffn(G, l):
            N = G.N
            li = l * 2 + 1
            norm_mod(G, lambda kc, seq: modA[li][:, kc, seq:seq + 1], lambda kc, seq: modB[li][:, kc, seq:seq + 1], [b_mod[li]])
            wg = w_gate[l].rearrange("(kc p) c -> p kc c", p=128)
            wu = w_up[l].rearrange("(kc p) c -> p kc c", p=128)
            for blk in range(FC // 2):
                wt, wb = wload([(0, KC, 256, wg[:, :, blk * 256:(blk + 1) * 256]), (KC * 256, KC, 256, wu[:, :, blk * 256:(blk + 1) * 256])])
                wgv = wview(wt, 0, KC, 256)
                wuv = wview(wt, KC * 256, KC, 256)
                for c2 in range(2):
                    j = blk * 2 + c2
                    pg, pgb = ps_next()
                    pu, pub = ps_next()
                    for kc in range(KC):
                        em.op("pe", lambda e, kc=kc, c2=c2, pg=pg, wgv=wgv: e.matmul(pg[:, 0:N], lhsT=wgv[:, kc, c2 * 128:(c2 + 1) * 128], rhs=H[:, kc, 0:N], start=(kc == 0), stop=(kc == KC - 1)), reads=[wb, b_H], writes=[pgb])
                    for kc in range(KC):
                        em.op("pe", lambda e, kc=kc, c2=c2, pu=pu, wuv=wuv: e.matmul(pu[:, 0:N], lhsT=wuv[:, kc, c2 * 128:(c2 + 1) * 128], rhs=H[:, kc, 0:N], start=(kc == 0), stop=(kc == KC - 1)), reads=[wb, b_H], writes=[pub])
                    tt, tb = tmpA()
                    em.op("act", lambda e, pg=pg, tt=tt: e.activation(out=tt[:, 0:N], in_=pg[:, 0:N], func=AF.Silu), reads=[pgb], writes=[tb])
                    em.op("dve", lambda e, pu=pu, tt=tt, j=j: e.tensor_tensor(out=HID[:, j, 0:N], in0=pu[:, 0:N], in1=tt[:, 0:N], op=ALU.mult), reads=[pub, tb], writes=[b_HID])
            wd = w_down[l].rearrange("(kc p) c -> p kc c", p=128)
            for blk in range(4):
                wt0, wb0 = wload([(0, FC // 2, 256, wd[:, 0:FC // 2, blk * 256:(blk + 1) * 256])])
                wt1, wb1 = wload([(0, FC // 2, 256, wd[:, FC // 2:FC, blk * 256:(blk + 1) * 256])])
                wdv = [wview(wt0, 0, FC // 2, 256), wview(wt1, 0, FC // 2, 256)]
                wdb = [wb0, wb1]
                for c2 in range(2):
                    j = blk * 2 + c2
                    pt, pb = ps_next()
                    for kc in range(FC):
                        hh, kk = kc // (FC // 2), kc % (FC // 2)
                        em.op("pe", lambda e, kc=kc, hh=hh, kk=kk, c2=c2, pt=pt, wdv=wdv: e.matmul(pt[:, 0:N], lhsT=wdv[hh][:, kk, c2 * 128:(c2 + 1) * 128], rhs=HID[:, kc, 0:N], start=(kc == 0), stop=(kc == FC - 1)), reads=[wdb[hh], b_HID], writes=[pb])
                    for si, seq in enumerate(G.seqs):
                        c0, c1 = si * G.n, (si + 1) * G.n
                        em.op("dve", lambda e, j=j, c0=c0, c1=c1, seq=seq, pt=pt: e.scalar_tensor_tensor(out=X[:, j, c0:c1], in0=pt[:, c0:c1], scalar=modG[li][:, j, seq:seq + 1], in1=X[:, j, c0:c1], op0=ALU.mult, op1=ALU.add),
                              reads=[pb, b_X, b_mod[li]], writes=[b_X])

        def conf_layer(G, l, gi):
            N, S, n = G.N, G.S, G.n
            li = l * 2
            norm_mod(G, lambda kc, seq: modA[li][:, kc, seq:seq + 1], lambda kc, seq: modB[li][:, kc, seq:seq + 1], [b_mod[li]])
            UBv = UB[:, :, 0:S * (n + 30)].rearrange("p k (s t) -> p k s t", s=S)
            if gi is None:
                for s in range(S):
                    ct, cb = bigt()
                    em.dma("sp", lambda e, ct=ct, s=s: e.dma_start(out=ct[0:30, 0:D], in_=stconv[l, s]), writes=[cb])
                    for k0 in range(0, KC, 4):
                        pt, pb = ps_next()
                        for kk in range(4):
                            em.op("pe", lambda e, ct=ct, kk=kk, k0=k0, pt=pt: e.transpose(pt[:, kk * 32:kk * 32 + 30], ct[0:30, (k0 + kk) * 128:(k0 + kk + 1) * 128], ident[0:30, 0:30]), reads=[cb, b_const], writes=[pb])
                        em.op("act", lambda e, k0=k0, s=s, pt=pt: e.copy(out=UBv[:, k0:k0 + 4, s, 0:30], in_=pt[:, 0:128].rearrange("p (k t) -> p k t", k=4)[:, :, 0:30]), reads=[pb], writes=[b_UB])
            elif gi == 0:
                em.op("dve", lambda e: e.memset(UBv[:, :, 0, 0:30], 0.0), writes=[b_UB])
            else:
                em.op("act", lambda e: e.copy(out=UBv[:, :, 0, 0:30], in_=hist[l][:]), reads=[b_hist[l]], writes=[b_UB])
            w1 = w_pw1[l].rearrange("(kc p) (t c) -> p kc t c", p=128, t=2)
            for blk in range(4):
                wt, wb = wload([(0, KC, 256, w1[:, :, 0, blk * 256:(blk + 1) * 256]), (KC * 256, KC, 256, w1[:, :, 1, blk * 256:(blk + 1) * 256])])
                wav = wview(wt, 0, KC, 256)
                wgv = wview(wt, KC * 256, KC, 256)
                for c2 in range(2):
                    j = blk * 2 + c2
                    pa, pab = ps_next()
                    pg, pgb = ps_next()
                    for kc in range(KC):
                        em.op("pe", lambda e, kc=kc, c2=c2, pa=pa, wav=wav: e.matmul(pa[:, 0:N], lhsT=wav[:, kc, c2 * 128:(c2 + 1) * 128], rhs=H[:, kc, 0:N], start=(kc == 0), stop=(kc == KC - 1)), reads=[wb, b_H], writes=[pab])
                    for kc in range(KC):
                        em.op("pe", lambda e, kc=kc, c2=c2, pg=pg, wgv=wgv: e.matmul(pg[:, 0:N], lhsT=wgv[:, kc, c2 * 128:(c2 + 1) * 128], rhs=H[:, kc, 0:N], start=(kc == 0), stop=(kc == KC - 1)), reads=[wb, b_H], writes=[pgb])
                    tt, tb = tmpA()
                    em.op("act", lambda e, pg=pg, tt=tt, j=j: e.activation(out=tt[:, 0:N], in_=pg[:, 0:N], func=AF.Sigmoid, bias=vcol(R_BPW1 + l * 2 + 1, j), scale=1.0), reads=[pgb, b_vec], writes=[tb])
                    if gi is not None:
                        em.op("dve", lambda e, tt=tt: e.tensor_tensor(out=tt[:, 0:N], in0=tt[:, 0:N], in1=VALID[:, 0:N], op=ALU.mult), reads=[tb, b_VALID], writes=[tb])
                    em.op("dve", lambda e, pa=pa, tt=tt, j=j: e.scalar_tensor_tensor(out=UBv[:, j, :, 30:30 + n], in0=pa[:, 0:N].rearrange("p (s t) -> p s t", s=S), scalar=vcol(R_BPW1 + l * 2, j), in1=tt[:, 0:N].rearrange("p (s t) -> p s t", s=S), op0=ALU.add, op1=ALU.mult),
                          reads=[pab, tb, b_vec], writes=[b_UB])
            if gi is not None and gi < NG - 1:
                em.op("act", lambda e: e.copy(out=hist[l][:], in_=UBv[:, :, 0, n:n + 30]), reads=[b_UB], writes=[b_hist[l]])
            if gi is None or gi == NG - 1:
                for s in range(S):
                    ct, cb = bigt()
                    w_ = 34 if gi is None else 30
                    o_ = 0 if gi is None else n
                    for k0 in range(0, KC, 4):
                        pt, pb = ps_next()
                        for kk in range(4):
                            em.op("pe", lambda e, kk=kk, k0=k0, s=s, pt=pt, w_=w_, o_=o_: e.transpose(pt[0:w_, kk * 128:(kk + 1) * 128], UBv[:, k0 + kk, s, o_:o_ + w_], ident[:]), reads=[b_UB, b_const], writes=[pb])
                        em.op("act", lambda e, k0=k0, ct=ct, pt=pt, w_=w_: e.copy(out=ct[0:w_, k0 * 128:(k0 + 4) * 128], in_=pt[0:w_, :]), reads=[pb], writes=[cb])
                    if gi is None:
                        em.dma("sp", lambda e, ct=ct, s=s: e.dma_start(out=conv_s[l, s], in_=ct[4:34, 0:D]), reads=[cb])
                    else:
                        em.dma("sp", lambda e, ct=ct: e.dma_start(out=conv_p[l], in_=ct[0:30, 0:D]), reads=[cb])
            Yv = Y[:, :, 0:N].rearrange("p k (s t) -> p k s t", s=S)
            for k in range(31):
                for j in range(KC):
                    wk = vcol(R_WDW + l * 31 + k, j)
                    if k == 0:
                        em.op("dve", lambda e, j=j, wk=wk: e.tensor_scalar(out=Yv[:, j], in0=UBv[:, j, :, 0:n], scalar1=wk, scalar2=vcol(R_BDW + l, j), op0=ALU.mult, op1=ALU.add), reads=[b_UB, b_vec], writes=[b_Y])
                    else:
                        em.op("dve", lambda e, j=j, wk=wk, k=k: e.scalar_tensor_tensor(out=Yv[:, j], in0=UBv[:, j, :, k:k + n], scalar=wk, in1=Yv[:, j], op0=ALU.mult, op1=ALU.add), reads=[b_UB, b_vec, b_Y], writes=[b_Y])
            em.op("act", lambda e: e.copy(out=ZS[:, :, 0:N], in_=Y[:, :, 0:N]), reads=[b_Y], writes=[b_ZS])
            p1, p1b = ps_next()
            p2, p2b = ps_next()
            for kc in range(KC):
                em.op("pe", lambda e, kc=kc, p1=p1: e.matmul(p1[:, 0:N], lhsT=onesb[:], rhs=ZS[:, kc, 0:N], start=(kc == 0), stop=(kc == KC - 1)), reads=[b_ZS, b_const], writes=[p1b])
            for kc in range(KC):
                sq, sqb = sqt()
                em.op("act", lambda e, kc=kc, sq=sq: e.activation(out=sq[:, 0:N], in_=Y[:, kc, 0:N], func=AF.Square), reads=[b_Y], writes=[sqb])
                em.op("pe", lambda e, kc=kc, p2=p2, sq=sq: e.matmul(p2[:, 0:N], lhsT=onesb[:], rhs=sq[:, 0:N], start=(kc == 0), stop=(kc == KC - 1)), reads=[sqb, b_const], writes=[p2b])
            em.op("dve", lambda e, p1=p1: e.tensor_scalar(out=MEAN[:, 0:N], in0=p1[:, 0:N], scalar1=1.0 / D, scalar2=None, op0=ALU.mult), reads=[p1b], writes=[b_MEAN])
            tt, tb = tmpA()
            em.op("dve", lambda e, tt=tt: e.tensor_tensor(out=tt[:, 0:N], in0=MEAN[:, 0:N], in1=MEAN[:, 0:N], op=ALU.mult), reads=[b_MEAN], writes=[tb])
            em.op("dve", lambda e, tt=tt, p2=p2: e.scalar_tensor_tensor(out=tt[:, 0:N], in0=p2[:, 0:N], scalar=1.0 / D, in1=tt[:, 0:N], op0=ALU.mult, op1=ALU.subtract), reads=[p2b, tb], writes=[tb])
            em.op("act", lambda e, tt=tt: e.activation(out=RSTD[:, 0:N], in_=tt[:, 0:N], func=AF.Sqrt, bias=epsc[:, 0:1], scale=1.0), reads=[tb, b_const], writes=[b_RSTD])
            em.op("dve", lambda e: e.reciprocal(out=RSTD[:, 0:N], in_=RSTD[:, 0:N]), reads=[b_RSTD], writes=[b_RSTD])
            for j in range(KC):
                tt, tb = tmpA()
                em.op("dve", lambda e, j=j, tt=tt: e.tensor_tensor(out=tt[:, 0:N], in0=Y[:, j, 0:N], in1=MEAN[:, 0:N], op=ALU.subtract), reads=[b_Y, b_MEAN], writes=[tb])
                em.op("dve", lambda e, tt=tt: e.tensor_tensor(out=tt[:, 0:N], in0=tt[:, 0:N], in1=RSTD[:, 0:N], op=ALU.mult), reads=[tb, b_RSTD], writes=[tb])
                em.op("act", lambda e, j=j, tt=tt: e.activation(out=ZS[:, j, 0:N], in_=tt[:, 0:N], func=AF.Silu, bias=vcol(R_LNB + l, j), scale=vcol(R_LNG + l, j)), reads=[tb, b_vec], writes=[b_ZS])
            w2 = w_pw2[l].rearrange("(kc p) c -> p kc c", p=128)
            for blk in range(4):
                wt, wb = wload([(0, KC, 256, w2[:, :, blk * 256:(blk + 1) * 256])])
                wv = wview(wt, 0, KC, 256)
                for c2 in range(2):
                    j = blk * 2 + c2
                    pt, pb = ps_next()
                    for kc in range(KC):
                        em.op("pe", lambda e, kc=kc, c2=c2, pt=pt, wv=wv: e.matmul(pt[:, 0:N], lhsT=wv[:, kc, c2 * 128:(c2 + 1) * 128], rhs=ZS[:, kc, 0:N], start=(kc == 0), stop=(kc == KC - 1)), reads=[wb, b_ZS], writes=[pb])
                    for si, seq in enumerate(G.seqs):
                        c0, c1 = si * n, (si + 1) * n
                        tt, tb = tmpA()
                        em.op("dve", lambda e, j=j, c0=c0, c1=c1, seq=seq, pt=pt, tt=tt: e.tensor_scalar(out=tt[:, c0:c1], in0=pt[:, c0:c1], scalar1=vcol(R_BPW2 + l, j), scalar2=modG[li][:, j, seq:seq + 1], op0=ALU.add, op1=ALU.mult), reads=[pb, b_vec, b_mod[li]], writes=[tb])
                        em.op("dve", lambda e, j=j, c0=c0, c1=c1, tt=tt: e.tensor_tensor(out=X[:, j, c0:c1], in0=X[:, j, c0:c1], in1=tt[:, c0:c1], op=ALU.add), reads=[tb, b_X], writes=[b_X])

        def kv_proj(G, out_rows, gi):
            N = G.N
            norm_mod(G, lambda kc, seq: vcol(R_GKV, kc), None, [b_vec])
            wk = w_kv.rearrange("(kc p) c -> p kc c", p=128)
            wts = []
            for cb in range(3):
                wt, wb = wload([(0, KC, 512, wk[:, :, cb * 512:(cb + 1) * 512])])
                wts.append((wview(wt, 0, KC, 512), wb))
            for t0 in range(0, N, 128):
                tn = min(128, N - t0)
                rt, rb = bigt()
                for cb in range(3):
                    wv, wb = wts[cb]
                    pt, pb = ps_next()
                    for kc in range(KC):
                        em.op("pe", lambda e, kc=kc, t0=t0, tn=tn, pt=pt, wv=wv: e.matmul(pt[0:tn, :], lhsT=H[:, kc, t0:t0 + tn], rhs=wv[:, kc, :], start=(kc == 0), stop=(kc == KC - 1)), reads=[wb, b_H], writes=[pb])
                    em.op("act", lambda e, cb=cb, tn=tn, pt=pt, rt=rt: e.copy(out=rt[0:tn, cb * 512:(cb + 1) * 512], in_=pt[0:tn, :]), reads=[pb], writes=[rb])
                if out_rows is not None:
                    em.dma("sp", lambda e, t0=t0, tn=tn, rt=rt: e.dma_start(out=out_rows[t0:t0 + tn, :], in_=rt[0:tn, :]), reads=[rb])
                if gi is None:
                    continue
                T = gi * 4 + t0 // 128
                tl = t0 // 128
                em.op("act", lambda e, rt=rt, T=T: e.copy(out=SV[:, T, :], in_=rt[:, 768:1024]), reads=[rb], writes=[b_SV])
                if T >= 12:
                    em.op("act", lambda e, rt=rt, T=T: e.copy(out=WV[:, T - 12, :], in_=rt[:, 1280:1536]), reads=[rb], writes=[b_WV])
                for which, cbase, p0 in ((0, 512, 0), (1, 1024, 64)):
                    pt, pb = ps_next()
                    for g in range(4):
                        em.op("pe", lambda e, g=g, rt=rt, pt=pt, cbase=cbase: e.transpose(pt[0:64, g * 128:(g + 1) * 128], rt[:, cbase + g * 64:cbase + (g + 1) * 64], ident[:]), reads=[rb, b_const], writes=[pb])
                    em.op("dve", lambda e, pt=pt, p0=p0, T=T: e.tensor_copy(out=KT[p0:p0 + 64, :, T * 128:(T + 1) * 128], in_=pt[0:64, :].rearrange("p (g t) -> p g t", g=4)), reads=[pb], writes=[b_KT])
                pt, pb = ps_next()
                for e_ in range(2):
                    for gp in range(2):
                        ix = e_ * 2 + gp
                        em.op("pe", lambda e, ix=ix, e_=e_, gp=gp, rt=rt, pt=pt: e.transpose(pt[:, ix * 128:(ix + 1) * 128], rt[:, e_ * 256 + gp * 128:e_ * 256 + (gp + 1) * 128], ident[:]), reads=[rb, b_const], writes=[pb])
                for e_ in range(2):
                    for gp in range(2):
                        ix = e_ * 2 + gp
                        pe_b = VEC2[:, e_ * 32:(e_ + 1) * 32].unsqueeze(1).broadcast_to([128, 4, 32])
                        em.op("dve", lambda e, ix=ix, e_=e_, gp=gp, pt=pt, tl=tl, pe_b=pe_b: e.tensor_tensor(out=CT[:, e_, gp, tl * 128:(tl + 1) * 128].rearrange("p (n s) -> p n s", s=32), in0=pt[:, ix * 128:(ix + 1) * 128].rearrange("p (n s) -> p n s", s=32), in1=pe_b, op=ALU.add),
                              reads=[pb, b_vec], writes=[b_CT])
            if gi is None:
                return
            for e_ in range(2):
                wi = w_rr[0]
                w_rr[0] = (wi + 1) % NW
                src = w_phi1[e_].rearrange("(s d) h -> d s h", d=64)
                for half in range(2):
                    em.dma("pool", lambda e, wi=wi, half=half, src=src: e.dma_start(out=wsl[wi][half * 64:(half + 1) * 64, 0:4096].rearrange("p (s h) -> p s h", s=32), in_=src), writes=[wsb[wi]])
                w1v = wsl[wi][:, 0:4096].rearrange("p (s h) -> p s h", s=32)
                w1b = wsb[wi]
                for g in range(4):
                    gp, base = g // 2, (g % 2) * 64
                    pt, pb = ps_next()
                    ctv = CT[base:base + 64, e_, gp, :].rearrange("p (n s) -> p n s", s=32)
                    for s_ in range(32):
                        em.op("pe", lambda e, s_=s_, base=base, pt=pt, ctv=ctv, w1v=w1v: e.matmul(pt[:, 0:16], lhsT=w1v[base:base + 64, s_, :], rhs=ctv[:, :, s_], start=(s_ == 0), stop=(s_ == 31)), reads=[b_CT, w1b], writes=[pb])
                    em.op("act", lambda e, e_=e_, g=g, pt=pt: e.activation(out=HIDC[:, e_, g, gi * 16:(gi + 1) * 16], in_=pt[:, 0:16], func=AF.Silu, bias=VEC2[:, 64 + e_:65 + e_], scale=1.0), reads=[pb, b_vec], writes=[b_HIDC])

        GP = Grp(1, GN, [0])
        GS = Grp(NS, ST, [1, 2, 3, 4])

        def run_group(G, gi, src_rows, out_rows):
            load_xT(G, src_rows)
            if gi is not None:
                em.dma("sp", lambda e: e.dma_start(out=VALID[:], in_=validt[:, gi * GN:(gi + 1) * GN].broadcast_to([128, GN])), writes=[b_VALID])
            for l in range(2):
                conf_layer(G, l, gi)
                ffn(G, l)
            kv_proj(G, out_rows, gi)
            if gi is not None and gi >= NG // 2:
                q = gi - NG // 2
                em.dma("sp", lambda e, q=q: e.dma_start(out=XS[:, :, q * GN:(q + 1) * GN], in_=X[:]), reads=[b_X], writes=[b_XS])

        if stage >= 1:
            for gi in range(NG):
                orow = rows_p[(gi - NG // 2) * GN:(gi - NG // 2 + 1) * GN, :] if gi >= NG // 2 else None
                run_group(GP, gi, xp[gi * GN:(gi + 1) * GN, :], orow)
            pt, pb = ps_next()
            em.op("pe", lambda e, pt=pt: e.matmul(pt[0:64, :], lhsT=W2[:, 0, :], rhs=HIDC[:, 0, :, :].rearrange("p g n -> p (g n)"), start=True, stop=True), reads=[b_HIDC, b_W1T], writes=[pb])
            em.op("act", lambda e, pt=pt: e.activation(out=CKT[:].rearrange("p g n -> p (g n)"), in_=pt[0:64, :], func=AF.Identity, bias=VEC2[0:64, 66:67], scale=1.0), reads=[pb, b_vec], writes=[b_CK])
            pt, pb = ps_next()
            for g in range(4):
                em.op("pe", lambda e, g=g, pt=pt: e.matmul(pt[:, g * 64:(g + 1) * 64], lhsT=HIDC[:, 1, g, :], rhs=W2[:, 1, :], start=True, stop=True), reads=[b_HIDC, b_W1T], writes=[pb])
            em.op("dve", lambda e, pt=pt: e.tensor_tensor(out=CV[:], in0=pt[:, 0:256].rearrange("p (g d) -> p g d", g=4), in1=B2V[:].unsqueeze(1).broadcast_to([128, 4, 64]), op=ALU.add), reads=[pb, b_W1T], writes=[b_CK])
            dbg("CKT", CKT[:], [b_CK])
            dbg("CV", CV[:], [b_CK])
            dbg("KT0", KT[:, 0, 0:512], [b_KT])
            run_group(GS, None, xs, rows_s)
            em.op("act", lambda e: e.copy(out=XSS[:], in_=X[:, :, 0:NS * ST]), reads=[b_X], writes=[b_XSS])
        em.barrier()
        stA.close()

        ps_nrot[0] = 6
        accs = [(pst[6], psb[6]), (pst[7], psb[7])]
        acc_rr = [0]

        def acc_next():
            i = acc_rr[0]
            acc_rr[0] = 1 - i
            return accs[i]

        wall_tiles = [BIG[:, 5632:5632 + 1664], sb("wall1", [128, 1664])]
        wall_bufs = [em.buf("wall0"), em.buf("wall1")]
        wall_rr = [0]
        wall_first = [True]

        def wall():
            i = wall_rr[0]
            wall_rr[0] = 1 - i
            return wall_tiles[i], wall_bufs[i]
        Pt = rot("Pt", [128, 512], BF16, 4)
        PT = rot("PTt", [128, 512], BF16, 3)
        RSr = rot("RSr", [128, 8], F32, 4)
        sm = rot("sm", [128, 4], F32, 8)
        GATE = sb("GATE", [128, 4, 48])
        IMP = sb("IMP", [128, 256])
        IMPB = sb("IMPB", [128, 128])
        SCORE = sb("SCORE", [128, 128])
        SC2 = sb("SC2", [128, 128])
        MASK = sb("MASK", [128, 128])
        T8 = sb("T8", [128, 16])
        b_GATE, b_IMP, b_SCORE, b_MASK = [em.buf(n) for n in ("GATE", "IMP", "SCORE", "MASK")]
        evac_rr = [0]
        dbg_once = []

        def evac(out, in_, reads, writes):
            evac_rr[0] ^= 1
            if evac_rr[0]:
                em.op("act", lambda e: e.copy(out=out, in_=in_), reads=reads, writes=writes)
            else:
                em.op("dve", lambda e: e.tensor_copy(out=out, in_=in_), reads=reads, writes=writes)

        def pipeline(units, lagB=1, lagC=2):
            n = len(units)
            for t in range(n + lagC):
                if t < n:
                    units[t][0]()
                if 0 <= t - lagB < n:
                    units[t - lagB][1]()
                if 0 <= t - lagC < n:
                    units[t - lagC][2]()

        def transposeP(u, wu):
            nq = u["nq"]
            tp, tpb = ps_next()
            tpv = tp[:].bitcast(BF16)
            p_, p_b = u["p_"], u["p_b"]
            off = 0
            j = 0
            while off < wu:
                w1 = min(128, wu - off)
                em.op("pe", lambda e, j=j, off=off, w1=w1, p_=p_, tpv=tpv: e.transpose(tpv[0:w1, j * 128:j * 128 + nq], p_[0:nq, off:off + w1], identb[0:nq, 0:nq]), reads=[p_b, b_const], writes=[tpb])
                off += w1
                j += 1
            pT, pTb = PT()
            if wu >= 128:
                evac(pT[:, 0:j * 128], tpv[:, 0:j * 128], [tpb], [pTb])
            else:
                evac(pT[0:wu, 0:128], tpv[0:wu, 0:128], [tpb], [pTb])
            u["pT"], u["pTb"] = pT, pTb

        def attn_core(nq, qb_cols, gate_ap, otm, otm_b, g, heads, q0, first_blk, cmpK, cmpV, b_cmp, n_lo, n_hi, vbc_ap, selK, selV_of, winK, winV_of, sel_units, win_units, need_vb, jb_cfg, QTv, b_QTv):
            units = []
            for r, h in enumerate(heads):
                u = {"nq": nq}

                def A(u=u, r=r, h=h):
                    ps, pb = ps_next()
                    em.op("pe", lambda e: e.matmul(ps[0:nq, 0:n_hi], lhsT=QTv[0:64, h, qb_cols], rhs=cmpK[0:64, g, 0:n_hi], start=True, stop=False), reads=[b_QTv, b_cmp], writes=[pb])
                    em.op("pe", lambda e: e.matmul(ps[0:nq, 0:n_hi], lhsT=onesb[0:1, 0:nq], rhs=vbc_ap[0:1, 0:n_hi], start=False, stop=True), reads=[b_const, b_bias], writes=[pb])
                    sc, scb = tmpA()
                    em.op("dve", lambda e: e.tensor_scalar(out=sc[0:nq, 0:n_lo], in0=ps[0:nq, 0:n_lo], scalar1=CH[0:nq, h:h + 1], scalar2=None, op0=ALU.add), reads=[pb, b_bias], writes=[scb])
                    em.op("dve", lambda e: e.tensor_tensor(out=sc[0:nq, n_lo:n_hi], in0=ps[0:nq, n_lo:n_hi], in1=BCALL[0:nq, h, 0:n_hi - n_lo], op=ALU.add), reads=[pb, b_bias], writes=[scb])
                    s4, s4b = sm()
                    em.op("act", lambda e: e.activation(out=sc[0:nq, 0:n_hi], in_=sc[0:nq, 0:n_hi], func=AF.Exp, bias=NEGM[0:nq, h:h + 1], scale=1.0), reads=[scb, b_bias], writes=[scb])
                    em.op("dve", lambda e: e.tensor_reduce(out=s4[0:nq, 0:1], in_=sc[0:nq, 0:n_hi], axis=AX.X, op=ALU.add), reads=[scb], writes=[s4b])
                    em.op("dve", lambda e: e.tensor_scalar(out=s4[0:nq, 1:2], in0=s4[0:nq, 0:1], scalar1=1e-30, scalar2=None, op0=ALU.max), reads=[s4b], writes=[s4b])
                    em.op("dve", lambda e: e.reciprocal(out=s4[0:nq, 2:3], in_=s4[0:nq, 1:2]), reads=[s4b], writes=[s4b])
                    if r == 0:
                        em.op("dve", lambda e: e.memset(IMP[:], 0.0), writes=[b_IMP])
                        em.op("dve", lambda e: e.tensor_scalar(out=IMP[0:nq, 0:n_hi], in0=sc[0:nq, 0:n_hi], scalar1=s4[0:nq, 2:3], scalar2=None, op0=ALU.mult), reads=[scb, s4b], writes=[b_IMP])
                    else:
                        em.op("dve", lambda e: e.scalar_tensor_tensor(out=IMP[0:nq, 0:n_hi], in0=sc[0:nq, 0:n_hi], scalar=s4[0:nq, 2:3], in1=IMP[0:nq, 0:n_hi], op0=ALU.mult, op1=ALU.add), reads=[scb, s4b, b_IMP], writes=[b_IMP])
                    em.op("dve", lambda e: e.tensor_tensor(out=s4[0:nq, 3:4], in0=s4[0:nq, 2:3], in1=gate_ap(h, 0), op=ALU.mult), reads=[s4b, b_GATE], writes=[s4b])
                    p_, p_b = Pt()
                    em.op("act", lambda e: e.copy(out=p_[0:nq, 0:n_hi], in_=sc[0:nq, 0:n_hi]), reads=[scb], writes=[p_b])
                    u.update(p_=p_, p_b=p_b, s4=s4, s4b=s4b)

                def B(u=u):
                    transposeP(u, n_hi)

                def C(u=u, h=h):
                    po, pob = ps_next()
                    pT, pTb = u["pT"], u["pTb"]
                    nt = (n_hi + 127) // 128
                    for j in range(nt):
                        w1 = min(128, n_hi - j * 128)
                        em.op("pe", lambda e, j=j, w1=w1: e.matmul(po[0:nq, 0:64], lhsT=pT[0:w1, j * 128:j * 128 + nq], rhs=cmpV(j, w1), start=(j == 0), stop=(j == nt - 1)), reads=[pTb, b_cmp], writes=[pob])
                    s4, s4b = u["s4"], u["s4b"]
                    em.op("dve", lambda e: e.tensor_scalar(out=otm[0:nq, h * 64:(h + 1) * 64], in0=po[0:nq, 0:64], scalar1=s4[0:nq, 3:4], scalar2=None, op0=ALU.mult), reads=[pob, s4b], writes=[otm_b])
                units.append((A, B, C))
            pipeline(units)
            jb_cfg()
            units = []
            for r, h in enumerate(heads):
                hs = {}
                for br in (1, 2):
                    ul = sel_units if br == 1 else win_units
                    ntile = sum((w_ + 127) // 128 for (_, w_, _) in ul)
                    bs = {"tcount": 0, "ntile": ntile}
                    for ui, (k0, wu, kind) in enumerate(ul):
                        u = {"nq": nq}

                        def A(u=u, h=h, br=br, ui=ui, k0=k0, wu=wu, kind=kind, hs=hs, bs=bs):
                            if "wl" not in hs:
                                wl, wlb = wall()
                                extra = [b_BIG] if wall_first[0] else []
                                wall_first[0] = False
                                em.dma("sp", lambda e: e.dma_start(out=wl[0:nq, :], in_=WS[h][0:nq, :]), reads=[b_ws] + extra, writes=[wlb])
                                hs["wl"], hs["wlb"] = wl, wlb
                            wl, wlb = hs["wl"], hs["wlb"]
                            ps, pb = ps_next()
                            p_, p_b = Pt()
                            if br == 1:
                                kap, kbuf = selK(k0, wu)
                                em.op("pe", lambda e: e.matmul(ps[0:nq, 0:wu], lhsT=QTv[0:64, h, qb_cols], rhs=kap, start=True, stop=True), reads=[b_QTv, kbuf], writes=[pb])
                                tt = None
                                if kind[0] == "near":
                                    cs, c0 = kind[1], kind[2]
                                    tt, tb = tmpA()
                                    if cs > 0:
                                        em.op("dve", lambda e: e.tensor_scalar(out=tt[0:nq, 0:cs], in0=ps[0:nq, 0:cs], scalar1=CH[0:nq, h:h + 1], scalar2=None, op0=ALU.add), reads=[pb, b_bias], writes=[tb])
                                    em.op("dve", lambda e: e.tensor_tensor(out=tt[0:nq, cs:wu], in0=ps[0:nq, cs:wu], in1=wl[0:nq, c0:c0 + wu - cs], op=ALU.add), reads=[pb, wlb], writes=[tb])
                                    pf, pfb = Pt()
                                    em.op("act", lambda e: e.activation(out=pf[0:nq, 0:wu], in_=tt[0:nq, 0:wu], func=AF.Exp, bias=NEGM[0:nq, h:h + 1], scale=1.0), reads=[tb, b_bias], writes=[pfb])
                                else:
                                    pf, pfb = Pt()
                                    em.op("act", lambda e: e.activation(out=pf[0:nq, 0:wu], in_=ps[0:nq, 0:wu], func=AF.Exp, bias=NEGMC[0:nq, h:h + 1], scale=1.0), reads=[pb, b_bias], writes=[pfb])
                                if kind[-1] == "nomask":
                                    p_, p_b = pf, pfb
                                else:
                                    nb = wu // 64
                                    mcol = kind[-1]
                                    mk = MASK[0:nq, mcol:mcol + nb].unsqueeze(2).broadcast_to([nq, nb, 64])
                                    em.op("dve", lambda e: e.scalar_tensor_tensor(out=p_[0:nq, 0:wu].rearrange("p (b k) -> p b k", k=64), in0=pf[0:nq, 0:wu].rearrange("p (b k) -> p b k", k=64), scalar=1.0, in1=mk, op0=ALU.mult, op1=ALU.mult),
                                          reads=[pfb, b_MASK], writes=[p_b])
                            else:
                                kap, kbuf = winK(k0, wu)
                                vb = need_vb(k0, wu)
                                em.op("pe", lambda e: e.matmul(ps[0:nq, 0:wu], lhsT=QTv[64:128, h, qb_cols], rhs=kap, start=True, stop=(vb is None)), reads=[b_QTv, kbuf], writes=[pb])
                                if vb is not None:
                                    em.op("pe", lambda e: e.matmul(ps[0:nq, 0:wu], lhsT=onesb[0:1, 0:nq], rhs=vb, start=False, stop=True), reads=[b_const, b_bias], writes=[pb])
                                c0 = kind[1]
                                tt, tb = tmpA()
                                em.op("dve", lambda e: e.tensor_tensor(out=tt[0:nq, 0:wu], in0=ps[0:nq, 0:wu], in1=wl[0:nq, c0:c0 + wu], op=ALU.add), reads=[pb, wlb], writes=[tb])
                                em.op("act", lambda e: e.activation(out=p_[0:nq, 0:wu], in_=tt[0:nq, 0:wu], func=AF.Exp, bias=NEGM[0:nq, h:h + 1], scale=1.0), reads=[tb, b_bias], writes=[p_b])
                            u.update(p_=p_, p_b=p_b)

                        def B(u=u, wu=wu):
                            transposeP(u, wu)

                        def C(u=u, h=h, br=br, ui=ui, k0=k0, wu=wu, bs=bs, nul=len(ul)):
                            if ui == 0:
                                bs["po"], bs["pob"] = acc_next()
                            po, pob = bs["po"], bs["pob"]
                            pT, pTb = u["pT"], u["pTb"]
                            nt = (wu + 127) // 128
                            for j in range(nt):
                                w1 = min(128, wu - j * 128)
                                vap, vbuf = (selV_of if br == 1 else winV_of)(k0 + j * 128, w1)
                                first = (bs["tcount"] == 0)
                                last = (bs["tcount"] == bs["ntile"] - 1)
                                em.op("pe", lambda e, j=j, w1=w1, vap=vap, first=first: e.matmul(po[0:nq, 0:64], lhsT=pT[0:w1, j * 128:j * 128 + nq], rhs=vap, start=first, stop=False), reads=[pTb, vbuf], writes=[pob])
                                em.op("pe", lambda e, j=j, w1=w1, last=last: e.matmul(po[0:nq, 64:65], lhsT=pT[0:w1, j * 128:j * 128 + nq], rhs=onesb[0:w1, 0:1], start=False, stop=last), reads=[pTb, b_const], writes=[pob])
                                bs["tcount"] += 1
                            if ui == nul - 1:
                                s4, s4b = sm()
                                em.op("dve", lambda e: e.reciprocal(out=s4[0:nq, 2:3], in_=po[0:nq, 64:65]), reads=[pob], writes=[s4b])
                                em.op("dve", lambda e: e.tensor_tensor(out=s4[0:nq, 3:4], in0=s4[0:nq, 2:3], in1=gate_ap(h, br), op=ALU.mult), reads=[s4b, b_GATE], writes=[s4b])
                                em.op("dve", lambda e: e.scalar_tensor_tensor(out=otm[0:nq, h * 64:(h + 1) * 64], in0=po[0:nq, 0:64], scalar=s4[0:nq, 3:4], in1=otm[0:nq, h * 64:(h + 1) * 64], op0=ALU.mult, op1=ALU.add), reads=[pob, s4b, otm_b], writes=[otm_b])
                        units.append((A, B, C))
            pipeline(units)

        def select_blocks(nq, memsets, kth, ncol=64, use_data=True):
            R_ = slice(0, nq)
            C_ = slice(0, ncol)
            em.op("dve", lambda e: e.tensor_tensor(out=SCORE[R_, C_], in0=IMP[R_, 0:2 * ncol:2], in1=IMP[R_, 1:2 * ncol:2], op=ALU.add), reads=[b_IMP], writes=[b_SCORE])
            for (r0, r1, c0, c1, val) in memsets:
                r1 = min(r1, nq)
                if r0 >= r1:
                    continue
                em.op("dve", lambda e, r0=r0, r1=r1, c0=c0, c1=c1, val=val: e.memset(SCORE[r0:r1, c0:c1], val), reads=[b_SCORE], writes=[b_SCORE])
            if use_data:
                em.op("dve", lambda e: e.tensor_tensor(out=SCORE[R_, C_], in0=SCORE[R_, C_], in1=F0[R_, C_], op=ALU.max), reads=[b_SCORE, b_bias], writes=[b_SCORE])
                em.op("dve", lambda e: e.tensor_tensor(out=SCORE[R_, C_], in0=SCORE[R_, C_], in1=VLIM[R_, C_], op=ALU.min), reads=[b_SCORE, b_bias], writes=[b_SCORE])
            em.op("dve", lambda e: e.max(out=T8[R_, 0:8], in_=SCORE[R_, C_]), reads=[b_SCORE], writes=[b_SCORE])
            em.op("dve", lambda e: e.match_replace(out=SC2[R_, C_], in_to_replace=T8[R_, 0:8], in_values=SCORE[R_, C_], imm_value=-1e30), reads=[b_SCORE], writes=[b_SCORE])
            em.op("dve", lambda e: e.max(out=T8[R_, 8:16], in_=SC2[R_, C_]), reads=[b_SCORE], writes=[b_SCORE])
            em.op("dve", lambda e: e.tensor_scalar(out=SC2[R_, C_], in0=SCORE[R_, C_], scalar1=T8[R_, kth - 1:kth], scalar2=None, op0=ALU.is_ge), reads=[b_SCORE], writes=[b_SCORE])
            em.op("dve", lambda e: e.scalar_tensor_tensor(out=MASK[R_, C_], in0=SCORE[R_, C_], scalar=0.0, in1=SC2[R_, C_], op0=ALU.is_ge, op1=ALU.mult), reads=[b_SCORE], writes=[b_MASK])

        def qg_proj(G, lb, gate_rows):
            N = G.N
            wq = w_qg[lb].rearrange("(kc p) c -> p kc c", p=128)
            for hb in range(4):
                wt, wb = wload([(0, KC, 256, wq[:, :, hb * 256:(hb + 1) * 256])])
                wv = wview(wt, 0, KC, 256)
                for hp in range(2):
                    pt, pb = ps_next()
                    for kc in range(KC):
                        em.op("pe", lambda e, kc=kc, hp=hp, pt=pt, wv=wv: e.matmul(pt[:, 0:N], lhsT=wv[:, kc, hp * 128:(hp + 1) * 128], rhs=H[:, kc, 0:N], start=(kc == 0), stop=(kc == KC - 1)), reads=[wb, b_H], writes=[pb])
                    h0 = hb * 4 + hp * 2
                    for (src0, hh) in ((0, h0), (64, h0 + 1)):
                        for dst0 in (0, 64):
                            if (src0 + dst0) % 128 == 0 and False:
                                pass
                            em.op("act", lambda e, src0=src0, dst0=dst0, hh=hh, pt=pt: e.activation(out=QT[dst0:dst0 + 64, hh, 0:N], in_=pt[src0:src0 + 64, 0:N], func=AF.Copy, scale=0.125), reads=[pb], writes=[b_QT])
            wt, wb = wload([(0, KC, 48, wq[:, :, 1024:1072])])
            wgv = wview(wt, 0, KC, 48)
            for qb, (t0, tn) in enumerate(gate_rows):
                pt, pb = ps_next()
                for kc in range(KC):
                    em.op("pe", lambda e, kc=kc, t0=t0, tn=tn, pt=pt, wgv=wgv: e.matmul(pt[0:tn, 0:48], lhsT=H[:, kc, t0:t0 + tn], rhs=wgv[:, kc, :], start=(kc == 0), stop=(kc == KC - 1)), reads=[wb, b_H], writes=[pb])
                em.op("act", lambda e, qb=qb, tn=tn, pt=pt: e.activation(out=GATE[0:tn, qb, :], in_=pt[0:tn, 0:48], func=AF.Sigmoid), reads=[pb], writes=[b_GATE])

        def wo_proj(G, lb, li):
            N = G.N
            wo = w_o[lb].rearrange("(kc p) c -> p kc c", p=128)
            for blk in range(4):
                wt, wb = wload([(0, KC, 256, wo[:, :, blk * 256:(blk + 1) * 256])])
                wv = wview(wt, 0, KC, 256)
                for c2 in range(2):
                    j = blk * 2 + c2
                    pt, pb = ps_next()
                    for kc in range(KC):
                        em.op("pe", lambda e, kc=kc, c2=c2, pt=pt, wv=wv: e.matmul(pt[:, 0:N], lhsT=wv[:, kc, c2 * 128:(c2 + 1) * 128], rhs=ZS[:, kc, 0:N], start=(kc == 0), stop=(kc == KC - 1)), reads=[wb, b_ZS], writes=[pb])
                    for si, seq in enumerate(G.seqs):
                        c0, c1 = si * G.n, (si + 1) * G.n
                        em.op("dve", lambda e, j=j, c0=c0, c1=c1, seq=seq, pt=pt: e.scalar_tensor_tensor(out=X[:, j, c0:c1], in0=pt[:, c0:c1], scalar=modG[li][:, j, seq:seq + 1], in1=X[:, j, c0:c1], op0=ALU.mult, op1=ALU.add), reads=[pb, b_X, b_mod[li]], writes=[b_X])

        def otm_to_ZS(nq, otm, otm_b, cols):
            for k0 in range(0, KC, 4):
                pt, pb = ps_next()
                for kk in range(4):
                    em.op("pe", lambda e, kk=kk, k0=k0, pt=pt: e.transpose(pt[:, kk * 128:kk * 128 + nq], otm[0:nq, (k0 + kk) * 128:(k0 + kk + 1) * 128], ident[0:nq, 0:nq]), reads=[otm_b, b_const], writes=[pb])
                evac(ZS[:, k0:k0 + 4, cols], pt[:].rearrange("p (k t) -> p k t", k=4)[:, :, 0:nq], [pb], [b_ZS])

        def attn_prompt(gq, lb):
            li = (2 + lb) * 2
            norm_mod(GP, lambda kc, seq: modA[li][:, kc, seq:seq + 1], lambda kc, seq: modB[li][:, kc, seq:seq + 1], [b_mod[li]])
            qg_proj(GP, lb, [(qb * 128, 128) for qb in range(4)])
            for qb in range(4):
                i = 4 * gq + qb
                q0 = QH0 + 128 * i
                qc = slice(qb * 128, (qb + 1) * 128)
                otm, otm_b = bigt()
                jb = q0 // 64
                n_hi = q0 // 32 + 4
                n_lo = q0 // 32 - 28
                sel_units = []
                k0 = 0
                while k0 < q0 + 128:
                    wu = min(512, q0 + 128 - k0)
                    if k0 + wu > q0 - 896:
                        cs = max(k0, q0 - 896) - k0
                        sel_units.append((k0, wu, ("near", cs, (k0 + cs) - (q0 - 896), k0 // 64)))
                    else:
                        sel_units.append((k0, wu, ("far", k0 // 64)))
                    k0 += 512
                win_units = [(q0 - 512, 512, ("win", 1024)), (q0, 128, ("win", 1024 + 512))]
                memsets = []
                if jb + 2 < 64:
                    memsets.append((0, 128, jb + 2, 64, -1.0))
                memsets += [(0, 64, jb + 1, jb + 2, -1.0), (64, 128, jb + 1, jb + 2, 1e4), (0, 128, jb, jb + 1, 1e4), (0, 64, jb - 1, jb, 1e4)]
                for g in range(4):
                    attn_core(
                        128, qc, lambda h, br, qb=qb: GATE[:, qb, h * 3 + br:h * 3 + br + 1], otm, otm_b, g, [4 * g + r for r in range(4)], q0, None,
                        CKT, lambda j, w1, g=g: CV[0:w1, g, :], b_CK, n_lo, n_hi, VBC,
                        lambda k0, wu, g=g: (KT[0:64, g, k0:k0 + wu], b_KT), lambda k, w1, g=g: (SV[0:w1, k // 128, g * 64:(g + 1) * 64], b_SV),
                        lambda k0, wu, g=g: (KT[64:128, g, k0:k0 + wu], b_KT), lambda k, w1, g=g: (WV[0:w1, k // 128 - 12, g * 64:(g + 1) * 64], b_WV),
                        sel_units, win_units,
                        (lambda k0, wu, i=i: VBW[0:1, k0 - 1536:k0 - 1536 + wu] if i < 4 else None),
                        lambda memsets=memsets: select_blocks(128, memsets, 16), QT, b_QT)
                if gq == 0 and lb == 0:
                    dbg("otm_q%d" % qb, otm[:, 0:D], [otm_b])
                otm_to_ZS(128, otm, otm_b, qc)
            if gq == 0 and lb == 0:
                dbg("BCALL", BCALL[:], [b_bias])
                dbg("CH", CH[:], [b_bias])
                dbg("WS", WS, [b_ws])
                dbg("IMP", IMP[:, 0:128], [b_IMP])
                dbg("SCORE", SCORE[:, 0:64], [b_SCORE])
                dbg("SC2", SC2[:, 0:64], [b_SCORE])
                dbg("T8", T8[:], [b_SCORE])
                dbg("MASK", MASK[:, 0:64], [b_MASK])
                dbg("QT", QT[:, :, 0:128], [b_QT])
                dbg("GATE", GATE[:], [b_GATE])
                dbg("ZS", ZS[:, :, 0:128], [b_ZS])
            wo_proj(GP, lb, li)

        def final_out(G, out_rows):
            N = G.N
            Yf = BIG[:, 0:KC * GN].rearrange("p (k t) -> p k t", k=KC)
            norm_mod(G, lambda kc, seq: vcol(R_FG, kc), None, [b_vec], out=Yf, out_buf=b_Y)
            for t0 in range(0, N, 128):
                tn = min(128, N - t0)
                yt, ytb = bigt()
                for k0 in range(0, KC, 4):
                    pt, pb = ps_next()
                    for kk in range(4):
                        em.op("pe", lambda e, kk=kk, k0=k0, t0=t0, tn=tn, pt=pt: e.transpose(pt[0:tn, kk * 128:(kk + 1) * 128], Yf[:, k0 + kk, t0:t0 + tn], ident[:]), reads=[b_Y, b_const], writes=[pb])
                    evac(yt[0:tn, k0 * 128:(k0 + 4) * 128], pt[0:tn, :], [pb], [ytb])
                em.dma("sp", lambda e, t0=t0, tn=tn, yt=yt: e.dma_start(out=out_rows[t0:t0 + tn, :], in_=yt[0:tn, 0:D]), reads=[ytb])

        if stage >= 3:
            ngq = 4 if stage >= 4 else 1
            for gq in range(ngq):
                em.dma("sp", lambda e, gq=gq: e.dma_start(out=X[:], in_=XS[:, :, gq * GN:(gq + 1) * GN]), reads=[b_XS], writes=[b_X])
                for lb in range(2):
                    attn_prompt(gq, lb)
                    dbg("xa%d_%d" % (gq, lb), X[:, :, 0:16], [b_X])
                    ffn(GP, 2 + lb)
                    dbg("xb%d_%d" % (gq, lb), X[:, :, 0:16], [b_X])
                final_out(GP, y_p[gq * GN:(gq + 1) * GN, :])

        if stage >= 5:
            em.barrier()
            q0s = 8192
            KTflat = KT[:].rearrange("p g k -> p (g k)")
            KTf32 = KTflat.bitcast(F32)
            PG = KTf32[:, 0:2048].rearrange("p (q c) -> p q c", q=4)
            WVs = KTflat[:, 4096:4096 + 1280].rearrange("p (t c) -> p t c", t=5)
            KN = KTflat[:, 5632:5632 + 16].rearrange("p (g k) -> p g k", g=4)
            WKN = KTflat[:, 5696:5696 + 16].rearrange("p (g k) -> p g k", g=4)
            VN = KTflat[:, 5760:5760 + 256]
            KU = KTflat[:, 8192:12288].rearrange("p (b g k) -> p b g k", b=2, g=4)
            VU = KTflat[:, 12288:14336].rearrange("p (b q c) -> p b q c", b=2, q=4)
            WKs = KTflat[:, 14336:16384].rearrange("p (g k) -> p g k", g=4)
            SVflat = SV[:].rearrange("p t c -> p (t c)")
            CKTs = SVflat[:, 0:4096].rearrange("p (s g n) -> p s g n", s=4, g=4)
            CVs = SVflat[:, 4096:6144].rearrange("p (s j g d) -> p s j g d", s=4, j=2, g=4)
            HIDCs = SVflat[:, 6144:8192].rearrange("p (e g n) -> p e g n", e=2, g=4)
            WVflat = WV[:].rearrange("p t c -> p (t c)")
            CTs = WVflat[:, 0:4096].rearrange("p (e a n) -> p e a n", e=2, a=2)
            IDXf = WVflat[:, 4096:4608].bitcast(F32)
            IDX = WVflat[:, 4608:5120].bitcast(I32)
            PIO = KTflat[:, 6016:6018].bitcast(F32)
            OACC = KTflat[0:4, 6080:6080 + 2080].bitcast(F32).rearrange("p (h d) -> p h d", h=16)
            RSs = KTflat[0:4, 14336:14336 + 640].bitcast(F32).rearrange("p (h u) -> p h u", h=16)
            MASKs = KTflat[0:4, 14336 + 640:14336 + 640 + 1024].bitcast(F32).rearrange("p (g n) -> p g n", g=4)
            ZB = KTflat[0:1, 14336 + 1664:14336 + 1664 + 256]
            b_PG, b_WVs, b_KN, b_KU, b_VU, b_WKs, b_scmp, b_HIDCs, b_CTs, b_IDX, b_MASKs, b_OACC, b_RSs = [em.buf(n) for n in
                ("PG", "WVs", "KN", "KU", "VU", "WKs", "scmp", "HIDCs", "CTs", "IDX", "MASKs", "OACC", "RSs")]
            b_KUb = [em.buf("KU0"), em.buf("KU1")]
            b_VUb = [em.buf("VU0"), em.buf("VU1")]
            em.op("dve", lambda e: e.memset(ZB[:], 0.0), writes=[b_IDX])
            em.dma("sp", lambda e: e.dma_start(out=IDX[:], in_=ptab.rearrange("s j -> (s j)").unsqueeze(0).broadcast_to([128, NS * 64])), writes=[b_IDX])
            em.op("pool", lambda e: e.iota(PIO[:], [[1, 1]], base=0, channel_multiplier=1, allow_small_or_imprecise_dtypes=True), writes=[b_IDX])
            em.op("dve", lambda e: e.tensor_copy(out=IDXf[:], in_=IDX[:]), reads=[b_IDX], writes=[b_IDX])
            em.op("dve", lambda e: e.tensor_scalar(out=IDXf[:], in0=IDXf[:], scalar1=128.0, scalar2=PIO[:, 0:1], op0=ALU.mult, op1=ALU.add), reads=[b_IDX], writes=[b_IDX])
            em.op("dve", lambda e: e.tensor_copy(out=IDX[:], in_=IDXf[:]), reads=[b_IDX], writes=[b_IDX])

            def gather4(cache, s, pg0):
                for q in range(4):
                    col = s * 64 + pg0 + q
                    em.dma("pool", lambda e, q=q, col=col: e.indirect_dma_start(out=PG[:, q, :], out_offset=None, in_=cache, in_offset=bass.IndirectOffsetOnAxis(ap=IDX[:, col:col + 1], axis=0)), reads=[b_IDX], writes=[b_PG])

            for s in range(NS):
                em.dma("sp", lambda e, s=s: e.dma_start(out=win_s[s, 0:508, :], in_=cache_win[s, 4:512, :]))
                em.dma("sp", lambda e, s=s: e.dma_start(out=win_s[s, 508:512, :], in_=rows_s[s * ST:(s + 1) * ST, 1024:1536]))

            for s in range(NS):
                for bt in range(8):
                    for sub in range(2):
                        gather4(cache_cmp, s, bt * 8 + sub * 4)
                        for q in range(4):
                            tl = sub * 4 + q
                            pt, pb = ps_next()
                            for e_ in range(2):
                                for gp in range(2):
                                    ix = e_ * 2 + gp
                                    em.op("pe", lambda e, ix=ix, e_=e_, gp=gp, q=q, pt=pt: e.transpose(pt[:, ix * 128:(ix + 1) * 128], PG[:, q, e_ * 256 + gp * 128:e_ * 256 + (gp + 1) * 128], ident[:]), reads=[b_PG, b_const], writes=[pb])
                            for e_ in range(2):
                                for gp in range(2):
                                    ix = e_ * 2 + gp
                                    pe_b = VEC2[:, e_ * 32:(e_ + 1) * 32].unsqueeze(1).broadcast_to([128, 4, 32])
                                    em.op("dve", lambda e, ix=ix, e_=e_, gp=gp, pt=pt, tl=tl, pe_b=pe_b: e.tensor_tensor(out=CTs[:, e_, gp, tl * 128:(tl + 1) * 128].rearrange("p (n s) -> p n s", s=32), in0=pt[:, ix * 128:(ix + 1) * 128].rearrange("p (n s) -> p n s", s=32), in1=pe_b, op=ALU.add),
                                          reads=[pb, b_vec], writes=[b_CTs])
                    for e_ in range(2):
                        wi = w_rr[0]
                        w_rr[0] = (wi + 1) % NW
                        src = w_phi1[e_].rearrange("(s d) h -> d s h", d=64)
                        for half in range(2):
                            em.dma("pool", lambda e, wi=wi, half=half, src=src: e.dma_start(out=wsl[wi][half * 64:(half + 1) * 64, 0:4096].rearrange("p (s h) -> p s h", s=32), in_=src), writes=[wsb[wi]])
                        w1v = wsl[wi][:, 0:4096].rearrange("p (s h) -> p s h", s=32)
                        w1b = wsb[wi]
                        for g in range(4):
                            gp, base = g // 2, (g % 2) * 64
                            pt, pb = ps_next()
                            ctv = CTs[base:base + 64, e_, gp, :].rearrange("p (n s) -> p n s", s=32)
                            for s_ in range(32):
                                em.op("pe", lambda e, s_=s_, base=base, pt=pt, ctv=ctv, w1v=w1v: e.matmul(pt[:, 0:32], lhsT=w1v[base:base + 64, s_, :], rhs=ctv[:, :, s_], start=(s_ == 0), stop=(s_ == 31)), reads=[b_CTs, w1b], writes=[pb])
                            em.op("act", lambda e, e_=e_, g=g, pt=pt, bt=bt: e.activation(out=HIDCs[:, e_, g, bt * 32:(bt + 1) * 32], in_=pt[:, 0:32], func=AF.Silu, bias=VEC2[:, 64 + e_:65 + e_], scale=1.0), reads=[pb, b_vec], writes=[b_HIDCs])
                for half in range(2):
                    pt, pb = ps_next()
                    em.op("pe", lambda e, pt=pt, half=half: e.matmul(pt[0:64, :], lhsT=W2[:, 0, :], rhs=HIDCs[:, 0, half * 2:half * 2 + 2, :].rearrange("p g n -> p (g n)"), start=True, stop=True), reads=[b_HIDCs, b_W1T], writes=[pb])
                    em.op("act", lambda e, pt=pt, half=half, s=s: e.activation(out=CKTs[0:64, s, half * 2:half * 2 + 2, :].rearrange("p g n -> p (g n)"), in_=pt[0:64, :], func=AF.Identity, bias=VEC2[0:64, 66:67], scale=1.0), reads=[pb, b_vec], writes=[b_scmp])
                for j in range(2):
                    pt, pb = ps_next()
                    for g in range(4):
                        em.op("pe", lambda e, g=g, j=j, pt=pt: e.matmul(pt[:, g * 64:(g + 1) * 64], lhsT=HIDCs[:, 1, g, j * 128:(j + 1) * 128], rhs=W2[:, 1, :], start=True, stop=True), reads=[b_HIDCs, b_W1T], writes=[pb])
                    em.op("dve", lambda e, pt=pt, j=j, s=s: e.tensor_tensor(out=CVs[:, s, j, :, :], in0=pt[:, 0:256].rearrange("p (g d) -> p g d", g=4), in1=B2V[:].unsqueeze(1).broadcast_to([128, 4, 64]), op=ALU.add), reads=[pb, b_W1T], writes=[b_scmp])

            def build_unit(cache_rows_tile_of, buf):
                for g in range(4):
                    pt, pb = ps_next()
                    for q in range(4):
                        em.op("pe", lambda e, g=g, q=q, pt=pt: e.transpose(pt[0:64, q * 128:(q + 1) * 128], PG[:, q, g * 64:(g + 1) * 64], ident[:]), reads=[b_PG, b_const], writes=[pb])
                    evac(KU[0:64, buf, g, :], pt[0:64, :], [pb], [b_KUb[buf]])
                em.op("act", lambda e: e.copy(out=VU[:, buf, :, :], in_=PG[:, :, 256:512]), reads=[b_PG], writes=[b_VUb[buf]])

            def sample_attn(lb):
                li = (2 + lb) * 2
                norm_mod(GS, lambda kc, seq: modA[li][:, kc, seq:seq + 1], lambda kc, seq: modB[li][:, kc, seq:seq + 1], [b_mod[li]])
                qg_proj(GS, lb, [(s * ST, ST) for s in range(NS)])
                for s in range(NS):
                    qc = slice(s * ST, (s + 1) * ST)
                    otm, otm_b = bigt()
                    nr, nrb = bigt()
                    em.dma("sp", lambda e, nr=nr, s=s: e.dma_start(out=nr[0:ST, :], in_=rows_s[s * ST:(s + 1) * ST, :]), writes=[nrb])
                    pt, pb = ps_next()
                    for g in range(4):
                        em.op("pe", lambda e, g=g, nr=nr, pt=pt: e.transpose(pt[0:64, g * 4:g * 4 + 4], nr[0:ST, 512 + g * 64:512 + (g + 1) * 64], ident[0:ST, 0:ST]), reads=[nrb, b_const], writes=[pb])
                        em.op("pe", lambda e, g=g, nr=nr, pt=pt: e.transpose(pt[0:64, 16 + g * 4:16 + g * 4 + 4], nr[0:ST, 1024 + g * 64:1024 + (g + 1) * 64], ident[0:ST, 0:ST]), reads=[nrb, b_const], writes=[pb])
                    em.op("dve", lambda e, pt=pt: e.tensor_copy(out=KN[0:64, :, :], in_=pt[0:64, 0:16].rearrange("p (g k) -> p g k", g=4)), reads=[pb], writes=[b_KN])
                    em.op("dve", lambda e, pt=pt: e.tensor_copy(out=WKN[64:128, :, :], in_=pt[0:64, 16:32].rearrange("p (g k) -> p g k", g=4)), reads=[pb], writes=[b_KN])
                    em.op("act", lambda e, nr=nr: e.copy(out=VN[0:ST, :], in_=nr[0:ST, 768:1024]), reads=[nrb], writes=[b_KN])
                    em.op("act", lambda e, nr=nr: e.copy(out=WVs[0:ST, 4, :], in_=nr[0:ST, 1280:1536]), reads=[nrb], writes=[b_WVs])
                    for q in range(4):
                        em.dma("sp", lambda e, q=q, s=s: e.dma_start(out=PG[:, q, :], in_=cache_win[s, q * 128:(q + 1) * 128, :]), writes=[b_PG])
                    for g in range(4):
                        pt, pb = ps_next()
                        for q in range(4):
                            em.op("pe", lambda e, g=g, q=q, pt=pt: e.transpose(pt[0:64, q * 128:(q + 1) * 128], PG[:, q, g * 64:(g + 1) * 64], ident[:]), reads=[b_PG, b_const], writes=[pb])
                        evac(WKs[64:128, g, :], pt[0:64, :], [pb], [b_WKs])
                    em.op("act", lambda e: e.copy(out=WVs[:, 0:4, :], in_=PG[:, :, 256:512]), reads=[b_PG], writes=[b_WVs])
                    win_units = [(0, 512, ("win", 1024)), (512, ST, ("win", 1024 + 512))]
                    for g in range(4):
                        def sel_cb(g=g):
                            select_blocks(ST, [(0, 128, 0, 1, 1e4), (0, 128, 127, 128, 1e4)], 15, ncol=128, use_data=False)
                            em.op("dve", lambda e: e.tensor_copy(out=MASKs[0:ST, g, :], in_=MASK[0:ST, 0:128]), reads=[b_MASK], writes=[b_MASKs])
                        attn_core(
                            ST, qc, lambda h, br, s=s: GATE[0:ST, s, h * 3 + br:h * 3 + br + 1], otm, otm_b, g, [4 * g + r for r in range(4)], q0s, None,
                            CKTs[:, s], lambda j, w1, g=g, s=s: CVs[0:w1, s, j, g, :], b_scmp, 228, 256, ZB,
                            None, None,
                            lambda k0, wu, g=g: ((WKs[64:128, g, k0:k0 + wu], b_WKs) if k0 < 512 else (WKN[64:128, g, 0:wu], b_KN)),
                            lambda k, w1, g=g: (WVs[0:w1, k // 128, g * 64:(g + 1) * 64], b_WVs),
                            [], win_units, (lambda k0, wu: None), sel_cb, QT, b_QT)
                    if s == 0 and lb == 0:
                        dbg("s_otm_cw", otm[0:ST, 0:D], [otm_b])
                        dbg("s_masks", MASKs[0:ST, :, :], [b_MASKs])
                        dbg("s_gate", GATE[0:ST, 0, :], [b_GATE])
                        dbg("s_qt", QT[0:64, :, 0:ST], [b_QT])
                    units = []
                    for u_ in range(17):
                        for h in range(16):
                            g = h // 4
                            u = {"nq": ST}

                            def A(u=u, u_=u_, h=h, g=g, s=s, qc=qc):
                                buf = u_ % 2
                                if h == 0 and u_ < 16:
                                    gather4(cache_sel, s, u_ * 4)
                                    build_unit(None, buf)
                                if u_ >= 14 and "wl" not in u:
                                    wl, wlb = wall()
                                    em.dma("sp", lambda e: e.dma_start(out=wl[0:ST, 0:1024], in_=WS[h][0:ST, 0:1024]), reads=[b_ws], writes=[wlb])
                                    u["wl"], u["wlb"] = wl, wlb
                                wu = 512 if u_ < 16 else ST
                                k0 = u_ * 512
                                ps, pb = ps_next()
                                if u_ < 16:
                                    em.op("pe", lambda e: e.matmul(ps[0:ST, 0:wu], lhsT=QT[0:64, h, qc], rhs=KU[0:64, buf, g, :], start=True, stop=True), reads=[b_QT, b_KUb[buf]], writes=[pb])
                                else:
                                    em.op("pe", lambda e: e.matmul(ps[0:ST, 0:wu], lhsT=QT[0:64, h, qc], rhs=KN[0:64, g, :], start=True, stop=True), reads=[b_QT, b_KN], writes=[pb])
                                pf, pfb = Pt()
                                if u_ >= 14:
                                    wl, wlb = u["wl"], u["wlb"]
                                    cs = max(k0, q0s - 896) - k0
                                    c0 = (k0 + cs) - (q0s - 896)
                                    tt, tb = tmpA()
                                    if cs > 0:
                                        em.op("dve", lambda e: e.tensor_scalar(out=tt[0:ST, 0:cs], in0=ps[0:ST, 0:cs], scalar1=CH[0:ST, h:h + 1], scalar2=None, op0=ALU.add), reads=[pb, b_bias], writes=[tb])
                                    em.op("dve", lambda e: e.tensor_tensor(out=tt[0:ST, cs:wu], in0=ps[0:ST, cs:wu], in1=wl[0:ST, c0:c0 + wu - cs], op=ALU.add), reads=[pb, wlb], writes=[tb])
                                    em.op("act", lambda e: e.activation(out=pf[0:ST, 0:wu], in_=tt[0:ST, 0:wu], func=AF.Exp, bias=NEGM[0:ST, h:h + 1], scale=1.0), reads=[tb, b_bias], writes=[pfb])
                                else:
                                    em.op("act", lambda e: e.activation(out=pf[0:ST, 0:wu], in_=ps[0:ST, 0:wu], func=AF.Exp, bias=NEGMC[0:ST, h:h + 1], scale=1.0), reads=[pb, b_bias], writes=[pfb])
                                p_, p_b = Pt()
                                if u_ < 16:
                                    mk = MASKs[0:ST, g, u_ * 8:u_ * 8 + 8].unsqueeze(2).broadcast_to([ST, 8, 64])
                                    em.op("dve", lambda e: e.scalar_tensor_tensor(out=p_[0:ST, 0:wu].rearrange("p (b k) -> p b k", k=64), in0=pf[0:ST, 0:wu].rearrange("p (b k) -> p b k", k=64), scalar=1.0, in1=mk, op0=ALU.mult, op1=ALU.mult),
                                          reads=[pfb, b_MASKs], writes=[p_b])
                                else:
                                    p_, p_b = pf, pfb
                                u.update(p_=p_, p_b=p_b, wu=wu)

                            def B(u=u):
                                transposeP(u, u["wu"])

                            def C(u=u, u_=u_, h=h, g=g, s=s, otm=otm, otm_b=otm_b):
                                buf = u_ % 2
                                po, pob = ps_next()
                                pT, pTb = u["pT"], u["pTb"]
                                if u_ < 16:
                                    for j in range(4):
                                        em.op("pe", lambda e, j=j: e.matmul(po[0:ST, 0:64], lhsT=pT[:, j * 128:j * 128 + ST], rhs=VU[:, buf, j, g * 64:(g + 1) * 64], start=(j == 0), stop=False), reads=[pTb, b_VUb[buf]], writes=[pob])
                                        em.op("pe", lambda e, j=j: e.matmul(po[0:ST, 64:65], lhsT=pT[:, j * 128:j * 128 + ST], rhs=onesb[:, 0:1], start=False, stop=(j == 3)), reads=[pTb, b_const], writes=[pob])
                                else:
                                    em.op("pe", lambda e: e.matmul(po[0:ST, 0:64], lhsT=pT[0:ST, 0:ST], rhs=VN[0:ST, g * 64:(g + 1) * 64], start=True, stop=False), reads=[pTb, b_KN], writes=[pob])
                                    em.op("pe", lambda e: e.matmul(po[0:ST, 64:65], lhsT=pT[0:ST, 0:ST], rhs=onesb[0:ST, 0:1], start=False, stop=True), reads=[pTb, b_const], writes=[pob])
                                if u_ == 0:
                                    em.op("dve", lambda e: e.tensor_copy(out=OACC[0:ST, h, :], in_=po[0:ST, 0:65]), reads=[pob], writes=[b_OACC])
                                else:
                                    em.op("dve", lambda e: e.tensor_tensor(out=OACC[0:ST, h, :], in0=OACC[0:ST, h, :], in1=po[0:ST, 0:65], op=ALU.add), reads=[pob, b_OACC], writes=[b_OACC])
                                if u_ == 16:
                                    s4, s4b = sm()
                                    em.op("dve", lambda e: e.reciprocal(out=s4[0:ST, 2:3], in_=OACC[0:ST, h, 64:65]), reads=[b_OACC], writes=[s4b])
                                    em.op("dve", lambda e: e.tensor_tensor(out=s4[0:ST, 3:4], in0=s4[0:ST, 2:3], in1=GATE[0:ST, s, h * 3 + 1:h * 3 + 2], op=ALU.mult), reads=[s4b, b_GATE], writes=[s4b])
                                    em.op("dve", lambda e: e.scalar_tensor_tensor(out=otm[0:ST, h * 64:(h + 1) * 64], in0=OACC[0:ST, h, 0:64], scalar=s4[0:ST, 3:4], in1=otm[0:ST, h * 64:(h + 1) * 64], op0=ALU.mult, op1=ALU.add), reads=[b_OACC, s4b, otm_b], writes=[otm_b])
                            units.append((A, B, C))
                    pipeline(units)
                    if s == 0 and lb == 0:
                        dbg("s_otm_all", otm[0:ST, 0:D], [otm_b])
                    otm_to_ZS(ST, otm, otm_b, qc)
                wo_proj(GS, lb, li)

            em.op("act", lambda e: e.copy(out=X[:, :, 0:NS * ST], in_=XSS[:]), reads=[b_XSS], writes=[b_X])
            for lb in range(2):
                sample_attn(lb)
                ffn(GS, 2 + lb)
            final_out(GS, y_s)
        em.finish()
    return nc


_CACHE = {}


def _prep_inputs(inp, c):
    b, half = c // 2, c % 2
    f = np.float32
    xp = np.zeros((TV, D), f)
    valid = np.zeros((1, TV), f)
    vbc = np.zeros((1, 128), f)
    vbw = np.zeros((1, TV), f)
    vlim = np.full((1, 64), 1e30, f)
    f0 = np.full((1, 64), -1.0, f)
    if half == 1:
        xp[:] = inp["x_prompt"][b]
        valid[:] = 1.0
        f0[0, 0] = 1e4
    else:
        xp[2048:] = inp["x_prompt"][b, :2048]
        valid[0, 2048:] = 1.0
        vbc[0, :64] = NEGV
        vbw[0, :2048] = NEGV
        vlim[0, :32] = -1.0
        f0[0, 32] = 1e4
    vec = np.zeros((NVEC, D), f)
    vec[R_BADA:R_BADA + 24] = np.asarray(inp["b_ada"]).reshape(24, D)
    vec[R_NG:R_NG + 8] = np.asarray(inp["norm_g"]).reshape(8, D)
    vec[R_BPW1:R_BPW1 + 4] = np.asarray(inp["b_pw1"]).reshape(4, D)
    vec[R_WDW:R_WDW + 62] = np.asarray(inp["w_dw"]).reshape(62, D)
    vec[R_BDW:R_BDW + 2] = inp["b_dw"]
    vec[R_LNG:R_LNG + 2] = inp["ln_g"]
    vec[R_LNB:R_LNB + 2] = inp["ln_b"]
    vec[R_BPW2:R_BPW2 + 2] = inp["b_pw2"]
    vec[R_GKV] = inp["g_kv"]
    vec[R_FG] = inp["final_g"]
    v2 = np.zeros((67, 128), f)
    for e in range(2):
        v2[e * 32:(e + 1) * 32] = np.concatenate([inp["pe_cmp"][e], inp["pe_cmp"][e]], axis=1)
    v2[64:66] = inp["b_phi1"]
    v2[66] = np.concatenate([inp["b_phi2"][0], inp["b_phi2"][1]])
    m = {
        "xp": xp, "validt": valid, "vecs": vec, "vec2": v2, "vbc": vbc, "vbw": vbw, "vlim": vlim, "f0": f0,
        "xs": np.ascontiguousarray(inp["x_sample"][NS * c:NS * (c + 1)].reshape(NS * ST, D)),
        "cvec": np.ascontiguousarray(np.concatenate([inp["c_prompt"][b:b + 1], inp["c_sample"][NS * c:NS * (c + 1)]], 0)),
        "stconv": np.ascontiguousarray(inp["state_conv"][:, NS * c:NS * (c + 1)]),
    }
    m["cache_cmp"] = np.asarray(inp["cache_kv_cmp"]).reshape(2560 * 128, 512)
    m["cache_sel"] = np.asarray(inp["cache_kv_sel"]).reshape(2560 * 128, 512)
    m["cache_win"] = np.ascontiguousarray(np.asarray(inp["cache_kv_win"])[NS * c:NS * (c + 1)].reshape(NS, 512, 512))
    m["ptab"] = np.ascontiguousarray(np.asarray(inp["page_table"])[NS * c:NS * (c + 1)]).astype(np.int32)
    for k in ("w_ada", "w_pw1", "w_pw2", "w_kv", "w_gate", "w_up", "w_down", "w_phi1", "w_phi2", "b_phi2", "rel_table", "w_qg", "w_o"):
        m[k] = np.asarray(inp[k])
    return m


def kernel(stage=9, cores=None, **inp):
    inp = {k: np.asarray(v) for k, v in inp.items()}
    if stage not in _CACHE:
        _CACHE[stage] = build(stage)
    nc = _CACHE[stage]
    if cores is not None:
        in_maps = [_prep_inputs(inp, c) for c in cores]
        res = run_bass_kernel_spmd(nc, in_maps, core_ids=list(range(len(cores))))
        kernel.raw = res.results
        return None
    in_maps = [_prep_inputs(inp, c) for c in range(8)]
    res = run_bass_kernel_spmd(nc, in_maps, core_ids=list(range(8)))
    R = res.results
    kernel.raw = R
    f = np.float32
    B, T = 4, 4096
    y_p = np.zeros((B, T, D), f)
    y_s = np.zeros((32, ST, D), f)
    rows = np.zeros((B, T, 1536), f)
    rows_s = np.zeros((32, ST, 1536), f)
    conv_p = np.zeros((2, B, 30, D), f)
    conv_s = np.zeros((2, 32, 30, D), f)
    win_s = np.zeros((32, 512, 2, 4, 64), f)
    for c in range(8):
        b, half = c // 2, c % 2
        rows[b, half * 2048:(half + 1) * 2048] = R[c]["rows_p"]
        y_p[b, half * 2048:(half + 1) * 2048] = R[c]["y_p"]
        y_s[NS * c:NS * (c + 1)] = R[c]["y_s"].reshape(NS, ST, D)
        rows_s[NS * c:NS * (c + 1)] = R[c]["rows_s"].reshape(NS, ST, 1536)
        if half == 1:
            conv_p[:, b] = R[c]["conv_p"]
        conv_s[:, NS * c:NS * (c + 1)] = R[c]["conv_s"]
        win_s[NS * c:NS * (c + 1)] = R[c]["win_s"].reshape(NS, 512, 2, 4, 64)
    rows = rows.reshape(B, T, 3, 2, 4, 64)
    rows_s = rows_s.reshape(32, ST, 3, 2, 4, 64)
    return (y_p, y_s, np.ascontiguousarray(rows[:, :, 0]), np.ascontiguousarray(rows_s[:, :, 0]),
            np.ascontiguousarray(rows[:, :, 1]), np.ascontiguousarray(rows_s[:, :, 1]),
            np.ascontiguousarray(rows[:, -512:, 2]), win_s, conv_p, conv_s)
```

```python
import numpy as np
from contextlib import ExitStack
import concourse.bass as bass
import concourse.mybir as mybir
from concourse.bass_utils import run_bass_kernel_spmd

F32 = mybir.dt.float32
BF16 = mybir.dt.bfloat16
I32 = mybir.dt.int32
U32 = mybir.dt.uint32
AF = mybir.ActivationFunctionType
ALU = mybir.AluOpType
AX = mybir.AxisListType

EPOCH = 30000
N_DMA_SEMS = 40
STRICT = False

D = 1024
KC = 8
DFF = 2816
FC = 22
TV = 4096
NG = 8
GN = 512
NS = 4
ST = 4
EPS = 1e-6
DEBUG = False
NEGV = -30000.0
QH0 = 2048


def _bucket_thresholds():
    import math
    th = []
    nb, ex, md = 32, 16, 1024
    def bucket(n):
        if n < ex:
            return n
        v = np.float32(np.log(np.float32(n) / np.float32(ex))) / np.float32(math.log(md / ex)) * np.float32(nb - ex)
        return min(ex + int(v), nb - 1)
    for b in range(1, 32):
        n = 0
        while bucket(n) < b:
            n += 1
        th.append(n)
    return th


TH = _bucket_thresholds()

R_BADA, R_NG, R_BPW1, R_WDW, R_BDW, R_LNG, R_LNB, R_BPW2, R_GKV, R_FG, NVEC = 0, 24, 32, 36, 98, 100, 102, 104, 106, 107, 108


class Buf:
    __slots__ = ("name", "w", "r")

    def __init__(self, name):
        self.name = name
        self.w = None
        self.r = {}


class Emitter:
    ENGS = ("pe", "act", "dve", "pool", "sp")

    def __init__(self, nc, stack):
        self.nc = nc
        self.stack = stack
        self.prog = {e: [] for e in self.ENGS}
        self.cnt = {e: 0 for e in self.ENGS}
        self.esems = {e: [] for e in self.ENGS}
        self.waited = {e: {} for e in self.ENGS}
        self.dma_sems = [stack.enter_context(nc.semaphore("dq%d" % i)) for i in range(N_DMA_SEMS)]
        self.dma_tot = [0] * N_DMA_SEMS
        self.dma_rr = 0
        self.nbuf = 0

    def buf(self, name=None):
        self.nbuf += 1
        return Buf(name or "b%d" % self.nbuf)

    def _esem(self, e, epoch):
        while len(self.esems[e]) <= epoch:
            self.esems[e].append(self.stack.enter_context(self.nc.semaphore("s_%s%d" % (e, len(self.esems[e])))))
        return self.esems[e][epoch]

    def _tok_sem(self, tok):
        if tok[0] == "dma":
            return self.dma_sems[tok[1]], tok[2], ("dma", tok[1])
        e, idx = tok
        return self._esem(e, idx // EPOCH), idx % EPOCH + 1, (e, idx // EPOCH)

    def _collect(self, e, reads, writes, is_dma):
        toks = set()
        same_ok = set()
        for b in reads:
            for w in (b.w or ()):
                toks.add(w)
                if e == "pe":
                    same_ok.add(w)
        for b in writes:
            for w in (b.w or ()):
                if w not in toks:
                    same_ok.add(w)
                toks.add(w)
            for t in b.r.values():
                if t not in toks:
                    same_ok.add(t)
                toks.add(t)
        waits = []
        for t in toks:
            if t[0] != "dma" and t[0] == e and not is_dma and (e == "pe" or (STRICT is False and t in same_ok)):
                continue
            sem, val, key = self._tok_sem(t)
            if self.waited[e].get(key, 0) >= val:
                continue
            self.waited[e][key] = val
            waits.append((sem, val))
        return waits

    def op(self, e, fn, reads=(), writes=()):
        waits = self._collect(e, reads, writes, False)
        idx = self.cnt[e]
        self.cnt[e] += 1
        sem = self._esem(e, idx // EPOCH)
        self.prog[e].append((waits, fn, sem, 1))
        tok = (e, idx)
        for b in reads:
            b.r[e] = tok
        for b in writes:
            b.w = (tok,)
            b.r = {}
        return tok

    def dma(self, e, fn, reads=(), writes=()):
        s = self.dma_rr
        self.dma_rr = (self.dma_rr + 1) % N_DMA_SEMS
        waits = self._collect(e, reads, writes, True)
        prev = self.dma_tot[s]
        if prev > 0 and self.waited[e].get(("dma", s), 0) < prev:
            self.waited[e][("dma", s)] = prev
            waits.append((self.dma_sems[s], prev))
        self.dma_tot[s] = prev + 16
        tok = ("dma", s, prev + 16)
        self.prog[e].append((waits, fn, self.dma_sems[s], 16))
        for b in reads:
            b.r["dma%d" % s] = tok
        for b in writes:
            if b.w and not b.r and all(t[0] == "dma" for t in b.w):
                b.w = b.w + (tok,)
            else:
                b.w = (tok,)
            b.r = {}
        return tok

    def barrier(self):
        toks = []
        for e2 in self.ENGS:
            if self.cnt[e2] > 0:
                toks.append((e2, self.cnt[e2] - 1))
        for e in self.ENGS:
            waits = []
            for t in toks:
                if t[0] == e:
                    continue
                sem, val, key = self._tok_sem(t)
                if self.waited[e].get(key, 0) < val:
                    self.waited[e][key] = val
                    waits.append((sem, val))
            for s in range(N_DMA_SEMS):
                if self.dma_tot[s] > 0 and self.waited[e].get(("dma", s), 0) < self.dma_tot[s]:
                    self.waited[e][("dma", s)] = self.dma_tot[s]
                    waits.append((self.dma_sems[s], self.dma_tot[s]))
            self.prog[e].append((waits, None, None, 0))

    def finish(self):
        waits = []
        for s in range(N_DMA_SEMS):
            if self.dma_tot[s] > 0:
                waits.append((self.dma_sems[s], self.dma_tot[s]))
        self.prog["sp"].append((waits, None, None, 0))
        engmap = {"pe": "tensor", "act": "scalar", "dve": "vector", "pool": "gpsimd", "sp": "sync"}
        with self.nc.Block() as block:
            for e in self.ENGS:
                prog = self.prog[e]

                def body(eng, prog=prog):
                    for waits, fn, sem, inc in prog:
                        for (ws, wv) in waits:
                            eng.wait_ge(ws, wv)
                        if fn is not None:
                            fn(eng).then_inc(sem, inc)

                getattr(block, engmap[e])(body)


class Grp:
    def __init__(self, S, n, seqs):
        self.S, self.n, self.N, self.seqs = S, n, S * n, seqs


def build(stage=9):
    nc = bass.Bass("TRN2", target_bir_lowering=False)

    def din(name, shape, dt=F32):
        return nc.dram_tensor(name, list(shape), dt, kind="ExternalInput").ap()

    def dout(name, shape, dt=F32):
        return nc.dram_tensor(name, list(shape), dt, kind="ExternalOutput").ap()

    xp = din("xp", [TV, D])
    xs = din("xs", [NS * ST, D])
    cvec = din("cvec", [1 + NS, D])
    vecs = din("vecs", [NVEC, D])
    vec2 = din("vec2", [67, 128])
    validt = din("validt", [1, TV])
    vbc_d = din("vbc", [1, 128])
    vbw_d = din("vbw", [1, TV])
    vlim_d = din("vlim", [1, 64])
    f0_d = din("f0", [1, 64])
    stconv = din("stconv", [2, NS, 30, D])
    w_ada = din("w_ada", [4, 2, D, 3 * D])
    w_pw1 = din("w_pw1", [2, D, 2 * D])
    w_pw2 = din("w_pw2", [2, D, D])
    w_kv = din("w_kv", [D, 1536])
    w_gate = din("w_gate", [4, D, DFF])
    w_up = din("w_up", [4, D, DFF])
    w_down = din("w_down", [4, DFF, D])
    w_phi1 = din("w_phi1", [2, 2048, 128])
    w_phi2 = din("w_phi2", [2, 128, 64])
    b_phi2 = din("b_phi2", [2, 64])
    rel_table = din("rel_table", [32, 16])
    w_qg = din("w_qg", [2, D, 1072])
    w_o = din("w_o", [2, D, D])
    cache_cmp = din("cache_cmp", [2560 * 128, 512])
    cache_sel = din("cache_sel", [2560 * 128, 512])
    cache_win = din("cache_win", [NS, 512, 512])
    ptab = din("ptab", [NS, 64], I32)

    rows_p = dout("rows_p", [2048, 1536])
    rows_s = dout("rows_s", [NS * ST, 1536])
    conv_p = dout("conv_p", [2, 30, D])
    conv_s = dout("conv_s", [2, NS, 30, D])
    y_p = dout("y_p", [2048, D])
    y_s = dout("y_s", [NS * ST, D])
    win_s = dout("win_s", [NS, 512, 512])

    XS = nc.dram_tensor("XS", [128, KC, 2048], F32, kind="Internal").ap()
    GSH = nc.dram_tensor("GSH", [16, 1920], F32, kind="Internal")
    GSL = nc.dram_tensor("GSL", [16, 1920], F32, kind="Internal")
    WS = nc.dram_tensor("WS", [16, 128, 1664], F32, kind="Internal").ap()

    with ExitStack() as st:
        em = Emitter(nc, st)

        def sb(name, shape, dt=F32):
            return st.enter_context(nc.sbuf_tensor(name, list(shape), dt))

        def dbg(name, ap, bufs):
            if not DEBUG:
                return
            o = nc.dram_tensor("dbg_" + name, list(ap.shape), ap.dtype, kind="ExternalOutput").ap()
            em.dma("sp", lambda e: e.dma_start(out=o, in_=ap), reads=bufs)

        ident = sb("ident", [128, 128])
        identb = sb("identb", [128, 128], BF16)
        antib = sb("antib", [128, 128], BF16)
        onesb = sb("onesb", [128, 128], BF16)
        epsc = sb("epsc", [128, 1])
        scr0 = sb("scr0", [128, 128])
        b_const = em.buf("const")
        b_scr0 = em.buf()
        em.op("pool", lambda e: e.iota(scr0[:], [[1, 128]], base=0, channel_multiplier=-1, allow_small_or_imprecise_dtypes=True), writes=[b_scr0])
        em.op("dve", lambda e: e.tensor_scalar(out=ident[:], in0=scr0[:], scalar1=0.0, scalar2=None, op0=ALU.is_equal), reads=[b_scr0], writes=[b_const])
        em.op("dve", lambda e: e.tensor_scalar(out=identb[:], in0=scr0[:], scalar1=0.0, scalar2=None, op0=ALU.is_equal), reads=[b_scr0], writes=[b_const])
        em.op("dve", lambda e: e.memset(onesb[:], 1.0), writes=[b_const])
        em.op("dve", lambda e: e.memset(epsc[:], EPS), writes=[b_const])
        em.op("pool", lambda e: e.iota(scr0[:], [[1, 128]], base=-127, channel_multiplier=1, allow_small_or_imprecise_dtypes=True), reads=[b_const], writes=[b_scr0])
        em.op("dve", lambda e: e.tensor_scalar(out=antib[:], in0=scr0[:], scalar1=0.0, scalar2=None, op0=ALU.is_equal), reads=[b_scr0], writes=[b_const])

        NPS = 8
        pst = [st.enter_context(nc.psum_tensor("ps%d" % i, [128, 512], F32)) for i in range(NPS)]
        psb = [em.buf("ps%d" % i) for i in range(NPS)]
        ps_rr = [0]
        ps_nrot = [NPS]

        def ps_next():
            i = ps_rr[0] % ps_nrot[0]
            ps_rr[0] = (i + 1) % ps_nrot[0]
            return pst[i], psb[i]

        def rot(name, shape, dt, n):
            tiles = [sb("%s%d" % (name, i), shape, dt) for i in range(n)]
            bufs = [em.buf("%s%d" % (name, i)) for i in range(n)]
            c = [0]

            def nxt():
                i = c[0]
                c[0] = (i + 1) % n
                return tiles[i], bufs[i]
            return nxt

        bigt = rot("bigt", [128, 1536], F32, 2)
        tmpA = rot("tmpA", [128, GN], F32, 4)
        sqt = rot("sqt", [128, GN], BF16, 2)

        TABf = sb("TABf", [32, 16])
        TABh = sb("TABh", [32, 16], BF16)
        TABl = sb("TABl", [32, 16], BF16)
        CH = sb("CH", [128, 16])
        BCALL = sb("BCALL", [128, 16, 32])
        VBC = sb("VBC", [1, 128], BF16)
        VBW = sb("VBW", [1, 1152], BF16)
        VLIM = sb("VLIM", [128, 64])
        F0 = sb("F0", [128, 64])
        NEGM = sb("NEGM", [128, 16])
        NEGMC = sb("NEGMC", [128, 16])
        b_tab = em.buf("tab")
        b_bias = em.buf("bias")
        em.dma("sp", lambda e: e.dma_start(out=TABf[:], in_=rel_table), writes=[b_tab])
        em.dma("sp", lambda e: e.dma_start(out=CH[:], in_=rel_table[31:32, :].broadcast_to([128, 16])), writes=[b_bias])
        em.dma("pool", lambda e: e.dma_start(out=VBC[:], in_=vbc_d), writes=[b_bias])
        em.dma("pool", lambda e: e.dma_start(out=VBW[:], in_=vbw_d[:, 1536:1536 + 1152]), writes=[b_bias])
        em.dma("sp", lambda e: e.dma_start(out=VLIM[:], in_=vlim_d.broadcast_to([128, 64])), writes=[b_bias])
        em.dma("sp", lambda e: e.dma_start(out=F0[:], in_=f0_d.broadcast_to([128, 64])), writes=[b_bias])
        em.op("dve", lambda e: e.memset(NEGM[:], 0.0), writes=[b_bias])
        em.op("dve", lambda e: e.tensor_copy(out=NEGMC[:], in_=CH[:]), reads=[b_bias], writes=[b_bias])
        em.op("dve", lambda e: e.tensor_copy(out=TABh[:], in_=TABf[:]), reads=[b_tab], writes=[b_tab])
        em.op("dve", lambda e: e.tensor_tensor(out=TABf[:], in0=TABf[:], in1=TABh[:], op=ALU.subtract), reads=[b_tab], writes=[b_tab])
        em.op("dve", lambda e: e.tensor_copy(out=TABl[:], in_=TABf[:]), reads=[b_tab], writes=[b_tab])
        GL_ = 1920
        CW = 384
        with ExitStack() as st3:
            def sb3(name, shape, dt=F32):
                return st3.enter_context(nc.sbuf_tensor(name, list(shape), dt))
            Drow = sb3("Drow", [32, CW])
            BKT = sb3("BKT", [32, CW])
            OH = sb3("OH", [32, CW], BF16)
            PIDX = sb3("PIDX", [32, 1])
            GT = sb3("GT", [16, CW])
            GTh = sb3("GTh", [16, CW], BF16)
            GTh32 = sb3("GTh32", [16, CW])
            VM = sb3("VM", [16, CW])
            NA = sb3("NA", [16, CW])
            bhk = sb3("bhk", [128, 2, 1664], BF16)
            wst = sb3("wst", [128, 1664])
            b_g = em.buf("gtab")
            b_bhk = em.buf()
            b_wst = em.buf()
            b_gs = em.buf("gs")
            em.op("pool", lambda e: e.iota(PIDX[:], [[1, 1]], base=0, channel_multiplier=1, allow_small_or_imprecise_dtypes=True), writes=[b_g])
            for ci in range(5):
                if ci < 3:
                    i0_, d0, vlo, vhi = ci * CW, 1023 - ci * CW, 0, 1023
                else:
                    i0_, d0, vlo, vhi = (ci - 3) * CW, 639 - (ci - 3) * CW, 128, 639
                col0 = ci * CW
                em.op("pool", lambda e, d0=d0: e.iota(Drow[:], [[-1, CW]], base=d0, channel_multiplier=0, allow_small_or_imprecise_dtypes=True), reads=[b_g], writes=[b_g])
                for i_, th in enumerate(TH):
                    if i_ == 0:
                        em.op("dve", lambda e, th=th: e.tensor_scalar(out=BKT[:], in0=Drow[:], scalar1=float(th), scalar2=None, op0=ALU.is_ge), reads=[b_g], writes=[b_g])
                    else:
                        em.op("dve", lambda e, th=th: e.scalar_tensor_tensor(out=BKT[:], in0=Drow[:], scalar=float(th), in1=BKT[:], op0=ALU.is_ge, op1=ALU.add), reads=[b_g], writes=[b_g])
                em.op("dve", lambda e: e.tensor_scalar(out=OH[:], in0=BKT[:], scalar1=PIDX[:, 0:1], scalar2=None, op0=ALU.is_equal), reads=[b_g], writes=[b_g])
                a_ = max(vlo, i0_) - i0_
                b_ = min(vhi + 1, i0_ + CW) - i0_
                em.op("dve", lambda e: e.memset(VM[:], 0.0), reads=[b_g], writes=[b_g])
                if b_ > a_:
                    em.op("dve", lambda e, a_=a_, b_=b_: e.memset(VM[:, a_:b_], 1.0), reads=[b_g], writes=[b_g])
                em.op("dve", lambda e: e.tensor_scalar(out=NA[:], in0=VM[:], scalar1=-NEGV, scalar2=NEGV, op0=ALU.mult, op1=ALU.add), reads=[b_g], writes=[b_g])
                pt, pb = ps_next()
                em.op("pe", lambda e, pt=pt: e.matmul(pt[0:16, 0:CW], lhsT=TABh[:], rhs=OH[:], start=True, stop=False), reads=[b_tab, b_g], writes=[pb])
                em.op("pe", lambda e, pt=pt: e.matmul(pt[0:16, 0:CW], lhsT=TABl[:], rhs=OH[:], start=False, stop=True), reads=[b_tab, b_g], writes=[pb])
                em.op("dve", lambda e, pt=pt: e.tensor_tensor(out=GT[:], in0=pt[0:16, 0:CW], in1=VM[:], op=ALU.mult), reads=[pb, b_g], writes=[b_g])
                em.op("dve", lambda e: e.tensor_tensor(out=GT[:], in0=GT[:], in1=NA[:], op=ALU.add), reads=[b_g], writes=[b_g])
                em.op("dve", lambda e: e.tensor_copy(out=GTh[:], in_=GT[:]), reads=[b_g], writes=[b_g])
                em.op("dve", lambda e: e.tensor_copy(out=GTh32[:], in_=GTh[:]), reads=[b_g], writes=[b_g])
                em.op("dve", lambda e: e.tensor_tensor(out=GT[:], in0=GT[:], in1=GTh32[:], op=ALU.subtract), reads=[b_g], writes=[b_g])
                em.dma("sp", lambda e, col0=col0: e.dma_start(out=GSH.ap()[:, col0:col0 + CW], in_=GTh32[:]), reads=[b_g], writes=[b_gs])
                em.dma("sp", lambda e, col0=col0: e.dma_start(out=GSL.ap()[:, col0:col0 + CW], in_=GT[:]), reads=[b_g], writes=[b_gs])
            b_ws = em.buf("ws")
            for h in range(16):
                for hl, GS_ in enumerate((GSH, GSL)):
                    em.dma("pool", lambda e, hl=hl, GS_=GS_, h=h: e.dma_start(out=bhk[:, hl, 0:1024], in_=bass.AP(GS_, h * GL_, [[1, 128], [1, 1024]])), reads=[b_gs], writes=[b_bhk])
                    em.dma("pool", lambda e, hl=hl, GS_=GS_, h=h: e.dma_start(out=bhk[:, hl, 1024:1664], in_=bass.AP(GS_, h * GL_ + 1152, [[1, 128], [1, 640]])), reads=[b_gs], writes=[b_bhk])
                for c0 in range(0, 1664, 512):
                    w_ = min(512, 1664 - c0)
                    pt, pb = ps_next()
                    em.op("pe", lambda e, c0=c0, w_=w_, pt=pt: e.matmul(pt[:, 0:w_], lhsT=antib[:], rhs=bhk[:, 0, c0:c0 + w_], start=True, stop=False), reads=[b_bhk, b_const], writes=[pb])
                    em.op("pe", lambda e, c0=c0, w_=w_, pt=pt: e.matmul(pt[:, 0:w_], lhsT=antib[:], rhs=bhk[:, 1, c0:c0 + w_], start=False, stop=True), reads=[b_bhk, b_const], writes=[pb])
                    em.op("act", lambda e, c0=c0, w_=w_, pt=pt: e.copy(out=wst[:, c0:c0 + w_], in_=pt[:, 0:w_]), reads=[pb], writes=[b_wst])
                em.op("dve", lambda e, h=h: e.tensor_copy(out=BCALL[:, h, :], in_=wst[:, 31:1024:32]), reads=[b_wst], writes=[b_bias])
                em.dma("sp", lambda e, h=h: e.dma_start(out=WS[h], in_=wst[:]), reads=[b_wst], writes=[b_ws])
            em.barrier()

        VEC = sb("VEC", [128, KC, NVEC])
        b_vec = em.buf("vec")
        vrows, b_vrows = bigt()
        em.dma("sp", lambda e: e.dma_start(out=vrows[0:NVEC, 0:D], in_=vecs), writes=[b_vrows])
        for kc in range(KC):
            pt, pb = ps_next()
            em.op("pe", lambda e, kc=kc, pt=pt: e.transpose(pt[:, 0:NVEC], vrows[0:NVEC, kc * 128:(kc + 1) * 128], ident[0:NVEC, 0:NVEC]), reads=[b_vrows, b_const], writes=[pb])
            em.op("dve", lambda e, kc=kc, pt=pt: e.tensor_copy(out=VEC[:, kc, :], in_=pt[:, 0:NVEC]), reads=[pb], writes=[b_vec])

        def vcol(r, kc):
            return VEC[:, kc, r:r + 1]

        VEC2 = sb("VEC2", [128, 67])
        v2rows, b_v2rows = bigt()
        em.dma("sp", lambda e: e.dma_start(out=v2rows[0:67, 0:128], in_=vec2), writes=[b_v2rows])
        pt, pb = ps_next()
        em.op("pe", lambda e, pt=pt: e.transpose(pt[:, 0:67], v2rows[0:67, 0:128], ident[0:67, 0:67]), reads=[b_v2rows, b_const], writes=[pb])
        em.op("dve", lambda e, pt=pt: e.tensor_copy(out=VEC2[:], in_=pt[:, 0:67]), reads=[pb], writes=[b_vec])

        NSEQ = 1 + NS
        crow, b_crow = bigt()
        em.dma("sp", lambda e: e.dma_start(out=crow[0:NSEQ, 0:D], in_=cvec), writes=[b_crow])
        em.op("act", lambda e: e.activation(out=crow[0:NSEQ, 0:D], in_=crow[0:NSEQ, 0:D], func=AF.Silu), reads=[b_crow], writes=[b_crow])
        scT = sb("scT", [128, KC, NSEQ], BF16)
        b_scT = em.buf()
        for kc in range(KC):
            pt, pb = ps_next()
            em.op("pe", lambda e, kc=kc, pt=pt: e.transpose(pt[:, 0:NSEQ], crow[0:NSEQ, kc * 128:(kc + 1) * 128], ident[0:NSEQ, 0:NSEQ]), reads=[b_crow, b_const], writes=[pb])
            em.op("dve", lambda e, kc=kc, pt=pt: e.tensor_copy(out=scT[:, kc, :], in_=pt[:, 0:NSEQ]), reads=[pb], writes=[b_scT])
        modA = [sb("modA%d" % i, [128, KC, NSEQ]) for i in range(8)]
        modB = [sb("modB%d" % i, [128, KC, NSEQ]) for i in range(8)]
        modG = [sb("modG%d" % i, [128, KC, NSEQ]) for i in range(8)]
        b_mod = [em.buf("mod%d" % i) for i in range(8)]

        WSLOT = 2 * KC * 256
        NW = 3
        wsl = [sb("wsl%d" % i, [128, WSLOT], BF16) for i in range(NW)]
        wsb = [em.buf("wsl%d" % i) for i in range(NW)]
        w_rr = [0]

        def wload(parts):
            i = w_rr[0]
            w_rr[0] = (i + 1) % NW
            for (off, kcn, cols, src) in parts:
                dst = wsl[i][:, off:off + kcn * cols].rearrange("p (k c) -> p k c", k=kcn)
                em.dma("pool", lambda e, dst=dst, src=src: e.dma_start(out=dst, in_=src), writes=[wsb[i]])
            return wsl[i], wsb[i]

        def wview(t, off, kcn, cols):
            return t[:, off:off + kcn * cols].rearrange("p (k c) -> p k c", k=kcn)

        n_li = 8
        blk = 0
        for li in range(n_li):
            l, i = li // 2, li % 2
            for cb in range(6):
                src = w_ada[l, i].rearrange("(kc p) c -> p kc c", p=128)[:, :, cb * 512:(cb + 1) * 512]
                wt, wb = wload([(0, KC, 512, src)])
                wv = wview(wt, 0, KC, 512)
                pt, pb = ps_next()
                for c4 in range(4):
                    for kc in range(KC):
                        em.op("pe", lambda e, c4=c4, kc=kc, pt=pt, wv=wv: e.matmul(pt[:, c4 * 8:c4 * 8 + NSEQ], lhsT=wv[:, kc, c4 * 128:(c4 + 1) * 128], rhs=scT[:, kc, :], start=(kc == 0), stop=(kc == KC - 1)),
                              reads=[wb, b_scT], writes=[pb])
                for c4 in range(4):
                    ch = cb * 4 + c4
                    which, kc = ch // 8, ch % 8
                    brow = R_BADA + li * 3 + which
                    src_ps = pt[:, c4 * 8:c4 * 8 + NSEQ]
                    if which == 0:
                        em.op("dve", lambda e, kc=kc, li=li, src_ps=src_ps, brow=brow: e.tensor_scalar(out=modB[li][:, kc, :], in0=src_ps, scalar1=vcol(brow, kc), scalar2=None, op0=ALU.add), reads=[pb, b_vec], writes=[b_mod[li]])
                    elif which == 1:
                        em.op("dve", lambda e, kc=kc, li=li, src_ps=src_ps, brow=brow: e.tensor_scalar(out=modA[li][:, kc, :], in0=src_ps, scalar1=vcol(brow, kc), scalar2=1.0, op0=ALU.add, op1=ALU.add), reads=[pb, b_vec], writes=[b_mod[li]])
                        em.op("dve", lambda e, kc=kc, li=li: e.tensor_scalar(out=modA[li][:, kc, :], in0=modA[li][:, kc, :], scalar1=vcol(R_NG + li, kc), scalar2=None, op0=ALU.mult), reads=[b_mod[li], b_vec], writes=[b_mod[li]])
                    else:
                        em.op("dve", lambda e, kc=kc, li=li, src_ps=src_ps, brow=brow: e.tensor_scalar(out=modG[li][:, kc, :], in0=src_ps, scalar1=vcol(brow, kc), scalar2=None, op0=ALU.add), reads=[pb, b_vec], writes=[b_mod[li]])

        X = sb("X", [128, KC, GN])
        H = sb("H", [128, KC, GN], BF16)
        ZS = sb("ZS", [128, KC, GN], BF16)
        BIG = sb("BIG", [128, KC * (GN + 30) + KC * GN])
        UB = BIG[:, 0:KC * (GN + 30)].rearrange("p (k t) -> p k t", k=KC)
        Y = BIG[:, KC * (GN + 30):].rearrange("p (k t) -> p k t", k=KC)
        BIGb = BIG[:].bitcast(BF16)
        HID = BIGb[:, 0:FC * GN].rearrange("p (k t) -> p k t", k=FC)
        QT = BIGb[:, 0:16 * GN].rearrange("p (k t) -> p k t", k=16)
        RSTD = sb("RSTD", [128, GN])
        XSS = sb("XSS", [128, KC, NS * ST])
        b_XSS = em.buf("XSS")
        b_XS = em.buf("XS")
        b_X, b_H, b_ZS, b_BIG, b_RSTD, b_MEAN, b_VALID = [em.buf(n) for n in ("X", "H", "ZS", "BIG", "RSTD", "MEAN", "VALID")]
        b_UB = b_Y = b_HID = b_QT = b_BIG
        b_hist = [em.buf("hist0"), em.buf("hist1")]

        KT = sb("KT", [128, 4, TV], BF16)
        SV = sb("SV", [128, 32, 256], BF16)
        WV = sb("WV", [128, 20, 256], BF16)
        CKT = sb("CKT", [64, 4, 128], BF16)
        CV = sb("CV", [128, 4, 64], BF16)
        W2 = sb("W2", [128, 2, 64], BF16)
        B2V = sb("B2V", [128, 64])
        stA = ExitStack()
        st.enter_context(stA)

        def sbA(name, shape, dt=F32):
            return stA.enter_context(nc.sbuf_tensor(name, list(shape), dt))
        VALID = sbA("VALID", [128, GN])
        MEAN = sbA("MEAN", [128, GN])
        hist = [sbA("hist%d" % l, [128, KC, 30]) for l in range(2)]
        HIDC = sbA("HIDC", [128, 2, 4, 128], BF16)
        CT = sbA("CT", [128, 2, 2, GN], BF16)
        b_KT, b_SV, b_WV, b_HIDC, b_CT, b_CK, b_W1T = [em.buf(n) for n in ("KT", "SV", "WV", "HIDC", "CT", "CK", "W1T")]
        for e_ in range(2):
            em.dma("pool", lambda e, e_=e_: e.dma_start(out=W2[:, e_, :], in_=w_phi2[e_]), writes=[b_W1T])
        em.dma("sp", lambda e: e.dma_start(out=B2V[:], in_=b_phi2[1:2, :].broadcast_to([128, 64])), writes=[b_W1T])

        def load_xT(G, src_rows):
            N = G.N
            for t0 in range(0, N, 128):
                tn = min(128, N - t0)
                xt, xb = bigt()
                em.dma("sp", lambda e, xt=xt, t0=t0, tn=tn: e.dma_start(out=xt[0:tn, 0:D], in_=src_rows[t0:t0 + tn, :]), writes=[xb])
                for k0 in range(0, KC, 4):
                    pt, pb = ps_next()
                    for kk in range(4):
                        kc = k0 + kk
                        em.op("pe", lambda e, xt=xt, kc=kc, kk=kk, tn=tn, pt=pt: e.transpose(pt[:, kk * 128:kk * 128 + tn], xt[0:tn, kc * 128:(kc + 1) * 128], ident[0:tn, 0:tn]), reads=[xb, b_const], writes=[pb])
                    em.op("act", lambda e, k0=k0, t0=t0, tn=tn, pt=pt: e.copy(out=X[:, k0:k0 + 4, t0:t0 + tn], in_=pt[:].rearrange("p (k t) -> p k t", k=4)[:, :, 0:tn]), reads=[pb], writes=[b_X])

        def rms_stats(G):
            N = G.N
            pt, pb = ps_next()
            for kc in range(KC):
                sq, sqb = sqt()
                em.op("act", lambda e, kc=kc, sq=sq: e.activation(out=sq[:, 0:N], in_=X[:, kc, 0:N], func=AF.Square), reads=[b_X], writes=[sqb])
                em.op("pe", lambda e, kc=kc, pt=pt, sq=sq: e.matmul(pt[:, 0:N], lhsT=onesb[:], rhs=sq[:, 0:N], start=(kc == 0), stop=(kc == KC - 1)), reads=[sqb, b_const], writes=[pb])
            em.op("act", lambda e, pt=pt: e.activation(out=RSTD[:, 0:N], in_=pt[:, 0:N], func=AF.Sqrt, bias=epsc[:, 0:1], scale=1.0 / D), reads=[pb, b_const], writes=[b_RSTD])
            em.op("dve", lambda e: e.reciprocal(out=RSTD[:, 0:N], in_=RSTD[:, 0:N]), reads=[b_RSTD], writes=[b_RSTD])

        def norm_mod(G, Acol, Bcol, abufs, out=None, out_buf=None):
            rms_stats(G)
            if out is None:
                out, out_buf = H, b_H
            for kc in range(KC):
                for si, seq in enumerate(G.seqs):
                    c0, c1 = si * G.n, (si + 1) * G.n
                    if Bcol is None:
                        em.op("dve", lambda e, kc=kc, c0=c0, c1=c1, seq=seq: e.scalar_tensor_tensor(out=out[:, kc, c0:c1], in0=X[:, kc, c0:c1], scalar=Acol(kc, seq), in1=RSTD[:, c0:c1], op0=ALU.mult, op1=ALU.mult),
                              reads=[b_X, b_RSTD] + abufs, writes=[out_buf])
                    else:
                        tt, tb = tmpA()
                        em.op("dve", lambda e, kc=kc, c0=c0, c1=c1, seq=seq, tt=tt: e.scalar_tensor_tensor(out=tt[:, c0:c1], in0=X[:, kc, c0:c1], scalar=Acol(kc, seq), in1=RSTD[:, c0:c1], op0=ALU.mult, op1=ALU.mult),
                              reads=[b_X, b_RSTD] + abufs, writes=[tb])
                        em.op("act", lambda e, kc=kc, c0=c0, c1=c1, seq=seq, tt=tt: e.activation(out=out[:, kc, c0:c1], in_=tt[:, c0:c1], func=AF.Identity, bias=Bcol(kc, seq), scale=1.0),
                              reads=[tb] + abufs, writes=[out_buf])

        def ffn(G, l):
            N = G.N
            li = l * 2 + 1
            norm_mod(G, lambda kc, seq: modA[li][:, kc, seq:seq + 1], lambda kc, seq: modB[li][:, kc, seq:seq + 1], [b_mod[li]])
            wg = w_gate[l].rearrange("(kc p) c -> p kc c", p=128)
            wu = w_up[l].rearrange("(kc p) c -> p kc c", p=128)
            for blk in range(FC // 2):
                wt, wb = wload([(0, KC, 256, wg[:, :, blk * 256:(blk + 1) * 256]), (KC * 256, KC, 256, wu[:, :, blk * 256:(blk + 1) * 256])])
                wgv = wview(wt, 0, KC, 256)
                wuv = wview(wt, KC * 256, KC, 256)
                for c2 in range(2):
                    j = blk * 2 + c2
                    pg, pgb = ps_next()
                    pu, pub = ps_next()
                    for kc in range(KC):
                        em.op("pe", lambda e, kc=kc, c2=c2, pg=pg, wgv=wgv: e.matmul(pg[:, 0:N], lhsT=wgv[:, kc, c2 * 128:(c2 + 1) * 128], rhs=H[:, kc, 0:N], start=(kc == 0), stop=(kc == KC - 1)), reads=[wb, b_H], writes=[pgb])
                    for kc in range(KC):
                        em.op("pe", lambda e, kc=kc, c2=c2, pu=pu, wuv=wuv: e.matmul(pu[:, 0:N], lhsT=wuv[:, kc, c2 * 128:(c2 + 1) * 128], rhs=H[:, kc, 0:N], start=(kc == 0), stop=(kc == KC - 1)), reads=[wb, b_H], writes=[pub])
                    tt, tb = tmpA()
                    em.op("act", lambda e, pg=pg, tt=tt: e.activation(out=tt[:, 0:N], in_=pg[:, 0:N], func=AF.Silu), reads=[pgb], writes=[tb])
                    em.op("dve", lambda e, pu=pu, tt=tt, j=j: e.tensor_tensor(out=HID[:, j, 0:N], in0=pu[:, 0:N], in1=tt[:, 0:N], op=ALU.mult), reads=[pub, tb], writes=[b_HID])
            wd = w_down[l].rearrange("(kc p) c -> p kc c", p=128)
            for blk in range(4):
                wt0, wb0 = wload([(0, FC // 2, 256, wd[:, 0:FC // 2, blk * 256:(blk + 1) * 256])])
                wt1, wb1 = wload([(0, FC // 2, 256, wd[:, FC // 2:FC, blk * 256:(blk + 1) * 256])])
                wdv = [wview(wt0, 0, FC // 2, 256), wview(wt1, 0, FC // 2, 256)]
                wdb = [wb0, wb1]
                for c2 in range(2):
                    j = blk * 2 + c2
                    pt, pb = ps_next()
                    for kc in range(FC):
                        hh, kk = kc // (FC // 2), kc % (FC // 2)
                        em.op("pe", lambda e, kc=kc, hh=hh, kk=kk, c2=c2, pt=pt, wdv=wdv: e.matmul(pt[:, 0:N], lhsT=wdv[hh][:, kk, c2 * 128:(c2 + 1) * 128], rhs=HID[:, kc, 0:N], start=(kc == 0), stop=(kc == FC - 1)), reads=[wdb[hh], b_HID], writes=[pb])
                    for si, seq in enumerate(G.seqs):
                        c0, c1 = si * G.n, (si + 1) * G.n
                        em.op("dve", lambda e, j=j, c0=c0, c1=c1, seq=seq, pt=pt: e.scalar_tensor_tensor(out=X[:, j, c0:c1], in0=pt[:, c0:c1], scalar=modG[li][:, j, seq:seq + 1], in1=X[:, j, c0:c1], op0=ALU.mult, op1=ALU.add),
                              reads=[pb, b_X, b_mod[li]], writes=[b_X])

        def conf_layer(G, l, gi):
            N, S, n = G.N, G.S, G.n
            li = l * 2
            norm_mod(G, lambda kc, seq: modA[li][:, kc, seq:seq + 1], lambda kc, seq: modB[li][:, kc, seq:seq + 1], [b_mod[li]])
            UBv = UB[:, :, 0:S * (n + 30)].rearrange("p k (s t) -> p k s t", s=S)
            if gi is None:
                for s in range(S):
                    ct, cb = bigt()
                    em.dma("sp", lambda e, ct=ct, s=s: e.dma_start(out=ct[0:30, 0:D], in_=stconv[l, s]), writes=[cb])
                    for k0 in range(0, KC, 4):
                        pt, pb = ps_next()
                        for kk in range(4):
                            em.op("pe", lambda e, ct=ct, kk=kk, k0=k0, pt=pt: e.transpose(pt[:, kk * 32:kk * 32 + 30], ct[0:30, (k0 + kk) * 128:(k0 + kk + 1) * 128], ident[0:30, 0:30]), reads=[cb, b_const], writes=[pb])
                        em.op("act", lambda e, k0=k0, s=s, pt=pt: e.copy(out=UBv[:, k0:k0 + 4, s, 0:30], in_=pt[:, 0:128].rearrange("p (k t) -> p k t", k=4)[:, :, 0:30]), reads=[pb], writes=[b_UB])
            elif gi == 0:
                em.op("dve", lambda e: e.memset(UBv[:, :, 0, 0:30], 0.0), writes=[b_UB])
            else:
                em.op("act", lambda e: e.copy(out=UBv[:, :, 0, 0:30], in_=hist[l][:]), reads=[b_hist[l]], writes=[b_UB])
            w1 = w_pw1[l].rearrange("(kc p) (t c) -> p kc t c", p=128, t=2)
            for blk in range(4):
                wt, wb = wload([(0, KC, 256, w1[:, :, 0, blk * 256:(blk + 1) * 256]), (KC * 256, KC, 256, w1[:, :, 1, blk * 256:(blk + 1) * 256])])
                wav = wview(wt, 0, KC, 256)
                wgv = wview(wt, KC * 256, KC, 256)
                for c2 in range(2):
                    j = blk * 2 + c2
                    pa, pab = ps_next()
                    pg, pgb = ps_next()
                    for kc in range(KC):
                        em.op("pe", lambda e, kc=kc, c2=c2, pa=pa, wav=wav: e.matmul(pa[:, 0:N], lhsT=wav[:, kc, c2 * 128:(c2 + 1) * 128], rhs=H[:, kc, 0:N], start=(kc == 0), stop=(kc == KC - 1)), reads=[wb, b_H], writes=[pab])
                    for kc in range(KC):
                        em.op("pe", lambda e, kc=kc, c2=c2, pg=pg, wgv=wgv: e.matmul(pg[:, 0:N], lhsT=wgv[:, kc, c2 * 128:(c2 + 1) * 128], rhs=H[:, kc, 0:N], start=(kc == 0), stop=(kc == KC - 1)), reads=[wb, b_H], writes=[pgb])
                    tt, tb = tmpA()
                    em.op("act", lambda e, pg=pg, tt=tt, j=j: e.activation(out=tt[:, 0:N], in_=pg[:, 0:N], func=AF.Sigmoid, bias=vcol(R_BPW1 + l * 2 + 1, j), scale=1.0), reads=[pgb, b_vec], writes=[tb])
                    if gi is not None:
                        em.op("dve", lambda e, tt=tt: e.tensor_tensor(out=tt[:, 0:N], in0=tt[:, 0:N], in1=VALID[:, 0:N], op=ALU.mult), reads=[tb, b_VALID], writes=[tb])
                    em.op("dve", lambda e, pa=pa, tt=tt, j=j: e.scalar_tensor_tensor(out=UBv[:, j, :, 30:30 + n], in0=pa[:, 0:N].rearrange("p (s t) -> p s t", s=S), scalar=vcol(R_BPW1 + l * 2, j), in1=tt[:, 0:N].rearrange("p (s t) -> p s t", s=S), op0=ALU.add, op1=ALU.mult),
                          reads=[pab, tb, b_vec], writes=[b_UB])
            if gi is not None and gi < NG - 1:
                em.op("act", lambda e: e.copy(out=hist[l][:], in_=UBv[:, :, 0, n:n + 30]), reads=[b_UB], writes=[b_hist[l]])
            if gi is None or gi == NG - 1:
                for s in range(S):
                    ct, cb = bigt()
                    w_ = 34 if gi is None else 30
                    o_ = 0 if gi is None else n
                    for k0 in range(0, KC, 4):
                        pt, pb = ps_next()
                        for kk in range(4):
                            em.op("pe", lambda e, kk=kk, k0=k0, s=s, pt=pt, w_=w_, o_=o_: e.transpose(pt[0:w_, kk * 128:(kk + 1) * 128], UBv[:, k0 + kk, s, o_:o_ + w_], ident[:]), reads=[b_UB, b_const], writes=[pb])
                        em.op("act", lambda e, k0=k0, ct=ct, pt=pt, w_=w_: e.copy(out=ct[0:w_, k0 * 128:(k0 + 4) * 128], in_=pt[0:w_, :]), reads=[pb], writes=[cb])
                    if gi is None:
                        em.dma("sp", lambda e, ct=ct, s=s: e.dma_start(out=conv_s[l, s], in_=ct[4:34, 0:D]), reads=[cb])
                    else:
                        em.dma("sp", lambda e, ct=ct: e.dma_start(out=conv_p[l], in_=ct[0:30, 0:D]), reads=[cb])
            Yv = Y[:, :, 0:N].rearrange("p k (s t) -> p k s t", s=S)
            for k in range(31):
                for j in range(KC):
                    wk = vcol(R_WDW + l * 31 + k, j)
                    if k == 0:
                        em.op("dve", lambda e, j=j, wk=wk: e.tensor_scalar(out=Yv[:, j], in0=UBv[:, j, :, 0:n], scalar1=wk, scalar2=vcol(R_BDW + l, j), op0=ALU.mult, op1=ALU.add), reads=[b_UB, b_vec], writes=[b_Y])
                    else:
                        em.op("dve", lambda e, j=j, wk=wk, k=k: e.scalar_tensor_tensor(out=Yv[:, j], in0=UBv[:, j, :, k:k + n], scalar=wk, in1=Yv[:, j], op0=ALU.mult, op1=ALU.add), reads=[b_UB, b_vec, b_Y], writes=[b_Y])
            em.op("act", lambda e: e.copy(out=ZS[:, :, 0:N], in_=Y[:, :, 0:N]), reads=[b_Y], writes=[b_ZS])
            p1, p1b = ps_next()
            p2, p2b = ps_next()
            for kc in range(KC):
                em.op("pe", lambda e, kc=kc, p1=p1: e.matmul(p1[:, 0:N], lhsT=onesb[:], rhs=ZS[:, kc, 0:N], start=(kc == 0), stop=(kc == KC - 1)), reads=[b_ZS, b_const], writes=[p1b])
            for kc in range(KC):
                sq, sqb = sqt()
                em.op("act", lambda e, kc=kc, sq=sq: e.activation(out=sq[:, 0:N], in_=Y[:, kc, 0:N], func=AF.Square), reads=[b_Y], writes=[sqb])
                em.op("pe", lambda e, kc=kc, p2=p2, sq=sq: e.matmul(p2[:, 0:N], lhsT=onesb[:], rhs=sq[:, 0:N], start=(kc == 0), stop=(kc == KC - 1)), reads=[sqb, b_const], writes=[p2b])
            em.op("dve", lambda e, p1=p1: e.tensor_scalar(out=MEAN[:, 0:N], in0=p1[:, 0:N], scalar1=1.0 / D, scalar2=None, op0=ALU.mult), reads=[p1b], writes=[b_MEAN])
            tt, tb = tmpA()
            em.op("dve", lambda e, tt=tt: e.tensor_tensor(out=tt[:, 0:N], in0=MEAN[:, 0:N], in1=MEAN[:, 0:N], op=ALU.mult), reads=[b_MEAN], writes=[tb])
            em.op("dve", lambda e, tt=tt, p2=p2: e.scalar_tensor_tensor(out=tt[:, 0:N], in0=p2[:, 0:N], scalar=1.0 / D, in1=tt[:, 0:N], op0=ALU.mult, op1=ALU.subtract), reads=[p2b, tb], writes=[tb])
            em.op("act", lambda e, tt=tt: e.activation(out=RSTD[:, 0:N], in_=tt[:, 0:N], func=AF.Sqrt, bias=epsc[:, 0:1], scale=1.0), reads=[tb, b_const], writes=[b_RSTD])
            em.op("dve", lambda e: e.reciprocal(out=RSTD[:, 0:N], in_=RSTD[:, 0:N]), reads=[b_RSTD], writes=[b_RSTD])
            for j in range(KC):
                tt, tb = tmpA()
                em.op("dve", lambda e, j=j, tt=tt: e.tensor_tensor(out=tt[:, 0:N], in0=Y[:, j, 0:N], in1=MEAN[:, 0:N], op=ALU.subtract), reads=[b_Y, b_MEAN], writes=[tb])
                em.op("dve", lambda e, tt=tt: e.tensor_tensor(out=tt[:, 0:N], in0=tt[:, 0:N], in1=RSTD[:, 0:N], op=ALU.mult), reads=[tb, b_RSTD], writes=[tb])
                em.op("act", lambda e, j=j, tt=tt: e.activation(out=ZS[:, j, 0:N], in_=tt[:, 0:N], func=AF.Silu, bias=vcol(R_LNB + l, j), scale=vcol(R_LNG + l, j)), reads=[tb, b_vec], writes=[b_ZS])
            w2 = w_pw2[l].rearrange("(kc p) c -> p kc c", p=128)
            for blk in range(4):
                wt, wb = wload([(0, KC, 256, w2[:, :, blk * 256:(blk + 1) * 256])])
                wv = wview(wt, 0, KC, 256)
                for c2 in range(2):
                    j = blk * 2 + c2
                    pt, pb = ps_next()
                    for kc in range(KC):
                        em.op("pe", lambda e, kc=kc, c2=c2, pt=pt, wv=wv: e.matmul(pt[:, 0:N], lhsT=wv[:, kc, c2 * 128:(c2 + 1) * 128], rhs=ZS[:, kc, 0:N], start=(kc == 0), stop=(kc == KC - 1)), reads=[wb, b_ZS], writes=[pb])
                    for si, seq in enumerate(G.seqs):
                        c0, c1 = si * n, (si + 1) * n
                        tt, tb = tmpA()
                        em.op("dve", lambda e, j=j, c0=c0, c1=c1, seq=seq, pt=pt, tt=tt: e.tensor_scalar(out=tt[:, c0:c1], in0=pt[:, c0:c1], scalar1=vcol(R_BPW2 + l, j), scalar2=modG[li][:, j, seq:seq + 1], op0=ALU.add, op1=ALU.mult), reads=[pb, b_vec, b_mod[li]], writes=[tb])
                        em.op("dve", lambda e, j=j, c0=c0, c1=c1, tt=tt: e.tensor_tensor(out=X[:, j, c0:c1], in0=X[:, j, c0:c1], in1=tt[:, c0:c1], op=ALU.add), reads=[tb, b_X], writes=[b_X])

        def kv_proj(G, out_rows, gi):
            N = G.N
            norm_mod(G, lambda kc, seq: vcol(R_GKV, kc), None, [b_vec])
            wk = w_kv.rearrange("(kc p) c -> p kc c", p=128)
            wts = []
            for cb in range(3):
                wt, wb = wload([(0, KC, 512, wk[:, :, cb * 512:(cb + 1) * 512])])
                wts.append((wview(wt, 0, KC, 512), wb))
            for t0 in range(0, N, 128):
                tn = min(128, N - t0)
                rt, rb = bigt()
                for cb in range(3):
                    wv, wb = wts[cb]
                    pt, pb = ps_next()
                    for kc in range(KC):
                        em.op("pe", lambda e, kc=kc, t0=t0, tn=tn, pt=pt, wv=wv: e.matmul(pt[0:tn, :], lhsT=H[:, kc, t0:t0 + tn], rhs=wv[:, kc, :], start=(kc == 0), stop=(kc == KC - 1)), reads=[wb, b_H], writes=[pb])
                    em.op("act", lambda e, cb=cb, tn=tn, pt=pt, rt=rt: e.copy(out=rt[0:tn, cb * 512:(cb + 1) * 512], in_=pt[0:tn, :]), reads=[pb], writes=[rb])
                if out_rows is not None:
                    em.dma("sp", lambda e, t0=t0, tn=tn, rt=rt: e.dma_start(out=out_rows[t0:t0 + tn, :], in_=rt[0:tn, :]), reads=[rb])
                if gi is None:
                    continue
                T = gi * 4 + t0 // 128
                tl = t0 // 128
                em.op("act", lambda e, rt=rt, T=T: e.copy(out=SV[:, T, :], in_=rt[:, 768:1024]), reads=[rb], writes=[b_SV])
                if T >= 12:
                    em.op("act", lambda e, rt=rt, T=T: e.copy(out=WV[:, T - 12, :], in_=rt[:, 1280:1536]), reads=[rb], writes=[b_WV])
                for which, cbase, p0 in ((0, 512, 0), (1, 1024, 64)):
                    pt, pb = ps_next()
                    for g in range(4):
                        em.op("pe", lambda e, g=g, rt=rt, pt=pt, cbase=cbase: e.transpose(pt[0:64, g * 128:(g + 1) * 128], rt[:, cbase + g * 64:cbase + (g + 1) * 64], ident[:]), reads=[rb, b_const], writes=[pb])
                    em.op("dve", lambda e, pt=pt, p0=p0, T=T: e.tensor_copy(out=KT[p0:p0 + 64, :, T * 128:(T + 1) * 128], in_=pt[0:64, :].rearrange("p (g t) -> p g t", g=4)), reads=[pb], writes=[b_KT])
                pt, pb = ps_next()
                for e_ in range(2):
                    for gp in range(2):
                        ix = e_ * 2 + gp
                        em.op("pe", lambda e, ix=ix, e_=e_, gp=gp, rt=rt, pt=pt: e.transpose(pt[:, ix * 128:(ix + 1) * 128], rt[:, e_ * 256 + gp * 128:e_ * 256 + (gp + 1) * 128], ident[:]), reads=[rb, b_const], writes=[pb])
                for e_ in range(2):
                    for gp in range(2):
                        ix = e_ * 2 + gp
                        pe_b = VEC2[:, e_ * 32:(e_ + 1) * 32].unsqueeze(1).broadcast_to([128, 4, 32])
                        em.op("dve", lambda e, ix=ix, e_=e_, gp=gp, pt=pt, tl=tl, pe_b=pe_b: e.tensor_tensor(out=CT[:, e_, gp, tl * 128:(tl + 1) * 128].rearrange("p (n s) -> p n s", s=32), in0=pt[:, ix * 128:(ix + 1) * 128].rearrange("p (n s) -> p n s", s=32), in1=pe_b, op=ALU.add),
                              reads=[pb, b_vec], writes=[b_CT])
            if gi is None:
                return
            for e_ in range(2):
                wi = w_rr[0]
                w_rr[0] = (wi + 1) % NW
                src = w_phi1[e_].rearrange("(s d) h -> d s h", d=64)
                for half in range(2):
                    em.dma("pool", lambda e, wi=wi, half=half, src=src: e.dma_start(out=wsl[wi][half * 64:(half + 1) * 64, 0:4096].rearrange("p (s h) -> p s h", s=32), in_=src), writes=[wsb[wi]])
                w1v = wsl[wi][:, 0:4096].rearrange("p (s h) -> p s h", s=32)
                w1b = wsb[wi]
                for g in range(4):
                    gp, base = g // 2, (g % 2) * 64
                    pt, pb = ps_next()
                    ctv = CT[base:base + 64, e_, gp, :].rearrange("p (n s) -> p n s", s=32)
                    for s_ in range(32):
                        em.op("pe", lambda e, s_=s_, base=base, pt=pt, ctv=ctv, w1v=w1v: e.matmul(pt[:, 0:16], lhsT=w1v[base:base + 64, s_, :], rhs=ctv[:, :, s_], start=(s_ == 0), stop=(s_ == 31)), reads=[b_CT, w1b], writes=[pb])
                    em.op("act", lambda e, e_=e_, g=g, pt=pt: e.activation(out=HIDC[:, e_, g, gi * 16:(gi + 1) * 16], in_=pt[:, 0:16], func=AF.Silu, bias=VEC2[:, 64 + e_:65 + e_], scale=1.0), reads=[pb, b_vec], writes=[b_HIDC])

        GP = Grp(1, GN, [0])
        GS = Grp(NS, ST, [1, 2, 3, 4])

        def run_group(G, gi, src_rows, out_rows):
            load_xT(G, src_rows)
            if gi is not None:
                em.dma("sp", lambda e: e.dma_start(out=VALID[:], in_=validt[:, gi * GN:(gi + 1) * GN].broadcast_to([128, GN])), writes=[b_VALID])
            for l in range(2):
                conf_layer(G, l, gi)
                ffn(G, l)
            kv_proj(G, out_rows, gi)
            if gi is not None and gi >= NG // 2:
                q = gi - NG // 2
                em.dma("sp", lambda e, q=q: e.dma_start(out=XS[:, :, q * GN:(q + 1) * GN], in_=X[:]), reads=[b_X], writes=[b_XS])

        if stage >= 1:
            for gi in range(NG):
                orow = rows_p[(gi - NG // 2) * GN:(gi - NG // 2 + 1) * GN, :] if gi >= NG // 2 else None
                run_group(GP, gi, xp[gi * GN:(gi + 1) * GN, :], orow)
            pt, pb = ps_next()
            em.op("pe", lambda e, pt=pt: e.matmul(pt[0:64, :], lhsT=W2[:, 0, :], rhs=HIDC[:, 0, :, :].rearrange("p g n -> p (g n)"), start=True, stop=True), reads=[b_HIDC, b_W1T], writes=[pb])
            em.op("act", lambda e, pt=pt: e.activation(out=CKT[:].rearrange("p g n -> p (g n)"), in_=pt[0:64, :], func=AF.Identity, bias=VEC2[0:64, 66:67], scale=1.0), reads=[pb, b_vec], writes=[b_CK])
            pt, pb = ps_next()
            for g in range(4):
                em.op("pe", lambda e, g=g, pt=pt: e.matmul(pt[:, g * 64:(g + 1) * 64], lhsT=HIDC[:, 1, g, :], rhs=W2[:, 1, :], start=True, stop=True), reads=[b_HIDC, b_W1T], writes=[pb])
            em.op("dve", lambda e, pt=pt: e.tensor_tensor(out=CV[:], in0=pt[:, 0:256].rearrange("p (g d) -> p g d", g=4), in1=B2V[:].unsqueeze(1).broadcast_to([128, 4, 64]), op=ALU.add), reads=[pb, b_W1T], writes=[b_CK])
            dbg("CKT", CKT[:], [b_CK])
            dbg("CV", CV[:], [b_CK])
            dbg("KT0", KT[:, 0, 0:512], [b_KT])
            run_group(GS, None, xs, rows_s)
            em.op("act", lambda e: e.copy(out=XSS[:], in_=X[:, :, 0:NS * ST]), reads=[b_X], writes=[b_XSS])
        em.barrier()
        stA.close()

        ps_nrot[0] = 6
        accs = [(pst[6], psb[6]), (pst[7], psb[7])]
        acc_rr = [0]

        def acc_next():
            i = acc_rr[0]
            acc_rr[0] = 1 - i
            return accs[i]

        wall_tiles = [BIG[:, 5632:5632 + 1664], sb("wall1", [128, 1664])]
        wall_bufs = [em.buf("wall0"), em.buf("wall1")]
        wall_rr = [0]
        wall_first = [True]

        def wall():
            i = wall_rr[0]
            wall_rr[0] = 1 - i
            return wall_tiles[i], wall_bufs[i]
        Pt = rot("Pt", [128, 512], BF16, 4)
        PT = rot("PTt", [128, 512], BF16, 3)
        RSr = rot("RSr", [128, 8], F32, 4)
        sm = rot("sm", [128, 4], F32, 8)
        GATE = sb("GATE", [128, 4, 48])
        IMP = sb("IMP", [128, 256])
        IMPB = sb("IMPB", [128, 128])
        SCORE = sb("SCORE", [128, 128])
        SC2 = sb("SC2", [128, 128])
        MASK = sb("MASK", [128, 128])
        T8 = sb("T8", [128, 16])
        b_GATE, b_IMP, b_SCORE, b_MASK = [em.buf(n) for n in ("GATE", "IMP", "SCORE", "MASK")]
        evac_rr = [0]
        dbg_once = []

        def evac(out, in_, reads, writes):
            evac_rr[0] ^= 1
            if evac_rr[0]:
                em.op("act", lambda e: e.copy(out=out, in_=in_), reads=reads, writes=writes)
            else:
                em.op("dve", lambda e: e.tensor_copy(out=out, in_=in_), reads=reads, writes=writes)

        def pipeline(units, lagB=1, lagC=2):
            n = len(units)
            for t in range(n + lagC):
                if t < n:
                    units[t][0]()
                if 0 <= t - lagB < n:
                    units[t - lagB][1]()
                if 0 <= t - lagC < n:
                    units[t - lagC][2]()

        def transposeP(u, wu):
            nq = u["nq"]
            tp, tpb = ps_next()
            tpv = tp[:].bitcast(BF16)
            p_, p_b = u["p_"], u["p_b"]
            off = 0
            j = 0
            while off < wu:
                w1 = min(128, wu - off)
                em.op("pe", lambda e, j=j, off=off, w1=w1, p_=p_, tpv=tpv: e.transpose(tpv[0:w1, j * 128:j * 128 + nq], p_[0:nq, off:off + w1], identb[0:nq, 0:nq]), reads=[p_b, b_const], writes=[tpb])
                off += w1
                j += 1
            pT, pTb = PT()
            if wu >= 128:
                evac(pT[:, 0:j * 128], tpv[:, 0:j * 128], [tpb], [pTb])
            else:
                evac(pT[0:wu, 0:128], tpv[0:wu, 0:128], [tpb], [pTb])
            u["pT"], u["pTb"] = pT, pTb

        def attn_core(nq, qb_cols, gate_ap, otm, otm_b, g, heads, q0, first_blk, cmpK, cmpV, b_cmp, n_lo, n_hi, vbc_ap, selK, selV_of, winK, winV_of, sel_units, win_units, need_vb, jb_cfg, QTv, b_QTv):
            units = []
            for r, h in enumerate(heads):
                u = {"nq": nq}

                def A(u=u, r=r, h=h):
                    ps, pb = ps_next()
                    em.op("pe", lambda e: e.matmul(ps[0:nq, 0:n_hi], lhsT=QTv[0:64, h, qb_cols], rhs=cmpK[0:64, g, 0:n_hi], start=True, stop=False), reads=[b_QTv, b_cmp], writes=[pb])
                    em.op("pe", lambda e: e.matmul(ps[0:nq, 0:n_hi], lhsT=onesb[0:1, 0:nq], rhs=vbc_ap[0:1, 0:n_hi], start=False, stop=True), reads=[b_const, b_bias], writes=[pb])
                    sc, scb = tmpA()
                    em.op("dve", lambda e: e.tensor_scalar(out=sc[0:nq, 0:n_lo], in0=ps[0:nq, 0:n_lo], scalar1=CH[0:nq, h:h + 1], scalar2=None, op0=ALU.add), reads=[pb, b_bias], writes=[scb])
                    em.op("dve", lambda e: e.tensor_tensor(out=sc[0:nq, n_lo:n_hi], in0=ps[0:nq, n_lo:n_hi], in1=BCALL[0:nq, h, 0:n_hi - n_lo], op=ALU.add), reads=[pb, b_bias], writes=[scb])
                    s4, s4b = sm()
                    em.op("act", lambda e: e.activation(out=sc[0:nq, 0:n_hi], in_=sc[0:nq, 0:n_hi], func=AF.Exp, bias=NEGM[0:nq, h:h + 1], scale=1.0), reads=[scb, b_bias], writes=[scb])
                    em.op("dve", lambda e: e.tensor_reduce(out=s4[0:nq, 0:1], in_=sc[0:nq, 0:n_hi], axis=AX.X, op=ALU.add), reads=[scb], writes=[s4b])
                    em.op("dve", lambda e: e.tensor_scalar(out=s4[0:nq, 1:2], in0=s4[0:nq, 0:1], scalar1=1e-30, scalar2=None, op0=ALU.max), reads=[s4b], writes=[s4b])
                    em.op("dve", lambda e: e.reciprocal(out=s4[0:nq, 2:3], in_=s4[0:nq, 1:2]), reads=[s4b], writes=[s4b])
                    if r == 0:
                        em.op("dve", lambda e: e.memset(IMP[:], 0.0), writes=[b_IMP])
                        em.op("dve", lambda e: e.tensor_scalar(out=IMP[0:nq, 0:n_hi], in0=sc[0:nq, 0:n_hi], scalar1=s4[0:nq, 2:3], scalar2=None, op0=ALU.mult), reads=[scb, s4b], writes=[b_IMP])
                    else:
                        em.op("dve", lambda e: e.scalar_tensor_tensor(out=IMP[0:nq, 0:n_hi], in0=sc[0:nq, 0:n_hi], scalar=s4[0:nq, 2:3], in1=IMP[0:nq, 0:n_hi], op0=ALU.mult, op1=ALU.add), reads=[scb, s4b, b_IMP], writes=[b_IMP])
                    em.op("dve", lambda e: e.tensor_tensor(out=s4[0:nq, 3:4], in0=s4[0:nq, 2:3], in1=gate_ap(h, 0), op=ALU.mult), reads=[s4b, b_GATE], writes=[s4b])
                    p_, p_b = Pt()
                    em.op("act", lambda e: e.copy(out=p_[0:nq, 0:n_hi], in_=sc[0:nq, 0:n_hi]), reads=[scb], writes=[p_b])
                    u.update(p_=p_, p_b=p_b, s4=s4, s4b=s4b)

                def B(u=u):
                    transposeP(u, n_hi)

                def C(u=u, h=h):
                    po, pob = ps_next()
                    pT, pTb = u["pT"], u["pTb"]
                    nt = (n_hi + 127) // 128
                    for j in range(nt):
                        w1 = min(128, n_hi - j * 128)
                        em.op("pe", lambda e, j=j, w1=w1: e.matmul(po[0:nq, 0:64], lhsT=pT[0:w1, j * 128:j * 128 + nq], rhs=cmpV(j, w1), start=(j == 0), stop=(j == nt - 1)), reads=[pTb, b_cmp], writes=[pob])
                    s4, s4b = u["s4"], u["s4b"]
                    em.op("dve", lambda e: e.tensor_scalar(out=otm[0:nq, h * 64:(h + 1) * 64], in0=po[0:nq, 0:64], scalar1=s4[0:nq, 3:4], scalar2=None, op0=ALU.mult), reads=[pob, s4b], writes=[otm_b])
                units.append((A, B, C))
            pipeline(units)
            jb_cfg()
            units = []
            for r, h in enumerate(heads):
                hs = {}
                for br in (1, 2):
                    ul = sel_units if br == 1 else win_units
                    ntile = sum((w_ + 127) // 128 for (_, w_, _) in ul)
                    bs = {"tcount": 0, "ntile": ntile}
                    for ui, (k0, wu, kind) in enumerate(ul):
                        u = {"nq": nq}

                        def A(u=u, h=h, br=br, ui=ui, k0=k0, wu=wu, kind=kind, hs=hs, bs=bs):
                            if "wl" not in hs:
                                wl, wlb = wall()
                                extra = [b_BIG] if wall_first[0] else []
                                wall_first[0] = False
                                em.dma("sp", lambda e: e.dma_start(out=wl[0:nq, :], in_=WS[h][0:nq, :]), reads=[b_ws] + extra, writes=[wlb])
                                hs["wl"], hs["wlb"] = wl, wlb
                            wl, wlb = hs["wl"], hs["wlb"]
                            ps, pb = ps_next()
                            p_, p_b = Pt()
                            if br == 1:
                                kap, kbuf = selK(k0, wu)
                                em.op("pe", lambda e: e.matmul(ps[0:nq, 0:wu], lhsT=QTv[0:64, h, qb_cols], rhs=kap, start=True, stop=True), reads=[b_QTv, kbuf], writes=[pb])
                                tt = None
                                if kind[0] == "near":
                                    cs, c0 = kind[1], kind[2]
                                    tt, tb = tmpA()
                                    if cs > 0:
                                        em.op("dve", lambda e: e.tensor_scalar(out=tt[0:nq, 0:cs], in0=ps[0:nq, 0:cs], scalar1=CH[0:nq, h:h + 1], scalar2=None, op0=ALU.add), reads=[pb, b_bias], writes=[tb])
                                    em.op("dve", lambda e: e.tensor_tensor(out=tt[0:nq, cs:wu], in0=ps[0:nq, cs:wu], in1=wl[0:nq, c0:c0 + wu - cs], op=ALU.add), reads=[pb, wlb], writes=[tb])
                                    pf, pfb = Pt()
                                    em.op("act", lambda e: e.activation(out=pf[0:nq, 0:wu], in_=tt[0:nq, 0:wu], func=AF.Exp, bias=NEGM[0:nq, h:h + 1], scale=1.0), reads=[tb, b_bias], writes=[pfb])
                                else:
                                    pf, pfb = Pt()
                                    em.op("act", lambda e: e.activation(out=pf[0:nq, 0:wu], in_=ps[0:nq, 0:wu], func=AF.Exp, bias=NEGMC[0:nq, h:h + 1], scale=1.0), reads=[pb, b_bias], writes=[pfb])
                                if kind[-1] == "nomask":
                                    p_, p_b = pf, pfb
                                else:
                                    nb = wu // 64
                                    mcol = kind[-1]
                                    mk = MASK[0:nq, mcol:mcol + nb].unsqueeze(2).broadcast_to([nq, nb, 64])
                                    em.op("dve", lambda e: e.scalar_tensor_tensor(out=p_[0:nq, 0:wu].rearrange("p (b k) -> p b k", k=64), in0=pf[0:nq, 0:wu].rearrange("p (b k) -> p b k", k=64), scalar=1.0, in1=mk, op0=ALU.mult, op1=ALU.mult),
                                          reads=[pfb, b_MASK], writes=[p_b])
                            else:
                                kap, kbuf = winK(k0, wu)
                                vb = need_vb(k0, wu)
                                em.op("pe", lambda e: e.matmul(ps[0:nq, 0:wu], lhsT=QTv[64:128, h, qb_cols], rhs=kap, start=True, stop=(vb is None)), reads=[b_QTv, kbuf], writes=[pb])
                                if vb is not None:
                                    em.op("pe", lambda e: e.matmul(ps[0:nq, 0:wu], lhsT=onesb[0:1, 0:nq], rhs=vb, start=False, stop=True), reads=[b_const, b_bias], writes=[pb])
                                c0 = kind[1]
                                tt, tb = tmpA()
                                em.op("dve", lambda e: e.tensor_tensor(out=tt[0:nq, 0:wu], in0=ps[0:nq, 0:wu], in1=wl[0:nq, c0:c0 + wu], op=ALU.add), reads=[pb, wlb], writes=[tb])
                                em.op("act", lambda e: e.activation(out=p_[0:nq, 0:wu], in_=tt[0:nq, 0:wu], func=AF.Exp, bias=NEGM[0:nq, h:h + 1], scale=1.0), reads=[tb, b_bias], writes=[p_b])
                            u.update(p_=p_, p_b=p_b)

                        def B(u=u, wu=wu):
                            transposeP(u, wu)

                        def C(u=u, h=h, br=br, ui=ui, k0=k0, wu=wu, bs=bs, nul=len(ul)):
                            if ui == 0:
                                bs["po"], bs["pob"] = acc_next()
                            po, pob = bs["po"], bs["pob"]
                            pT, pTb = u["pT"], u["pTb"]
                            nt = (wu + 127) // 128
                            for j in range(nt):
                                w1 = min(128, wu - j * 128)
                                vap, vbuf = (selV_of if br == 1 else winV_of)(k0 + j * 128, w1)
                                first = (bs["tcount"] == 0)
                                last = (bs["tcount"] == bs["ntile"] - 1)
                                em.op("pe", lambda e, j=j, w1=w1, vap=vap, first=first: e.matmul(po[0:nq, 0:64], lhsT=pT[0:w1, j * 128:j * 128 + nq], rhs=vap, start=first, stop=False), reads=[pTb, vbuf], writes=[pob])
                                em.op("pe", lambda e, j=j, w1=w1, last=last: e.matmul(po[0:nq, 64:65], lhsT=pT[0:w1, j * 128:j * 128 + nq], rhs=onesb[0:w1, 0:1], start=False, stop=last), reads=[pTb, b_const], writes=[pob])
                                bs["tcount"] += 1
                            if ui == nul - 1:
                                s4, s4b = sm()
                                em.op("dve", lambda e: e.reciprocal(out=s4[0:nq, 2:3], in_=po[0:nq, 64:65]), reads=[pob], writes=[s4b])
                                em.op("dve", lambda e: e.tensor_tensor(out=s4[0:nq, 3:4], in0=s4[0:nq, 2:3], in1=gate_ap(h, br), op=ALU.mult), reads=[s4b, b_GATE], writes=[s4b])
                                em.op("dve", lambda e: e.scalar_tensor_tensor(out=otm[0:nq, h * 64:(h + 1) * 64], in0=po[0:nq, 0:64], scalar=s4[0:nq, 3:4], in1=otm[0:nq, h * 64:(h + 1) * 64], op0=ALU.mult, op1=ALU.add), reads=[pob, s4b, otm_b], writes=[otm_b])
                        units.append((A, B, C))
            pipeline(units)

        def select_blocks(nq, memsets, kth, ncol=64, use_data=True):
            R_ = slice(0, nq)
            C_ = slice(0, ncol)
            em.op("dve", lambda e: e.tensor_tensor(out=SCORE[R_, C_], in0=IMP[R_, 0:2 * ncol:2], in1=IMP[R_, 1:2 * ncol:2], op=ALU.add), reads=[b_IMP], writes=[b_SCORE])
            for (r0, r1, c0, c1, val) in memsets:
                r1 = min(r1, nq)
                if r0 >= r1:
                    continue
                em.op("dve", lambda e, r0=r0, r1=r1, c0=c0, c1=c1, val=val: e.memset(SCORE[r0:r1, c0:c1], val), reads=[b_SCORE], writes=[b_SCORE])
            if use_data:
                em.op("dve", lambda e: e.tensor_tensor(out=SCORE[R_, C_], in0=SCORE[R_, C_], in1=F0[R_, C_], op=ALU.max), reads=[b_SCORE, b_bias], writes=[b_SCORE])
                em.op("dve", lambda e: e.tensor_tensor(out=SCORE[R_, C_], in0=SCORE[R_, C_], in1=VLIM[R_, C_], op=ALU.min), reads=[b_SCORE, b_bias], writes=[b_SCORE])
            em.op("dve", lambda e: e.max(out=T8[R_, 0:8], in_=SCORE[R_, C_]), reads=[b_SCORE], writes=[b_SCORE])
            em.op("dve", lambda e: e.match_replace(out=SC2[R_, C_], in_to_replace=T8[R_, 0:8], in_values=SCORE[R_, C_], imm_value=-1e30), reads=[b_SCORE], writes=[b_SCORE])
            em.op("dve", lambda e: e.max(out=T8[R_, 8:16], in_=SC2[R_, C_]), reads=[b_SCORE], writes=[b_SCORE])
            em.op("dve", lambda e: e.tensor_scalar(out=SC2[R_, C_], in0=SCORE[R_, C_], scalar1=T8[R_, kth - 1:kth], scalar2=None, op0=ALU.is_ge), reads=[b_SCORE], writes=[b_SCORE])
            em.op("dve", lambda e: e.scalar_tensor_tensor(out=MASK[R_, C_], in0=SCORE[R_, C_], scalar=0.0, in1=SC2[R_, C_], op0=ALU.is_ge, op1=ALU.mult), reads=[b_SCORE], writes=[b_MASK])

        def qg_proj(G, lb, gate_rows):
            N = G.N
            wq = w_qg[lb].rearrange("(kc p) c -> p kc c", p=128)
            for hb in range(4):
                wt, wb = wload([(0, KC, 256, wq[:, :, hb * 256:(hb + 1) * 256])])
                wv = wview(wt, 0, KC, 256)
                for hp in range(2):
                    pt, pb = ps_next()
                    for kc in range(KC):
                        em.op("pe", lambda e, kc=kc, hp=hp, pt=pt, wv=wv: e.matmul(pt[:, 0:N], lhsT=wv[:, kc, hp * 128:(hp + 1) * 128], rhs=H[:, kc, 0:N], start=(kc == 0), stop=(kc == KC - 1)), reads=[wb, b_H], writes=[pb])
                    h0 = hb * 4 + hp * 2
                    for (src0, hh) in ((0, h0), (64, h0 + 1)):
                        for dst0 in (0, 64):
                            if (src0 + dst0) % 128 == 0 and False:
                                pass
                            em.op("act", lambda e, src0=src0, dst0=dst0, hh=hh, pt=pt: e.activation(out=QT[dst0:dst0 + 64, hh, 0:N], in_=pt[src0:src0 + 64, 0:N], func=AF.Copy, scale=0.125), reads=[pb], writes=[b_QT])
            wt, wb = wload([(0, KC, 48, wq[:, :, 1024:1072])])
            wgv = wview(wt, 0, KC, 48)
            for qb, (t0, tn) in enumerate(gate_rows):
                pt, pb = ps_next()
                for kc in range(KC):
                    em.op("pe", lambda e, kc=kc, t0=t0, tn=tn, pt=pt, wgv=wgv: e.matmul(pt[0:tn, 0:48], lhsT=H[:, kc, t0:t0 + tn], rhs=wgv[:, kc, :], start=(kc == 0), stop=(kc == KC - 1)), reads=[wb, b_H], writes=[pb])
                em.op("act", lambda e, qb=qb, tn=tn, pt=pt: e.activation(out=GATE[0:tn, qb, :], in_=pt[0:tn, 0:48], func=AF.Sigmoid), reads=[pb], writes=[b_GATE])

        def wo_proj(G, lb, li):
            N = G.N
            wo = w_o[lb].rearrange("(kc p) c -> p kc c", p=128)
            for blk in range(4):
                wt, wb = wload([(0, KC, 256, wo[:, :, blk * 256:(blk + 1) * 256])])
                wv = wview(wt, 0, KC, 256)
                for c2 in range(2):
                    j = blk * 2 + c2
                    pt, pb = ps_next()
                    for kc in range(KC):
                        em.op("pe", lambda e, kc=kc, c2=c2, pt=pt, wv=wv: e.matmul(pt[:, 0:N], lhsT=wv[:, kc, c2 * 128:(c2 + 1) * 128], rhs=ZS[:, kc, 0:N], start=(kc == 0), stop=(kc == KC - 1)), reads=[wb, b_ZS], writes=[pb])
                    for si, seq in enumerate(G.seqs):
                        c0, c1 = si * G.n, (si + 1) * G.n
                        em.op("dve", lambda e, j=j, c0=c0, c1=c1, seq=seq, pt=pt: e.scalar_tensor_tensor(out=X[:, j, c0:c1], in0=pt[:, c0:c1], scalar=modG[li][:, j, seq:seq + 1], in1=X[:, j, c0:c1], op0=ALU.mult, op1=ALU.add), reads=[pb, b_X, b_mod[li]], writes=[b_X])

        def otm_to_ZS(nq, otm, otm_b, cols):
            for k0 in range(0, KC, 4):
                pt, pb = ps_next()
                for kk in range(4):
                    em.op("pe", lambda e, kk=kk, k0=k0, pt=pt: e.transpose(pt[:, kk * 128:kk * 128 + nq], otm[0:nq, (k0 + kk) * 128:(k0 + kk + 1) * 128], ident[0:nq, 0:nq]), reads=[otm_b, b_const], writes=[pb])
                evac(ZS[:, k0:k0 + 4, cols], pt[:].rearrange("p (k t) -> p k t", k=4)[:, :, 0:nq], [pb], [b_ZS])

        def attn_prompt(gq, lb):
            li = (2 + lb) * 2
            norm_mod(GP, lambda kc, seq: modA[li][:, kc, seq:seq + 1], lambda kc, seq: modB[li][:, kc, seq:seq + 1], [b_mod[li]])
            qg_proj(GP, lb, [(qb * 128, 128) for qb in range(4)])
            for qb in range(4):
                i = 4 * gq + qb
                q0 = QH0 + 128 * i
                qc = slice(qb * 128, (qb + 1) * 128)
                otm, otm_b = bigt()
                jb = q0 // 64
                n_hi = q0 // 32 + 4
                n_lo = q0 // 32 - 28
                sel_units = []
                k0 = 0
                while k0 < q0 + 128:
                    wu = min(512, q0 + 128 - k0)
                    if k0 + wu > q0 - 896:
                        cs = max(k0, q0 - 896) - k0
                        sel_units.append((k0, wu, ("near", cs, (k0 + cs) - (q0 - 896), k0 // 64)))
                    else:
                        sel_units.append((k0, wu, ("far", k0 // 64)))
                    k0 += 512
                win_units = [(q0 - 512, 512, ("win", 1024)), (q0, 128, ("win", 1024 + 512))]
                memsets = []
                if jb + 2 < 64:
                    memsets.append((0, 128, jb + 2, 64, -1.0))
                memsets += [(0, 64, jb + 1, jb + 2, -1.0), (64, 128, jb + 1, jb + 2, 1e4), (0, 128, jb, jb + 1, 1e4), (0, 64, jb - 1, jb, 1e4)]
                for g in range(4):
                    attn_core(
                        128, qc, lambda h, br, qb=qb: GATE[:, qb, h * 3 + br:h * 3 + br + 1], otm, otm_b, g, [4 * g + r for r in range(4)], q0, None,
                        CKT, lambda j, w1, g=g: CV[0:w1, g, :], b_CK, n_lo, n_hi, VBC,
                        lambda k0, wu, g=g: (KT[0:64, g, k0:k0 + wu], b_KT), lambda k, w1, g=g: (SV[0:w1, k // 128, g * 64:(g + 1) * 64], b_SV),
                        lambda k0, wu, g=g: (KT[64:128, g, k0:k0 + wu], b_KT), lambda k, w1, g=g: (WV[0:w1, k // 128 - 12, g * 64:(g + 1) * 64], b_WV),
                        sel_units, win_units,
                        (lambda k0, wu, i=i: VBW[0:1, k0 - 1536:k0 - 1536 + wu] if i < 4 else None),
                        lambda memsets=memsets: select_blocks(128, memsets, 16), QT, b_QT)
                if gq == 0 and lb == 0:
                    dbg("otm_q%d" % qb, otm[:, 0:D], [otm_b])
                otm_to_ZS(128, otm, otm_b, qc)
            if gq == 0 and lb == 0:
                dbg("BCALL", BCALL[:], [b_bias])
                dbg("CH", CH[:], [b_bias])
                dbg("WS", WS, [b_ws])
                dbg("IMP", IMP[:, 0:128], [b_IMP])
                dbg("SCORE", SCORE[:, 0:64], [b_SCORE])
                dbg("SC2", SC2[:, 0:64], [b_SCORE])
                dbg("T8", T8[:], [b_SCORE])
                dbg("MASK", MASK[:, 0:64], [b_MASK])
                dbg("QT", QT[:, :, 0:128], [b_QT])
                dbg("GATE", GATE[:], [b_GATE])
                dbg("ZS", ZS[:, :, 0:128], [b_ZS])
            wo_proj(GP, lb, li)

        def final_out(G, out_rows):
            N = G.N
            Yf = BIG[:, 0:KC * GN].rearrange("p (k t) -> p k t", k=KC)
            norm_mod(G, lambda kc, seq: vcol(R_FG, kc), None, [b_vec], out=Yf, out_buf=b_Y)
            for t0 in range(0, N, 128):
                tn = min(128, N - t0)
                yt, ytb = bigt()
                for k0 in range(0, KC, 4):
                    pt, pb = ps_next()
                    for kk in range(4):
                        em.op("pe", lambda e, kk=kk, k0=k0, t0=t0, tn=tn, pt=pt: e.transpose(pt[0:tn, kk * 128:(kk + 1) * 128], Yf[:, k0 + kk, t0:t0 + tn], ident[:]), reads=[b_Y, b_const], writes=[pb])
                    evac(yt[0:tn, k0 * 128:(k0 + 4) * 128], pt[0:tn, :], [pb], [ytb])
                em.dma("sp", lambda e, t0=t0, tn=tn, yt=yt: e.dma_start(out=out_rows[t0:t0 + tn, :], in_=yt[0:tn, 0:D]), reads=[ytb])

        if stage >= 3:
            ngq = 4 if stage >= 4 else 1
            for gq in range(ngq):
                em.dma("sp", lambda e, gq=gq: e.dma_start(out=X[:], in_=XS[:, :, gq * GN:(gq + 1) * GN]), reads=[b_XS], writes=[b_X])
                for lb in range(2):
                    attn_prompt(gq, lb)
                    dbg("xa%d_%d" % (gq, lb), X[:, :, 0:16], [b_X])
                    ffn(GP, 2 + lb)
                    dbg("xb%d_%d" % (gq, lb), X[:, :, 0:16], [b_X])
                final_out(GP, y_p[gq * GN:(gq + 1) * GN, :])

        if stage >= 5:
            em.barrier()
            q0s = 8192
            KTflat = KT[:].rearrange("p g k -> p (g k)")
            KTf32 = KTflat.bitcast(F32)
            PG = KTf32[:, 0:2048].rearrange("p (q c) -> p q c", q=4)
            WVs = KTflat[:, 4096:4096 + 1280].rearrange("p (t c) -> p t c", t=5)
            KN = KTflat[:, 5632:5632 + 16].rearrange("p (g k) -> p g k", g=4)
            WKN = KTflat[:, 5696:5696 + 16].rearrange("p (g k) -> p g k", g=4)
            VN = KTflat[:, 5760:5760 + 256]
            KU = KTflat[:, 8192:12288].rearrange("p (b g k) -> p b g k", b=2, g=4)
            VU = KTflat[:, 12288:14336].rearrange("p (b q c) -> p b q c", b=2, q=4)
            WKs = KTflat[:, 14336:16384].rearrange("p (g k) -> p g k", g=4)
            SVflat = SV[:].rearrange("p t c -> p (t c)")
            CKTs = SVflat[:, 0:4096].rearrange("p (s g n) -> p s g n", s=4, g=4)
            CVs = SVflat[:, 4096:6144].rearrange("p (s j g d) -> p s j g d", s=4, j=2, g=4)
            HIDCs = SVflat[:, 6144:8192].rearrange("p (e g n) -> p e g n", e=2, g=4)
            WVflat = WV[:].rearrange("p t c -> p (t c)")
            CTs = WVflat[:, 0:4096].rearrange("p (e a n) -> p e a n", e=2, a=2)
            IDXf = WVflat[:, 4096:4608].bitcast(F32)
            IDX = WVflat[:, 4608:5120].bitcast(I32)
            PIO = KTflat[:, 6016:6018].bitcast(F32)
            OACC = KTflat[0:4, 6080:6080 + 2080].bitcast(F32).rearrange("p (h d) -> p h d", h=16)
            RSs = KTflat[0:4, 14336:14336 + 640].bitcast(F32).rearrange("p (h u) -> p h u", h=16)
            MASKs = KTflat[0:4, 14336 + 640:14336 + 640 + 1024].bitcast(F32).rearrange("p (g n) -> p g n", g=4)
            ZB = KTflat[0:1, 14336 + 1664:14336 + 1664 + 256]
            b_PG, b_WVs, b_KN, b_KU, b_VU, b_WKs, b_scmp, b_HIDCs, b_CTs, b_IDX, b_MASKs, b_OACC, b_RSs = [em.buf(n) for n in
                ("PG", "WVs", "KN", "KU", "VU", "WKs", "scmp", "HIDCs", "CTs", "IDX", "MASKs", "OACC", "RSs")]
            b_KUb = [em.buf("KU0"), em.buf("KU1")]
            b_VUb = [em.buf("VU0"), em.buf("VU1")]
            em.op("dve", lambda e: e.memset(ZB[:], 0.0), writes=[b_IDX])
            em.dma("sp", lambda e: e.dma_start(out=IDX[:], in_=ptab.rearrange("s j -> (s j)").unsqueeze(0).broadcast_to([128, NS * 64])), writes=[b_IDX])
            em.op("pool", lambda e: e.iota(PIO[:], [[1, 1]], base=0, channel_multiplier=1, allow_small_or_imprecise_dtypes=True), writes=[b_IDX])
            em.op("dve", lambda e: e.tensor_copy(out=IDXf[:], in_=IDX[:]), reads=[b_IDX], writes=[b_IDX])
            em.op("dve", lambda e: e.tensor_scalar(out=IDXf[:], in0=IDXf[:], scalar1=128.0, scalar2=PIO[:, 0:1], op0=ALU.mult, op1=ALU.add), reads=[b_IDX], writes=[b_IDX])
            em.op("dve", lambda e: e.tensor_copy(out=IDX[:], in_=IDXf[:]), reads=[b_IDX], writes=[b_IDX])

            def gather4(cache, s, pg0):
                for q in range(4):
                    col = s * 64 + pg0 + q
                    em.dma("pool", lambda e, q=q, col=col: e.indirect_dma_start(out=PG[:, q, :], out_offset=None, in_=cache, in_offset=bass.IndirectOffsetOnAxis(ap=IDX[:, col:col + 1], axis=0)), reads=[b_IDX], writes=[b_PG])

            for s in range(NS):
                em.dma("sp", lambda e, s=s: e.dma_start(out=win_s[s, 0:508, :], in_=cache_win[s, 4:512, :]))
                em.dma("sp", lambda e, s=s: e.dma_start(out=win_s[s, 508:512, :], in_=rows_s[s * ST:(s + 1) * ST, 1024:1536]))

            for s in range(NS):
                for bt in range(8):
                    for sub in range(2):
                        gather4(cache_cmp, s, bt * 8 + sub * 4)
                        for q in range(4):
                            tl = sub * 4 + q
                            pt, pb = ps_next()
                            for e_ in range(2):
                                for gp in range(2):
                                    ix = e_ * 2 + gp
                                    em.op("pe", lambda e, ix=ix, e_=e_, gp=gp, q=q, pt=pt: e.transpose(pt[:, ix * 128:(ix + 1) * 128], PG[:, q, e_ * 256 + gp * 128:e_ * 256 + (gp + 1) * 128], ident[:]), reads=[b_PG, b_const], writes=[pb])
                            for e_ in range(2):
                                for gp in range(2):
                                    ix = e_ * 2 + gp
                                    pe_b = VEC2[:, e_ * 32:(e_ + 1) * 32].unsqueeze(1).broadcast_to([128, 4, 32])
                                    em.op("dve", lambda e, ix=ix, e_=e_, gp=gp, pt=pt, tl=tl, pe_b=pe_b: e.tensor_tensor(out=CTs[:, e_, gp, tl * 128:(tl + 1) * 128].rearrange("p (n s) -> p n s", s=32), in0=pt[:, ix * 128:(ix + 1) * 128].rearrange("p (n s) -> p n s", s=32), in1=pe_b, op=ALU.add),
                                          reads=[pb, b_vec], writes=[b_CTs])
                    for e_ in range(2):
                        wi = w_rr[0]
                        w_rr[0] = (wi + 1) % NW
                        src = w_phi1[e_].rearrange("(s d) h -> d s h", d=64)
                        for half in range(2):
                            em.dma("pool", lambda e, wi=wi, half=half, src=src: e.dma_start(out=wsl[wi][half * 64:(half + 1) * 64, 0:4096].rearrange("p (s h) -> p s h", s=32), in_=src), writes=[wsb[wi]])
                        w1v = wsl[wi][:, 0:4096].rearrange("p (s h) -> p s h", s=32)
                        w1b = wsb[wi]
                        for g in range(4):
                            gp, base = g // 2, (g % 2) * 64
                            pt, pb = ps_next()
                            ctv = CTs[base:base + 64, e_, gp, :].rearrange("p (n s) -> p n s", s=32)
                            for s_ in range(32):
                                em.op("pe", lambda e, s_=s_, base=base, pt=pt, ctv=ctv, w1v=w1v: e.matmul(pt[:, 0:32], lhsT=w1v[base:base + 64, s_, :], rhs=ctv[:, :, s_], start=(s_ == 0), stop=(s_ == 31)), reads=[b_CTs, w1b], writes=[pb])
                            em.op("act", lambda e, e_=e_, g=g, pt=pt, bt=bt: e.activation(out=HIDCs[:, e_, g, bt * 32:(bt + 1) * 32], in_=pt[:, 0:32], func=AF.Silu, bias=VEC2[:, 64 + e_:65 + e_], scale=1.0), reads=[pb, b_vec], writes=[b_HIDCs])
                for half in range(2):
                    pt, pb = ps_next()
                    em.op("pe", lambda e, pt=pt, half=half: e.matmul(pt[0:64, :], lhsT=W2[:, 0, :], rhs=HIDCs[:, 0, half * 2:half * 2 + 2, :].rearrange("p g n -> p (g n)"), start=True, stop=True), reads=[b_HIDCs, b_W1T], writes=[pb])
                    em.op("act", lambda e, pt=pt, half=half, s=s: e.activation(out=CKTs[0:64, s, half * 2:half * 2 + 2, :].rearrange("p g n -> p (g n)"), in_=pt[0:64, :], func=AF.Identity, bias=VEC2[0:64, 66:67], scale=1.0), reads=[pb, b_vec], writes=[b_scmp])
                for j in range(2):
                    pt, pb = ps_next()
                    for g in range(4):
                        em.op("pe", lambda e, g=g, j=j, pt=pt: e.matmul(pt[:, g * 64:(g + 1) * 64], lhsT=HIDCs[:, 1, g, j * 128:(j + 1) * 128], rhs=W2[:, 1, :], start=True, stop=True), reads=[b_HIDCs, b_W1T], writes=[pb])
                    em.op("dve", lambda e, pt=pt, j=j, s=s: e.tensor_tensor(out=CVs[:, s, j, :, :], in0=pt[:, 0:256].rearrange("p (g d) -> p g d", g=4), in1=B2V[:].unsqueeze(1).broadcast_to([128, 4, 64]), op=ALU.add), reads=[pb, b_W1T], writes=[b_scmp])

            def build_unit(cache_rows_tile_of, buf):
                for g in range(4):
                    pt, pb = ps_next()
                    for q in range(4):
                        em.op("pe", lambda e, g=g, q=q, pt=pt: e.transpose(pt[0:64, q * 128:(q + 1) * 128], PG[:, q, g * 64:(g + 1) * 64], ident[:]), reads=[b_PG, b_const], writes=[pb])
                    evac(KU[0:64, buf, g, :], pt[0:64, :], [pb], [b_KUb[buf]])
                em.op("act", lambda e: e.copy(out=VU[:, buf, :, :], in_=PG[:, :, 256:512]), reads=[b_PG], writes=[b_VUb[buf]])

            def sample_attn(lb):
                li = (2 + lb) * 2
                norm_mod(GS, lambda kc, seq: modA[li][:, kc, seq:seq + 1], lambda kc, seq: modB[li][:, kc, seq:seq + 1], [b_mod[li]])
                qg_proj(GS, lb, [(s * ST, ST) for s in range(NS)])
                for s in range(NS):
                    qc = slice(s * ST, (s + 1) * ST)
                    otm, otm_b = bigt()
                    nr, nrb = bigt()
                    em.dma("sp", lambda e, nr=nr, s=s: e.dma_start(out=nr[0:ST, :], in_=rows_s[s * ST:(s + 1) * ST, :]), writes=[nrb])
                    pt, pb = ps_next()
                    for g in range(4):
                        em.op("pe", lambda e, g=g, nr=nr, pt=pt: e.transpose(pt[0:64, g * 4:g * 4 + 4], nr[0:ST, 512 + g * 64:512 + (g + 1) * 64], ident[0:ST, 0:ST]), reads=[nrb, b_const], writes=[pb])
                        em.op("pe", lambda e, g=g, nr=nr, pt=pt: e.transpose(pt[0:64, 16 + g * 4:16 + g * 4 + 4], nr[0:ST, 1024 + g * 64:1024 + (g + 1) * 64], ident[0:ST, 0:ST]), reads=[nrb, b_const], writes=[pb])
                    em.op("dve", lambda e, pt=pt: e.tensor_copy(out=KN[0:64, :, :], in_=pt[0:64, 0:16].rearrange("p (g k) -> p g k", g=4)), reads=[pb], writes=[b_KN])
                    em.op("dve", lambda e, pt=pt: e.tensor_copy(out=WKN[64:128, :, :], in_=pt[0:64, 16:32].rearrange("p (g k) -> p g k", g=4)), reads=[pb], writes=[b_KN])
                    em.op("act", lambda e, nr=nr: e.copy(out=VN[0:ST, :], in_=nr[0:ST, 768:1024]), reads=[nrb], writes=[b_KN])
                    em.op("act", lambda e, nr=nr: e.copy(out=WVs[0:ST, 4, :], in_=nr[0:ST, 1280:1536]), reads=[nrb], writes=[b_WVs])
                    for q in range(4):
                        em.dma("sp", lambda e, q=q, s=s: e.dma_start(out=PG[:, q, :], in_=cache_win[s, q * 128:(q + 1) * 128, :]), writes=[b_PG])
                    for g in range(4):
                        pt, pb = ps_next()
                        for q in range(4):
                            em.op("pe", lambda e, g=g, q=q, pt=pt: e.transpose(pt[0:64, q * 128:(q + 1) * 128], PG[:, q, g * 64:(g + 1) * 64], ident[:]), reads=[b_PG, b_const], writes=[pb])
                        evac(WKs[64:128, g, :], pt[0:64, :], [pb], [b_WKs])
                    em.op("act", lambda e: e.copy(out=WVs[:, 0:4, :], in_=PG[:, :, 256:512]), reads=[b_PG], writes=[b_WVs])
                    win_units = [(0, 512, ("win", 1024)), (512, ST, ("win", 1024 + 512))]
                    for g in range(4):
                        def sel_cb(g=g):
                            select_blocks(ST, [(0, 128, 0, 1, 1e4), (0, 128, 127, 128, 1e4)], 15, ncol=128, use_data=False)
                            em.op("dve", lambda e: e.tensor_copy(out=MASKs[0:ST, g, :], in_=MASK[0:ST, 0:128]), reads=[b_MASK], writes=[b_MASKs])
                        attn_core(
                            ST, qc, lambda h, br, s=s: GATE[0:ST, s, h * 3 + br:h * 3 + br + 1], otm, otm_b, g, [4 * g + r for r in range(4)], q0s, None,
                            CKTs[:, s], lambda j, w1, g=g, s=s: CVs[0:w1, s, j, g, :], b_scmp, 228, 256, ZB,
                            None, None,
                            lambda k0, wu, g=g: ((WKs[64:128, g, k0:k0 + wu], b_WKs) if k0 < 512 else (WKN[64:128, g, 0:wu], b_KN)),
                            lambda k, w1, g=g: (WVs[0:w1, k // 128, g * 64:(g + 1) * 64], b_WVs),
                            [], win_units, (lambda k0, wu: None), sel_cb, QT, b_QT)
                    if s == 0 and lb == 0:
                        dbg("s_otm_cw", otm[0:ST, 0:D], [otm_b])
                        dbg("s_masks", MASKs[0:ST, :, :], [b_MASKs])
                        dbg("s_gate", GATE[0:ST, 0, :], [b_GATE])
                        dbg("s_qt", QT[0:64, :, 0:ST], [b_QT])
                    units = []
                    for u_ in range(17):
                        for h in range(16):
                            g = h // 4
                            u = {"nq": ST}

                            def A(u=u, u_=u_, h=h, g=g, s=s, qc=qc):
                                buf = u_ % 2
                                if h == 0 and u_ < 16:
                                    gather4(cache_sel, s, u_ * 4)
                                    build_unit(None, buf)
                                if u_ >= 14 and "wl" not in u:
                                    wl, wlb = wall()
                                    em.dma("sp", lambda e: e.dma_start(out=wl[0:ST, 0:1024], in_=WS[h][0:ST, 0:1024]), reads=[b_ws], writes=[wlb])
                                    u["wl"], u["wlb"] = wl, wlb
                                wu = 512 if u_ < 16 else ST
                                k0 = u_ * 512
                                ps, pb = ps_next()
                                if u_ < 16:
                                    em.op("pe", lambda e: e.matmul(ps[0:ST, 0:wu], lhsT=QT[0:64, h, qc], rhs=KU[0:64, buf, g, :], start=True, stop=True), reads=[b_QT, b_KUb[buf]], writes=[pb])
                                else:
                                    em.op("pe", lambda e: e.matmul(ps[0:ST, 0:wu], lhsT=QT[0:64, h, qc], rhs=KN[0:64, g, :], start=True, stop=True), reads=[b_QT, b_KN], writes=[pb])
                                pf, pfb = Pt()
                                if u_ >= 14:
                                    wl, wlb = u["wl"], u["wlb"]
                                    cs = max(k0, q0s - 896) - k0
                                    c0 = (k0 + cs) - (q0s - 896)
                                    tt, tb = tmpA()
                                    if cs > 0:
                                        em.op("dve", lambda e: e.tensor_scalar(out=tt[0:ST, 0:cs], in0=ps[0:ST, 0:cs], scalar1=CH[0:ST, h:h + 1], scalar2=None, op0=ALU.add), reads=[pb, b_bias], writes=[tb])
                                    em.op("dve", lambda e: e.tensor_tensor(out=tt[0:ST, cs:wu], in0=ps[0:ST, cs:wu], in1=wl[0:ST, c0:c0 + wu - cs], op=ALU.add), reads=[pb, wlb], writes=[tb])
                                    em.op("act", lambda e: e.activation(out=pf[0:ST, 0:wu], in_=tt[0:ST, 0:wu], func=AF.Exp, bias=NEGM[0:ST, h:h + 1], scale=1.0), reads=[tb, b_bias], writes=[pfb])
                                else:
                                    em.op("act", lambda e: e.activation(out=pf[0:ST, 0:wu], in_=ps[0:ST, 0:wu], func=AF.Exp, bias=NEGMC[0:ST, h:h + 1], scale=1.0), reads=[pb, b_bias], writes=[pfb])
                                p_, p_b = Pt()
                                if u_ < 16:
                                    mk = MASKs[0:ST, g, u_ * 8:u_ * 8 + 8].unsqueeze(2).broadcast_to([ST, 8, 64])
                                    em.op("dve", lambda e: e.scalar_tensor_tensor(out=p_[0:ST, 0:wu].rearrange("p (b k) -> p b k", k=64), in0=pf[0:ST, 0:wu].rearrange("p (b k) -> p b k", k=64), scalar=1.0, in1=mk, op0=ALU.mult, op1=ALU.mult),
                                          reads=[pfb, b_MASKs], writes=[p_b])
                                else:
                                    p_, p_b = pf, pfb
                                u.update(p_=p_, p_b=p_b, wu=wu)

                            def B(u=u):
                                transposeP(u, u["wu"])

                            def C(u=u, u_=u_, h=h, g=g, s=s, otm=otm, otm_b=otm_b):
                                buf = u_ % 2
                                po, pob = ps_next()
                                pT, pTb = u["pT"], u["pTb"]
                                if u_ < 16:
                                    for j in range(4):
                                        em.op("pe", lambda e, j=j: e.matmul(po[0:ST, 0:64], lhsT=pT[:, j * 128:j * 128 + ST], rhs=VU[:, buf, j, g * 64:(g + 1) * 64], start=(j == 0), stop=False), reads=[pTb, b_VUb[buf]], writes=[pob])
                                        em.op("pe", lambda e, j=j: e.matmul(po[0:ST, 64:65], lhsT=pT[:, j * 128:j * 128 + ST], rhs=onesb[:, 0:1], start=False, stop=(j == 3)), reads=[pTb, b_const], writes=[pob])
                                else:
                                    em.op("pe", lambda e: e.matmul(po[0:ST, 0:64], lhsT=pT[0:ST, 0:ST], rhs=VN[0:ST, g * 64:(g + 1) * 64], start=True, stop=False), reads=[pTb, b_KN], writes=[pob])
                                    em.op("pe", lambda e: e.matmul(po[0:ST, 64:65], lhsT=pT[0:ST, 0:ST], rhs=onesb[0:ST, 0:1], start=False, stop=True), reads=[pTb, b_const], writes=[pob])
                                if u_ == 0:
                                    em.op("dve", lambda e: e.tensor_copy(out=OACC[0:ST, h, :], in_=po[0:ST, 0:65]), reads=[pob], writes=[b_OACC])
                                else:
                                    em.op("dve", lambda e: e.tensor_tensor(out=OACC[0:ST, h, :], in0=OACC[0:ST, h, :], in1=po[0:ST, 0:65], op=ALU.add), reads=[pob, b_OACC], writes=[b_OACC])
                                if u_ == 16:
                                    s4, s4b = sm()
                                    em.op("dve", lambda e: e.reciprocal(out=s4[0:ST, 2:3], in_=OACC[0:ST, h, 64:65]), reads=[b_OACC], writes=[s4b])
                                    em.op("dve", lambda e: e.tensor_tensor(out=s4[0:ST, 3:4], in0=s4[0:ST, 2:3], in1=GATE[0:ST, s, h * 3 + 1:h * 3 + 2], op=ALU.mult), reads=[s4b, b_GATE], writes=[s4b])
                                    em.op("dve", lambda e: e.scalar_tensor_tensor(out=otm[0:ST, h * 64:(h + 1) * 64], in0=OACC[0:ST, h, 0:64], scalar=s4[0:ST, 3:4], in1=otm[0:ST, h * 64:(h + 1) * 64], op0=ALU.mult, op1=ALU.add), reads=[b_OACC, s4b, otm_b], writes=[otm_b])
                            units.append((A, B, C))
                    pipeline(units)
                    if s == 0 and lb == 0:
                        dbg("s_otm_all", otm[0:ST, 0:D], [otm_b])
                    otm_to_ZS(ST, otm, otm_b, qc)
                wo_proj(GS, lb, li)

            em.op("act", lambda e: e.copy(out=X[:, :, 0:NS * ST], in_=XSS[:]), reads=[b_XSS], writes=[b_X])
            for lb in range(2):
                sample_attn(lb)
                ffn(GS, 2 + lb)
            final_out(GS, y_s)
        em.finish()
    return nc


_CACHE = {}


def _prep_inputs(inp, c):
    b, half = c // 2, c % 2
    f = np.float32
    xp = np.zeros((TV, D), f)
    valid = np.zeros((1, TV), f)
    vbc = np.zeros((1, 128), f)
    vbw = np.zeros((1, TV), f)
    vlim = np.full((1, 64), 1e30, f)
    f0 = np.full((1, 64), -1.0, f)
    if half == 1:
        xp[:] = inp["x_prompt"][b]
        valid[:] = 1.0
        f0[0, 0] = 1e4
    else:
        xp[2048:] = inp["x_prompt"][b, :2048]
        valid[0, 2048:] = 1.0
        vbc[0, :64] = NEGV
        vbw[0, :2048] = NEGV
        vlim[0, :32] = -1.0
        f0[0, 32] = 1e4
    vec = np.zeros((NVEC, D), f)
    vec[R_BADA:R_BADA + 24] = np.asarray(inp["b_ada"]).reshape(24, D)
    vec[R_NG:R_NG + 8] = np.asarray(inp["norm_g"]).reshape(8, D)
    vec[R_BPW1:R_BPW1 + 4] = np.asarray(inp["b_pw1"]).reshape(4, D)
    vec[R_WDW:R_WDW + 62] = np.asarray(inp["w_dw"]).reshape(62, D)
    vec[R_BDW:R_BDW + 2] = inp["b_dw"]
    vec[R_LNG:R_LNG + 2] = inp["ln_g"]
    vec[R_LNB:R_LNB + 2] = inp["ln_b"]
    vec[R_BPW2:R_BPW2 + 2] = inp["b_pw2"]
    vec[R_GKV] = inp["g_kv"]
    vec[R_FG] = inp["final_g"]
    v2 = np.zeros((67, 128), f)
    for e in range(2):
        v2[e * 32:(e + 1) * 32] = np.concatenate([inp["pe_cmp"][e], inp["pe_cmp"][e]], axis=1)
    v2[64:66] = inp["b_phi1"]
    v2[66] = np.concatenate([inp["b_phi2"][0], inp["b_phi2"][1]])
    m = {
        "xp": xp, "validt": valid, "vecs": vec, "vec2": v2, "vbc": vbc, "vbw": vbw, "vlim": vlim, "f0": f0,
        "xs": np.ascontiguousarray(inp["x_sample"][NS * c:NS * (c + 1)].reshape(NS * ST, D)),
        "cvec": np.ascontiguousarray(np.concatenate([inp["c_prompt"][b:b + 1], inp["c_sample"][NS * c:NS * (c + 1)]], 0)),
        "stconv": np.ascontiguousarray(inp["state_conv"][:, NS * c:NS * (c + 1)]),
    }
    m["cache_cmp"] = np.asarray(inp["cache_kv_cmp"]).reshape(2560 * 128, 512)
    m["cache_sel"] = np.asarray(inp["cache_kv_sel"]).reshape(2560 * 128, 512)
    m["cache_win"] = np.ascontiguousarray(np.asarray(inp["cache_kv_win"])[NS * c:NS * (c + 1)].reshape(NS, 512, 512))
    m["ptab"] = np.ascontiguousarray(np.asarray(inp["page_table"])[NS * c:NS * (c + 1)]).astype(np.int32)
    for k in ("w_ada", "w_pw1", "w_pw2", "w_kv", "w_gate", "w_up", "w_down", "w_phi1", "w_phi2", "b_phi2", "rel_table", "w_qg", "w_o"):
        m[k] = np.asarray(inp[k])
    return m


def kernel(stage=9, cores=None, **inp):
    inp = {k: np.asarray(v) for k, v in inp.items()}
    if stage not in _CACHE:
        _CACHE[stage] = build(stage)
    nc = _CACHE[stage]
    if cores is not None:
        in_maps = [_prep_inputs(inp, c) for c in cores]
        res = run_bass_kernel_spmd(nc, in_maps, core_ids=list(range(len(cores))))
        kernel.raw = res.results
        return None
    in_maps = [_prep_inputs(inp, c) for c in range(8)]
    res = run_bass_kernel_spmd(nc, in_maps, core_ids=list(range(8)))
    R = res.results
    kernel.raw = R
    f = np.float32
    B, T = 4, 4096
    y_p = np.zeros((B, T, D), f)
    y_s = np.zeros((32, ST, D), f)
    rows = np.zeros((B, T, 1536), f)
    rows_s = np.zeros((32, ST, 1536), f)
    conv_p = np.zeros((2, B, 30, D), f)
    conv_s = np.zeros((2, 32, 30, D), f)
    win_s = np.zeros((32, 512, 2, 4, 64), f)
    for c in range(8):
        b, half = c // 2, c % 2
        rows[b, half * 2048:(half + 1) * 2048] = R[c]["rows_p"]
        y_p[b, half * 2048:(half + 1) * 2048] = R[c]["y_p"]
        y_s[NS * c:NS * (c + 1)] = R[c]["y_s"].reshape(NS, ST, D)
        rows_s[NS * c:NS * (c + 1)] = R[c]["rows_s"].reshape(NS, ST, 1536)
        if half == 1:
            conv_p[:, b] = R[c]["conv_p"]
        conv_s[:, NS * c:NS * (c + 1)] = R[c]["conv_s"]
        win_s[NS * c:NS * (c + 1)] = R[c]["win_s"].reshape(NS, 512, 2, 4, 64)
    rows = rows.reshape(B, T, 3, 2, 4, 64)
    rows_s = rows_s.reshape(32, ST, 3, 2, 4, 64)
    return (y_p, y_s, np.ascontiguousarray(rows[:, :, 0]), np.ascontiguousarray(rows_s[:, :, 0]),
            np.ascontiguousarray(rows[:, :, 1]), np.ascontiguousarray(rows_s[:, :, 1]),
            np.ascontiguousarray(rows[:, -512:, 2]), win_s, conv_p, conv_s)
```

```python
import numpy as np
from contextlib import ExitStack
import concourse.bass as bass
import concourse.mybir as mybir
from concourse.bass_utils import run_bass_kernel_spmd

F32 = mybir.dt.float32
BF16 = mybir.dt.bfloat16
I32 = mybir.dt.int32
U32 = mybir.dt.uint32
AF = mybir.ActivationFunctionType
ALU = mybir.AluOpType
AX = mybir.AxisListType

EPOCH = 30000
N_DMA_SEMS = 40
STRICT = False

D = 1024
KC = 8
DFF = 2816
FC = 22
TV = 4096
NG = 8
GN = 512
NS = 4
ST = 4
EPS = 1e-6
DEBUG = False
NEGV = -30000.0
QH0 = 2048


def _bucket_thresholds():
    import math
    th = []
    nb, ex, md = 32, 16, 1024
    def bucket(n):
        if n < ex:
            return n
        v = np.float32(np.log(np.float32(n) / np.float32(ex))) / np.float32(math.log(md / ex)) * np.float32(nb - ex)
        return min(ex + int(v), nb - 1)
    for b in range(1, 32):
        n = 0
        while bucket(n) < b:
            n += 1
        th.append(n)
    return th


TH = _bucket_thresholds()

R_BADA, R_NG, R_BPW1, R_WDW, R_BDW, R_LNG, R_LNB, R_BPW2, R_GKV, R_FG, NVEC = 0, 24, 32, 36, 98, 100, 102, 104, 106, 107, 108


class Buf:
    __slots__ = ("name", "w", "r")

    def __init__(self, name):
        self.name = name
        self.w = None
        self.r = {}


class Emitter:
    ENGS = ("pe", "act", "dve", "pool", "sp")

    def __init__(self, nc, stack):
        self.nc = nc
        self.stack = stack
        self.prog = {e: [] for e in self.ENGS}
        self.cnt = {e: 0 for e in self.ENGS}
        self.esems = {e: [] for e in self.ENGS}
        self.waited = {e: {} for e in self.ENGS}
        self.dma_sems = [stack.enter_context(nc.semaphore("dq%d" % i)) for i in range(N_DMA_SEMS)]
        self.dma_tot = [0] * N_DMA_SEMS
        self.dma_rr = 0
        self.nbuf = 0

    def buf(self, name=None):
        self.nbuf += 1
        return Buf(name or "b%d" % self.nbuf)

    def _esem(self, e, epoch):
        while len(self.esems[e]) <= epoch:
            self.esems[e].append(self.stack.enter_context(self.nc.semaphore("s_%s%d" % (e, len(self.esems[e])))))
        return self.esems[e][epoch]

    def _tok_sem(self, tok):
        if tok[0] == "dma":
            return self.dma_sems[tok[1]], tok[2], ("dma", tok[1])
        e, idx = tok
        return self._esem(e, idx // EPOCH), idx % EPOCH + 1, (e, idx // EPOCH)

    def _collect(self, e, reads, writes, is_dma):
        toks = set()
        same_ok = set()
        for b in reads:
            for w in (b.w or ()):
                toks.add(w)
                if e == "pe":
                    same_ok.add(w)
        for b in writes:
            for w in (b.w or ()):
                if w not in toks:
                    same_ok.add(w)
                toks.add(w)
            for t in b.r.values():
                if t not in toks:
                    same_ok.add(t)
                toks.add(t)
        waits = []
        for t in toks:
            if t[0] != "dma" and t[0] == e and not is_dma and (e == "pe" or (STRICT is False and t in same_ok)):
                continue
            sem, val, key = self._tok_sem(t)
            if self.waited[e].get(key, 0) >= val:
                continue
            self.waited[e][key] = val
            waits.append((sem, val))
        return waits

    def op(self, e, fn, reads=(), writes=()):
        waits = self._collect(e, reads, writes, False)
        idx = self.cnt[e]
        self.cnt[e] += 1
        sem = self._esem(e, idx // EPOCH)
        self.prog[e].append((waits, fn, sem, 1))
        tok = (e, idx)
        for b in reads:
            b.r[e] = tok
        for b in writes:
            b.w = (tok,)
            b.r = {}
        return tok

    def dma(self, e, fn, reads=(), writes=()):
        s = self.dma_rr
        self.dma_rr = (self.dma_rr + 1) % N_DMA_SEMS
        waits = self._collect(e, reads, writes, True)
        prev = self.dma_tot[s]
        if prev > 0 and self.waited[e].get(("dma", s), 0) < prev:
            self.waited[e][("dma", s)] = prev
            waits.append((self.dma_sems[s], prev))
        self.dma_tot[s] = prev + 16
        tok = ("dma", s, prev + 16)
        self.prog[e].append((waits, fn, self.dma_sems[s], 16))
        for b in reads:
            b.r["dma%d" % s] = tok
        for b in writes:
            if b.w and not b.r and all(t[0] == "dma" for t in b.w):
                b.w = b.w + (tok,)
            else:
                b.w = (tok,)
            b.r = {}
        return tok

    def barrier(self):
        toks = []
        for e2 in self.ENGS:
            if self.cnt[e2] > 0:
                toks.append((e2, self.cnt[e2] - 1))
        for e in self.ENGS:
            waits = []
            for t in toks:
                if t[0] == e:
                    continue
                sem, val, key = self._tok_sem(t)
                if self.waited[e].get(key, 0) < val:
                    self.waited[e][key] = val
                    waits.append((sem, val))
            for s in range(N_DMA_SEMS):
                if self.dma_tot[s] > 0 and self.waited[e].get(("dma", s), 0) < self.dma_tot[s]:
                    self.waited[e][("dma", s)] = self.dma_tot[s]
                    waits.append((self.dma_sems[s], self.dma_tot[s]))
            self.prog[e].append((waits, None, None, 0))

    def finish(self):
        waits = []
        for s in range(N_DMA_SEMS):
            if self.dma_tot[s] > 0:
                waits.append((self.dma_sems[s], self.dma_tot[s]))
        self.prog["sp"].append((waits, None, None, 0))
        engmap = {"pe": "tensor", "act": "scalar", "dve": "vector", "pool": "gpsimd", "sp": "sync"}
        with self.nc.Block() as block:
            for e in self.ENGS:
                prog = self.prog[e]

                def body(eng, prog=prog):
                    for waits, fn, sem, inc in prog:
                        for (ws, wv) in waits:
                            eng.wait_ge(ws, wv)
                        if fn is not None:
                            fn(eng).then_inc(sem, inc)

                getattr(block, engmap[e])(body)


class Grp:
    def __init__(self, S, n, seqs):
        self.S, self.n, self.N, self.seqs = S, n, S * n, seqs


def build(stage=9):
    nc = bass.Bass("TRN2", target_bir_lowering=False)

    def din(name, shape, dt=F32):
        return nc.dram_tensor(name, list(shape), dt, kind="ExternalInput").ap()

    def dout(name, shape, dt=F32):
        return nc.dram_tensor(name, list(shape), dt, kind="ExternalOutput").ap()

    xp = din("xp", [TV, D])
    xs = din("xs", [NS * ST, D])
    cvec = din("cvec", [1 + NS, D])
    vecs = din("vecs", [NVEC, D])
    vec2 = din("vec2", [67, 128])
    validt = din("validt", [1, TV])
    vbc_d = din("vbc", [1, 128])
    vbw_d = din("vbw", [1, TV])
    vlim_d = din("vlim", [1, 64])
    f0_d = din("f0", [1, 64])
    stconv = din("stconv", [2, NS, 30, D])
    w_ada = din("w_ada", [4, 2, D, 3 * D])
    w_pw1 = din("w_pw1", [2, D, 2 * D])
    w_pw2 = din("w_pw2", [2, D, D])
    w_kv = din("w_kv", [D, 1536])
    w_gate = din("w_gate", [4, D, DFF])
    w_up = din("w_up", [4, D, DFF])
    w_down = din("w_down", [4, DFF, D])
    w_phi1 = din("w_phi1", [2, 2048, 128])
    w_phi2 = din("w_phi2", [2, 128, 64])
    b_phi2 = din("b_phi2", [2, 64])
    rel_table = din("rel_table", [32, 16])
    w_qg = din("w_qg", [2, D, 1072])
    w_o = din("w_o", [2, D, D])
    cache_cmp = din("cache_cmp", [2560 * 128, 512])
    cache_sel = din("cache_sel", [2560 * 128, 512])
    cache_win = din("cache_win", [NS, 512, 512])
    ptab = din("ptab", [NS, 64], I32)

    rows_p = dout("rows_p", [2048, 1536])
    rows_s = dout("rows_s", [NS * ST, 1536])
    conv_p = dout("conv_p", [2, 30, D])
    conv_s = dout("conv_s", [2, NS, 30, D])
    y_p = dout("y_p", [2048, D])
    y_s = dout("y_s", [NS * ST, D])
    win_s = dout("win_s", [NS, 512, 512])

    XS = nc.dram_tensor("XS", [128, KC, 2048], F32, kind="Internal").ap()
    GSH = nc.dram_tensor("GSH", [16, 1920], F32, kind="Internal")
    GSL = nc.dram_tensor("GSL", [16, 1920], F32, kind="Internal")
    WS = nc.dram_tensor("WS", [16, 128, 1664], F32, kind="Internal").ap()

    with ExitStack() as st:
        em = Emitter(nc, st)

        def sb(name, shape, dt=F32):
            return st.enter_context(nc.sbuf_tensor(name, list(shape), dt))

        def dbg(name, ap, bufs):
            if not DEBUG:
                return
            o = nc.dram_tensor("dbg_" + name, list(ap.shape), ap.dtype, kind="ExternalOutput").ap()
            em.dma("sp", lambda e: e.dma_start(out=o, in_=ap), reads=bufs)

        ident = sb("ident", [128, 128])
        identb = sb("identb", [128, 128], BF16)
        antib = sb("antib", [128, 128], BF16)
        onesb = sb("onesb", [128, 128], BF16)
        epsc = sb("epsc", [128, 1])
        scr0 = sb("scr0", [128, 128])
        b_const = em.buf("const")
        b_scr0 = em.buf()
        em.op("pool", lambda e: e.iota(scr0[:], [[1, 128]], base=0, channel_multiplier=-1, allow_small_or_imprecise_dtypes=True), writes=[b_scr0])
        em.op("dve", lambda e: e.tensor_scalar(out=ident[:], in0=scr0[:], scalar1=0.0, scalar2=None, op0=ALU.is_equal), reads=[b_scr0], writes=[b_const])
        em.op("dve", lambda e: e.tensor_scalar(out=identb[:], in0=scr0[:], scalar1=0.0, scalar2=None, op0=ALU.is_equal), reads=[b_scr0], writes=[b_const])
        em.op("dve", lambda e: e.memset(onesb[:], 1.0), writes=[b_const])
        em.op("dve", lambda e: e.memset(epsc[:], EPS), writes=[b_const])
        em.op("pool", lambda e: e.iota(scr0[:], [[1, 128]], base=-127, channel_multiplier=1, allow_small_or_imprecise_dtypes=True), reads=[b_const], writes=[b_scr0])
        em.op("dve", lambda e: e.tensor_scalar(out=antib[:], in0=scr0[:], scalar1=0.0, scalar2=None, op0=ALU.is_equal), reads=[b_scr0], writes=[b_const])

        NPS = 8
        pst = [st.enter_context(nc.psum_tensor("ps%d" % i, [128, 512], F32)) for i in range(NPS)]
        psb = [em.buf("ps%d" % i) for i in range(NPS)]
        ps_rr = [0]
        ps_nrot = [NPS]

        def ps_next():
            i = ps_rr[0] % ps_nrot[0]
            ps_rr[0] = (i + 1) % ps_nrot[0]
            return pst[i], psb[i]

        def rot(name, shape, dt, n):
            tiles = [sb("%s%d" % (name, i), shape, dt) for i in range(n)]
            bufs = [em.buf("%s%d" % (name, i)) for i in range(n)]
            c = [0]

            def nxt():
                i = c[0]
                c[0] = (i + 1) % n
                return tiles[i], bufs[i]
            nxt.tiles = tiles
            nxt.bufs = bufs
            return nxt

        bigt = rot("bigt", [128, 1536], F32, 2)
        tmpA = rot("tmpA", [128, GN], F32, 4)
        sqt = rot("sqt", [128, GN], BF16, 2)

        TABf = sb("TABf", [32, 16])
        TABh = sb("TABh", [32, 16], BF16)
        TABl = sb("TABl", [32, 16], BF16)
        CH = sb("CH", [128, 16])
        BCALL = sb("BCALL", [128, 16, 32])
        VBC = sb("VBC", [1, 128], BF16)
        VBW = sb("VBW", [1, 1152], BF16)
        VLIM = sb("VLIM", [128, 64])
        F0 = sb("F0", [128, 64])
        NEGM = sb("NEGM", [128, 16])
        NEGMC = sb("NEGMC", [128, 16])
        b_tab = em.buf("tab")
        b_bias = em.buf("bias")
        em.dma("sp", lambda e: e.dma_start(out=TABf[:], in_=rel_table), writes=[b_tab])
        em.dma("sp", lambda e: e.dma_start(out=CH[:], in_=rel_table[31:32, :].broadcast_to([128, 16])), writes=[b_bias])
        em.dma("pool", lambda e: e.dma_start(out=VBC[:], in_=vbc_d), writes=[b_bias])
        em.dma("pool", lambda e: e.dma_start(out=VBW[:], in_=vbw_d[:, 1536:1536 + 1152]), writes=[b_bias])
        em.dma("sp", lambda e: e.dma_start(out=VLIM[:], in_=vlim_d.broadcast_to([128, 64])), writes=[b_bias])
        em.dma("sp", lambda e: e.dma_start(out=F0[:], in_=f0_d.broadcast_to([128, 64])), writes=[b_bias])
        em.op("dve", lambda e: e.memset(NEGM[:], 0.0), writes=[b_bias])
        em.op("dve", lambda e: e.tensor_copy(out=NEGMC[:], in_=CH[:]), reads=[b_bias], writes=[b_bias])
        em.op("dve", lambda e: e.tensor_copy(out=TABh[:], in_=TABf[:]), reads=[b_tab], writes=[b_tab])
        em.op("dve", lambda e: e.tensor_tensor(out=TABf[:], in0=TABf[:], in1=TABh[:], op=ALU.subtract), reads=[b_tab], writes=[b_tab])
        em.op("dve", lambda e: e.tensor_copy(out=TABl[:], in_=TABf[:]), reads=[b_tab], writes=[b_tab])
        GL_ = 1920
        CW = 384
        with ExitStack() as st3:
            def sb3(name, shape, dt=F32):
                return st3.enter_context(nc.sbuf_tensor(name, list(shape), dt))
            Drow = sb3("Drow", [32, CW])
            BKT = sb3("BKT", [32, CW])
            OH = sb3("OH", [32, CW], BF16)
            PIDX = sb3("PIDX", [32, 1])
            GT = sb3("GT", [16, CW])
            GTh = sb3("GTh", [16, CW], BF16)
            GTh32 = sb3("GTh32", [16, CW])
            VM = sb3("VM", [16, CW])
            NA = sb3("NA", [16, CW])
            bhk = sb3("bhk", [128, 2, 1664], BF16)
            wst = sb3("wst", [128, 1664])
            b_g = em.buf("gtab")
            b_bhk = em.buf()
            b_wst = em.buf()
            b_gs = em.buf("gs")
            em.op("pool", lambda e: e.iota(PIDX[:], [[1, 1]], base=0, channel_multiplier=1, allow_small_or_imprecise_dtypes=True), writes=[b_g])
            for ci in range(5):
                if ci < 3:
                    i0_, d0, vlo, vhi = ci * CW, 1023 - ci * CW, 0, 1023
                else:
                    i0_, d0, vlo, vhi = (ci - 3) * CW, 639 - (ci - 3) * CW, 128, 639
                col0 = ci * CW
                em.op("pool", lambda e, d0=d0: e.iota(Drow[:], [[-1, CW]], base=d0, channel_multiplier=0, allow_small_or_imprecise_dtypes=True), reads=[b_g], writes=[b_g])
                for i_, th in enumerate(TH):
                    if i_ == 0:
                        em.op("dve", lambda e, th=th: e.tensor_scalar(out=BKT[:], in0=Drow[:], scalar1=float(th), scalar2=None, op0=ALU.is_ge), reads=[b_g], writes=[b_g])
                    else:
                        em.op("dve", lambda e, th=th: e.scalar_tensor_tensor(out=BKT[:], in0=Drow[:], scalar=float(th), in1=BKT[:], op0=ALU.is_ge, op1=ALU.add), reads=[b_g], writes=[b_g])
                em.op("dve", lambda e: e.tensor_scalar(out=OH[:], in0=BKT[:], scalar1=PIDX[:, 0:1], scalar2=None, op0=ALU.is_equal), reads=[b_g], writes=[b_g])
                a_ = max(vlo, i0_) - i0_
                b_ = min(vhi + 1, i0_ + CW) - i0_
                em.op("dve", lambda e: e.memset(VM[:], 0.0), reads=[b_g], writes=[b_g])
                if b_ > a_:
                    em.op("dve", lambda e, a_=a_, b_=b_: e.memset(VM[:, a_:b_], 1.0), reads=[b_g], writes=[b_g])
                em.op("dve", lambda e: e.tensor_scalar(out=NA[:], in0=VM[:], scalar1=-NEGV, scalar2=NEGV, op0=ALU.mult, op1=ALU.add), reads=[b_g], writes=[b_g])
                pt, pb = ps_next()
                em.op("pe", lambda e, pt=pt: e.matmul(pt[0:16, 0:CW], lhsT=TABh[:], rhs=OH[:], start=True, stop=False), reads=[b_tab, b_g], writes=[pb])
                em.op("pe", lambda e, pt=pt: e.matmul(pt[0:16, 0:CW], lhsT=TABl[:], rhs=OH[:], start=False, stop=True), reads=[b_tab, b_g], writes=[pb])
                em.op("dve", lambda e, pt=pt: e.tensor_tensor(out=GT[:], in0=pt[0:16, 0:CW], in1=VM[:], op=ALU.mult), reads=[pb, b_g], writes=[b_g])
                em.op("dve", lambda e: e.tensor_tensor(out=GT[:], in0=GT[:], in1=NA[:], op=ALU.add), reads=[b_g], writes=[b_g])
                em.op("dve", lambda e: e.tensor_copy(out=GTh[:], in_=GT[:]), reads=[b_g], writes=[b_g])
                em.op("dve", lambda e: e.tensor_copy(out=GTh32[:], in_=GTh[:]), reads=[b_g], writes=[b_g])
                em.op("dve", lambda e: e.tensor_tensor(out=GT[:], in0=GT[:], in1=GTh32[:], op=ALU.subtract), reads=[b_g], writes=[b_g])
                em.dma("sp", lambda e, col0=col0: e.dma_start(out=GSH.ap()[:, col0:col0 + CW], in_=GTh32[:]), reads=[b_g], writes=[b_gs])
                em.dma("sp", lambda e, col0=col0: e.dma_start(out=GSL.ap()[:, col0:col0 + CW], in_=GT[:]), reads=[b_g], writes=[b_gs])
            b_ws = em.buf("ws")
            for h in range(16):
                for hl, GS_ in enumerate((GSH, GSL)):
                    em.dma("pool", lambda e, hl=hl, GS_=GS_, h=h: e.dma_start(out=bhk[:, hl, 0:1024], in_=bass.AP(GS_, h * GL_, [[1, 128], [1, 1024]])), reads=[b_gs], writes=[b_bhk])
                    em.dma("pool", lambda e, hl=hl, GS_=GS_, h=h: e.dma_start(out=bhk[:, hl, 1024:1664], in_=bass.AP(GS_, h * GL_ + 1152, [[1, 128], [1, 640]])), reads=[b_gs], writes=[b_bhk])
                for c0 in range(0, 1664, 512):
                    w_ = min(512, 1664 - c0)
                    pt, pb = ps_next()
                    em.op("pe", lambda e, c0=c0, w_=w_, pt=pt: e.matmul(pt[:, 0:w_], lhsT=antib[:], rhs=bhk[:, 0, c0:c0 + w_], start=True, stop=False), reads=[b_bhk, b_const], writes=[pb])
                    em.op("pe", lambda e, c0=c0, w_=w_, pt=pt: e.matmul(pt[:, 0:w_], lhsT=antib[:], rhs=bhk[:, 1, c0:c0 + w_], start=False, stop=True), reads=[b_bhk, b_const], writes=[pb])
                    em.op("act", lambda e, c0=c0, w_=w_, pt=pt: e.copy(out=wst[:, c0:c0 + w_], in_=pt[:, 0:w_]), reads=[pb], writes=[b_wst])
                em.op("dve", lambda e, h=h: e.tensor_copy(out=BCALL[:, h, :], in_=wst[:, 31:1024:32]), reads=[b_wst], writes=[b_bias])
                em.dma("sp", lambda e, h=h: e.dma_start(out=WS[h], in_=wst[:]), reads=[b_wst], writes=[b_ws])
            em.barrier()

        VEC = sb("VEC", [128, KC, NVEC])
        b_vec = em.buf("vec")
        vrows, b_vrows = bigt()
        em.dma("sp", lambda e: e.dma_start(out=vrows[0:NVEC, 0:D], in_=vecs), writes=[b_vrows])
        for kc in range(KC):
            pt, pb = ps_next()
            em.op("pe", lambda e, kc=kc, pt=pt: e.transpose(pt[:, 0:NVEC], vrows[0:NVEC, kc * 128:(kc + 1) * 128], ident[0:NVEC, 0:NVEC]), reads=[b_vrows, b_const], writes=[pb])
            em.op("dve", lambda e, kc=kc, pt=pt: e.tensor_copy(out=VEC[:, kc, :], in_=pt[:, 0:NVEC]), reads=[pb], writes=[b_vec])

        def vcol(r, kc):
            return VEC[:, kc, r:r + 1]

        VEC2 = sb("VEC2", [128, 67])
        v2rows, b_v2rows = bigt()
        em.dma("sp", lambda e: e.dma_start(out=v2rows[0:67, 0:128], in_=vec2), writes=[b_v2rows])
        pt, pb = ps_next()
        em.op("pe", lambda e, pt=pt: e.transpose(pt[:, 0:67], v2rows[0:67, 0:128], ident[0:67, 0:67]), reads=[b_v2rows, b_const], writes=[pb])
        em.op("dve", lambda e, pt=pt: e.tensor_copy(out=VEC2[:], in_=pt[:, 0:67]), reads=[pb], writes=[b_vec])

        NSEQ = 1 + NS
        crow, b_crow = bigt()
        em.dma("sp", lambda e: e.dma_start(out=crow[0:NSEQ, 0:D], in_=cvec), writes=[b_crow])
        em.op("act", lambda e: e.activation(out=crow[0:NSEQ, 0:D], in_=crow[0:NSEQ, 0:D], func=AF.Silu), reads=[b_crow], writes=[b_crow])
        scT = sb("scT", [128, KC, NSEQ], BF16)
        b_scT = em.buf()
        for kc in range(KC):
            pt, pb = ps_next()
            em.op("pe", lambda e, kc=kc, pt=pt: e.transpose(pt[:, 0:NSEQ], crow[0:NSEQ, kc * 128:(kc + 1) * 128], ident[0:NSEQ, 0:NSEQ]), reads=[b_crow, b_const], writes=[pb])
            em.op("dve", lambda e, kc=kc, pt=pt: e.tensor_copy(out=scT[:, kc, :], in_=pt[:, 0:NSEQ]), reads=[pb], writes=[b_scT])
        modA = [sb("modA%d" % i, [128, KC, NSEQ]) for i in range(8)]
        modB = [sb("modB%d" % i, [128, KC, NSEQ]) for i in range(8)]
        modG = [sb("modG%d" % i, [128, KC, NSEQ]) for i in range(8)]
        b_mod = [em.buf("mod%d" % i) for i in range(8)]

        WSLOT = 2 * KC * 256
        NW = 3
        wsl = [sb("wsl%d" % i, [128, WSLOT], BF16) for i in range(NW)]
        wsb = [em.buf("wsl%d" % i) for i in range(NW)]
        w_rr = [0]

        def wload(parts):
            i = w_rr[0]
            w_rr[0] = (i + 1) % NW
            for (off, kcn, cols, src) in parts:
                dst = wsl[i][:, off:off + kcn * cols].rearrange("p (k c) -> p k c", k=kcn)
                em.dma("pool", lambda e, dst=dst, src=src: e.dma_start(out=dst, in_=src), writes=[wsb[i]])
            return wsl[i], wsb[i]

        def wview(t, off, kcn, cols):
            return t[:, off:off + kcn * cols].rearrange("p (k c) -> p k c", k=kcn)

        n_li = 8
        blk = 0
        for li in range(n_li):
            l, i = li // 2, li % 2
            for cb in range(6):
                src = w_ada[l, i].rearrange("(kc p) c -> p kc c", p=128)[:, :, cb * 512:(cb + 1) * 512]
                wt, wb = wload([(0, KC, 512, src)])
                wv = wview(wt, 0, KC, 512)
                pt, pb = ps_next()
                for c4 in range(4):
                    for kc in range(KC):
                        em.op("pe", lambda e, c4=c4, kc=kc, pt=pt, wv=wv: e.matmul(pt[:, c4 * 8:c4 * 8 + NSEQ], lhsT=wv[:, kc, c4 * 128:(c4 + 1) * 128], rhs=scT[:, kc, :], start=(kc == 0), stop=(kc == KC - 1)),
                              reads=[wb, b_scT], writes=[pb])
                for c4 in range(4):
                    ch = cb * 4 + c4
                    which, kc = ch // 8, ch % 8
                    brow = R_BADA + li * 3 + which
                    src_ps = pt[:, c4 * 8:c4 * 8 + NSEQ]
                    if which == 0:
                        em.op("dve", lambda e, kc=kc, li=li, src_ps=src_ps, brow=brow: e.tensor_scalar(out=modB[li][:, kc, :], in0=src_ps, scalar1=vcol(brow, kc), scalar2=None, op0=ALU.add), reads=[pb, b_vec], writes=[b_mod[li]])
                    elif which == 1:
                        em.op("dve", lambda e, kc=kc, li=li, src_ps=src_ps, brow=brow: e.tensor_scalar(out=modA[li][:, kc, :], in0=src_ps, scalar1=vcol(brow, kc), scalar2=1.0, op0=ALU.add, op1=ALU.add), reads=[pb, b_vec], writes=[b_mod[li]])
                        em.op("dve", lambda e, kc=kc, li=li: e.tensor_scalar(out=modA[li][:, kc, :], in0=modA[li][:, kc, :], scalar1=vcol(R_NG + li, kc), scalar2=None, op0=ALU.mult), reads=[b_mod[li], b_vec], writes=[b_mod[li]])
                    else:
                        em.op("dve", lambda e, kc=kc, li=li, src_ps=src_ps, brow=brow: e.tensor_scalar(out=modG[li][:, kc, :], in0=src_ps, scalar1=vcol(brow, kc), scalar2=None, op0=ALU.add), reads=[pb, b_vec], writes=[b_mod[li]])

        X = sb("X", [128, KC, GN])
        H = sb("H", [128, KC, GN], BF16)
        ZS = sb("ZS", [128, KC, GN], BF16)
        BIG = sb("BIG", [128, KC * (GN + 30) + KC * GN])
        UB = BIG[:, 0:KC * (GN + 30)].rearrange("p (k t) -> p k t", k=KC)
        Y = BIG[:, KC * (GN + 30):].rearrange("p (k t) -> p k t", k=KC)
        BIGb = BIG[:].bitcast(BF16)
        HID = BIGb[:, 0:FC * GN].rearrange("p (k t) -> p k t", k=FC)
        QT = BIGb[:, 0:16 * GN].rearrange("p (k t) -> p k t", k=16)
        RSTD = sb("RSTD", [128, GN])
        XSS = sb("XSS", [128, KC, NS * ST])
        b_XSS = em.buf("XSS")
        b_XS = em.buf("XS")
        b_X, b_H, b_ZS, b_BIG, b_RSTD, b_MEAN, b_VALID = [em.buf(n) for n in ("X", "H", "ZS", "BIG", "RSTD", "MEAN", "VALID")]
        b_UB = b_Y = b_HID = b_QT = b_BIG
        b_hist = [em.buf("hist0"), em.buf("hist1")]

        KT = sb("KT", [128, 4, TV], BF16)
        SV = sb("SV", [128, 32, 256], BF16)
        WV = sb("WV", [128, 20, 256], BF16)
        CKT = sb("CKT", [64, 4, 128], BF16)
        CV = sb("CV", [128, 4, 64], BF16)
        W2 = sb("W2", [128, 2, 64], BF16)
        B2V = sb("B2V", [128, 64])
        stA = ExitStack()
        st.enter_context(stA)

        def sbA(name, shape, dt=F32):
            return stA.enter_context(nc.sbuf_tensor(name, list(shape), dt))
        VALID = sbA("VALID", [128, GN])
        MEAN = sbA("MEAN", [128, GN])
        hist = [sbA("hist%d" % l, [128, KC, 30]) for l in range(2)]
        HIDC = sbA("HIDC", [128, 2, 4, 128], BF16)
        CT = sbA("CT", [128, 2, 2, GN], BF16)
        b_KT, b_SV, b_WV, b_HIDC, b_CT, b_CK, b_W1T = [em.buf(n) for n in ("KT", "SV", "WV", "HIDC", "CT", "CK", "W1T")]
        for e_ in range(2):
            em.dma("pool", lambda e, e_=e_: e.dma_start(out=W2[:, e_, :], in_=w_phi2[e_]), writes=[b_W1T])
        em.dma("sp", lambda e: e.dma_start(out=B2V[:], in_=b_phi2[1:2, :].broadcast_to([128, 64])), writes=[b_W1T])

        def load_xT(G, src_rows):
            N = G.N
            for t0 in range(0, N, 128):
                tn = min(128, N - t0)
                xt, xb = bigt()
                em.dma("sp", lambda e, xt=xt, t0=t0, tn=tn: e.dma_start(out=xt[0:tn, 0:D], in_=src_rows[t0:t0 + tn, :]), writes=[xb])
                for k0 in range(0, KC, 4):
                    pt, pb = ps_next()
                    for kk in range(4):
                        kc = k0 + kk
                        em.op("pe", lambda e, xt=xt, kc=kc, kk=kk, tn=tn, pt=pt: e.transpose(pt[:, kk * 128:kk * 128 + tn], xt[0:tn, kc * 128:(kc + 1) * 128], ident[0:tn, 0:tn]), reads=[xb, b_const], writes=[pb])
                    em.op("act", lambda e, k0=k0, t0=t0, tn=tn, pt=pt: e.copy(out=X[:, k0:k0 + 4, t0:t0 + tn], in_=pt[:].rearrange("p (k t) -> p k t", k=4)[:, :, 0:tn]), reads=[pb], writes=[b_X])

        def rms_stats(G):
            N = G.N
            pt, pb = ps_next()
            for kc in range(KC):
                sq, sqb = sqt()
                em.op("act", lambda e, kc=kc, sq=sq: e.activation(out=sq[:, 0:N], in_=X[:, kc, 0:N], func=AF.Square), reads=[b_X], writes=[sqb])
                em.op("pe", lambda e, kc=kc, pt=pt, sq=sq: e.matmul(pt[:, 0:N], lhsT=onesb[:], rhs=sq[:, 0:N], start=(kc == 0), stop=(kc == KC - 1)), reads=[sqb, b_const], writes=[pb])
            em.op("act", lambda e, pt=pt: e.activation(out=RSTD[:, 0:N], in_=pt[:, 0:N], func=AF.Sqrt, bias=epsc[:, 0:1], scale=1.0 / D), reads=[pb, b_const], writes=[b_RSTD])
            em.op("dve", lambda e: e.reciprocal(out=RSTD[:, 0:N], in_=RSTD[:, 0:N]), reads=[b_RSTD], writes=[b_RSTD])

        def norm_mod(G, Acol, Bcol, abufs, out=None, out_buf=None):
            rms_stats(G)
            if out is None:
                out, out_buf = H, b_H
            for kc in range(KC):
                for si, seq in enumerate(G.seqs):
                    c0, c1 = si * G.n, (si + 1) * G.n
                    if Bcol is None:
                        em.op("dve", lambda e, kc=kc, c0=c0, c1=c1, seq=seq: e.scalar_tensor_tensor(out=out[:, kc, c0:c1], in0=X[:, kc, c0:c1], scalar=Acol(kc, seq), in1=RSTD[:, c0:c1], op0=ALU.mult, op1=ALU.mult),
                              reads=[b_X, b_RSTD] + abufs, writes=[out_buf])
                    else:
                        tt, tb = tmpA()
                        em.op("dve", lambda e, kc=kc, c0=c0, c1=c1, seq=seq, tt=tt: e.scalar_tensor_tensor(out=tt[:, c0:c1], in0=X[:, kc, c0:c1], scalar=Acol(kc, seq), in1=RSTD[:, c0:c1], op0=ALU.mult, op1=ALU.mult),
                              reads=[b_X, b_RSTD] + abufs, writes=[tb])
                        em.op("act", lambda e, kc=kc, c0=c0, c1=c1, seq=seq, tt=tt: e.activation(out=out[:, kc, c0:c1], in_=tt[:, c0:c1], func=AF.Identity, bias=Bcol(kc, seq), scale=1.0),
                              reads=[tb] + abufs, writes=[out_buf])

        def ffn(G, l):
            N = G.N
            li = l * 2 + 1
            norm_mod(G, lambda kc, seq: modA[li][:, kc, seq:seq + 1], lambda kc, seq: modB[li][:, kc, seq:seq + 1], [b_mod[li]])
            wg = w_gate[l].rearrange("(kc p) c -> p kc c", p=128)
            wu = w_up[l].rearrange("(kc p) c -> p kc c", p=128)
            for blk in range(FC // 2):
                wt, wb = wload([(0, KC, 256, wg[:, :, blk * 256:(blk + 1) * 256]), (KC * 256, KC, 256, wu[:, :, blk * 256:(blk + 1) * 256])])
                wgv = wview(wt, 0, KC, 256)
                wuv = wview(wt, KC * 256, KC, 256)
                for c2 in range(2):
                    j = blk * 2 + c2
                    pg, pgb = ps_next()
                    pu, pub = ps_next()
                    for kc in range(KC):
                        em.op("pe", lambda e, kc=kc, c2=c2, pg=pg, wgv=wgv: e.matmul(pg[:, 0:N], lhsT=wgv[:, kc, c2 * 128:(c2 + 1) * 128], rhs=H[:, kc, 0:N], start=(kc == 0), stop=(kc == KC - 1)), reads=[wb, b_H], writes=[pgb])
                    for kc in range(KC):
                        em.op("pe", lambda e, kc=kc, c2=c2, pu=pu, wuv=wuv: e.matmul(pu[:, 0:N], lhsT=wuv[:, kc, c2 * 128:(c2 + 1) * 128], rhs=H[:, kc, 0:N], start=(kc == 0), stop=(kc == KC - 1)), reads=[wb, b_H], writes=[pub])
                    tt, tb = tmpA()
                    em.op("act", lambda e, pg=pg, tt=tt: e.activation(out=tt[:, 0:N], in_=pg[:, 0:N], func=AF.Silu), reads=[pgb], writes=[tb])
                    em.op("dve", lambda e, pu=pu, tt=tt, j=j: e.tensor_tensor(out=HID[:, j, 0:N], in0=pu[:, 0:N], in1=tt[:, 0:N], op=ALU.mult), reads=[pub, tb], writes=[b_HID])
            wd = w_down[l].rearrange("(kc p) c -> p kc c", p=128)
            for blk in range(4):
                wt0, wb0 = wload([(0, FC // 2, 256, wd[:, 0:FC // 2, blk * 256:(blk + 1) * 256])])
                wt1, wb1 = wload([(0, FC // 2, 256, wd[:, FC // 2:FC, blk * 256:(blk + 1) * 256])])
                wdv = [wview(wt0, 0, FC // 2, 256), wview(wt1, 0, FC // 2, 256)]
                wdb = [wb0, wb1]
                for c2 in range(2):
                    j = blk * 2 + c2
                    pt, pb = ps_next()
                    for kc in range(FC):
                        hh, kk = kc // (FC // 2), kc % (FC // 2)
                        em.op("pe", lambda e, kc=kc, hh=hh, kk=kk, c2=c2, pt=pt, wdv=wdv: e.matmul(pt[:, 0:N], lhsT=wdv[hh][:, kk, c2 * 128:(c2 + 1) * 128], rhs=HID[:, kc, 0:N], start=(kc == 0), stop=(kc == FC - 1)), reads=[wdb[hh], b_HID], writes=[pb])
                    for si, seq in enumerate(G.seqs):
                        c0, c1 = si * G.n, (si + 1) * G.n
                        em.op("dve", lambda e, j=j, c0=c0, c1=c1, seq=seq, pt=pt: e.scalar_tensor_tensor(out=X[:, j, c0:c1], in0=pt[:, c0:c1], scalar=modG[li][:, j, seq:seq + 1], in1=X[:, j, c0:c1], op0=ALU.mult, op1=ALU.add),
                              reads=[pb, b_X, b_mod[li]], writes=[b_X])

        def conf_layer(G, l, gi):
            N, S, n = G.N, G.S, G.n
            li = l * 2
            norm_mod(G, lambda kc, seq: modA[li][:, kc, seq:seq + 1], lambda kc, seq: modB[li][:, kc, seq:seq + 1], [b_mod[li]])
            UBv = UB[:, :, 0:S * (n + 30)].rearrange("p k (s t) -> p k s t", s=S)
            if gi is None:
                for s in range(S):
                    ct, cb = bigt()
                    em.dma("sp", lambda e, ct=ct, s=s: e.dma_start(out=ct[0:30, 0:D], in_=stconv[l, s]), writes=[cb])
                    for k0 in range(0, KC, 4):
                        pt, pb = ps_next()
                        for kk in range(4):
                            em.op("pe", lambda e, ct=ct, kk=kk, k0=k0, pt=pt: e.transpose(pt[:, kk * 32:kk * 32 + 30], ct[0:30, (k0 + kk) * 128:(k0 + kk + 1) * 128], ident[0:30, 0:30]), reads=[cb, b_const], writes=[pb])
                        em.op("act", lambda e, k0=k0, s=s, pt=pt: e.copy(out=UBv[:, k0:k0 + 4, s, 0:30], in_=pt[:, 0:128].rearrange("p (k t) -> p k t", k=4)[:, :, 0:30]), reads=[pb], writes=[b_UB])
            elif gi == 0:
                em.op("dve", lambda e: e.memset(UBv[:, :, 0, 0:30], 0.0), writes=[b_UB])
            else:
                em.op("act", lambda e: e.copy(out=UBv[:, :, 0, 0:30], in_=hist[l][:]), reads=[b_hist[l]], writes=[b_UB])
            w1 = w_pw1[l].rearrange("(kc p) (t c) -> p kc t c", p=128, t=2)
            for blk in range(4):
                wt, wb = wload([(0, KC, 256, w1[:, :, 0, blk * 256:(blk + 1) * 256]), (KC * 256, KC, 256, w1[:, :, 1, blk * 256:(blk + 1) * 256])])
                wav = wview(wt, 0, KC, 256)
                wgv = wview(wt, KC * 256, KC, 256)
                for c2 in range(2):
                    j = blk * 2 + c2
                    pa, pab = ps_next()
                    pg, pgb = ps_next()
                    for kc in range(KC):
                        em.op("pe", lambda e, kc=kc, c2=c2, pa=pa, wav=wav: e.matmul(pa[:, 0:N], lhsT=wav[:, kc, c2 * 128:(c2 + 1) * 128], rhs=H[:, kc, 0:N], start=(kc == 0), stop=(kc == KC - 1)), reads=[wb, b_H], writes=[pab])
                    for kc in range(KC):
                        em.op("pe", lambda e, kc=kc, c2=c2, pg=pg, wgv=wgv: e.matmul(pg[:, 0:N], lhsT=wgv[:, kc, c2 * 128:(c2 + 1) * 128], rhs=H[:, kc, 0:N], start=(kc == 0), stop=(kc == KC - 1)), reads=[wb, b_H], writes=[pgb])
                    tt, tb = tmpA()
                    em.op("act", lambda e, pg=pg, tt=tt, j=j: e.activation(out=tt[:, 0:N], in_=pg[:, 0:N], func=AF.Sigmoid, bias=vcol(R_BPW1 + l * 2 + 1, j), scale=1.0), reads=[pgb, b_vec], writes=[tb])
                    if gi is not None:
                        em.op("dve", lambda e, tt=tt: e.tensor_tensor(out=tt[:, 0:N], in0=tt[:, 0:N], in1=VALID[:, 0:N], op=ALU.mult), reads=[tb, b_VALID], writes=[tb])
                    em.op("dve", lambda e, pa=pa, tt=tt, j=j: e.scalar_tensor_tensor(out=UBv[:, j, :, 30:30 + n], in0=pa[:, 0:N].rearrange("p (s t) -> p s t", s=S), scalar=vcol(R_BPW1 + l * 2, j), in1=tt[:, 0:N].rearrange("p (s t) -> p s t", s=S), op0=ALU.add, op1=ALU.mult),
                          reads=[pab, tb, b_vec], writes=[b_UB])
            if gi is not None and gi < NG - 1:
                em.op("act", lambda e: e.copy(out=hist[l][:], in_=UBv[:, :, 0, n:n + 30]), reads=[b_UB], writes=[b_hist[l]])
            if gi is None or gi == NG - 1:
                for s in range(S):
                    ct, cb = bigt()
                    w_ = 34 if gi is None else 30
                    o_ = 0 if gi is None else n
                    for k0 in range(0, KC, 4):
                        pt, pb = ps_next()
                        for kk in range(4):
                            em.op("pe", lambda e, kk=kk, k0=k0, s=s, pt=pt, w_=w_, o_=o_: e.transpose(pt[0:w_, kk * 128:(kk + 1) * 128], UBv[:, k0 + kk, s, o_:o_ + w_], ident[:]), reads=[b_UB, b_const], writes=[pb])
                        em.op("act", lambda e, k0=k0, ct=ct, pt=pt, w_=w_: e.copy(out=ct[0:w_, k0 * 128:(k0 + 4) * 128], in_=pt[0:w_, :]), reads=[pb], writes=[cb])
                    if gi is None:
                        em.dma("sp", lambda e, ct=ct, s=s: e.dma_start(out=conv_s[l, s], in_=ct[4:34, 0:D]), reads=[cb])
                    else:
                        em.dma("sp", lambda e, ct=ct: e.dma_start(out=conv_p[l], in_=ct[0:30, 0:D]), reads=[cb])
            Yv = Y[:, :, 0:N].rearrange("p k (s t) -> p k s t", s=S)
            for k in range(31):
                for j in range(KC):
                    wk = vcol(R_WDW + l * 31 + k, j)
                    if k == 0:
                        em.op("dve", lambda e, j=j, wk=wk: e.tensor_scalar(out=Yv[:, j], in0=UBv[:, j, :, 0:n], scalar1=wk, scalar2=vcol(R_BDW + l, j), op0=ALU.mult, op1=ALU.add), reads=[b_UB, b_vec], writes=[b_Y])
                    else:
                        em.op("dve", lambda e, j=j, wk=wk, k=k: e.scalar_tensor_tensor(out=Yv[:, j], in0=UBv[:, j, :, k:k + n], scalar=wk, in1=Yv[:, j], op0=ALU.mult, op1=ALU.add), reads=[b_UB, b_vec, b_Y], writes=[b_Y])
            em.op("act", lambda e: e.copy(out=ZS[:, :, 0:N], in_=Y[:, :, 0:N]), reads=[b_Y], writes=[b_ZS])
            p1, p1b = ps_next()
            p2, p2b = ps_next()
            for kc in range(KC):
                em.op("pe", lambda e, kc=kc, p1=p1: e.matmul(p1[:, 0:N], lhsT=onesb[:], rhs=ZS[:, kc, 0:N], start=(kc == 0), stop=(kc == KC - 1)), reads=[b_ZS, b_const], writes=[p1b])
            for kc in range(KC):
                sq, sqb = sqt()
                em.op("act", lambda e, kc=kc, sq=sq: e.activation(out=sq[:, 0:N], in_=Y[:, kc, 0:N], func=AF.Square), reads=[b_Y], writes=[sqb])
                em.op("pe", lambda e, kc=kc, p2=p2, sq=sq: e.matmul(p2[:, 0:N], lhsT=onesb[:], rhs=sq[:, 0:N], start=(kc == 0), stop=(kc == KC - 1)), reads=[sqb, b_const], writes=[p2b])
            em.op("dve", lambda e, p1=p1: e.tensor_scalar(out=MEAN[:, 0:N], in0=p1[:, 0:N], scalar1=1.0 / D, scalar2=None, op0=ALU.mult), reads=[p1b], writes=[b_MEAN])
            tt, tb = tmpA()
            em.op("dve", lambda e, tt=tt: e.tensor_tensor(out=tt[:, 0:N], in0=MEAN[:, 0:N], in1=MEAN[:, 0:N], op=ALU.mult), reads=[b_MEAN], writes=[tb])
            em.op("dve", lambda e, tt=tt, p2=p2: e.scalar_tensor_tensor(out=tt[:, 0:N], in0=p2[:, 0:N], scalar=1.0 / D, in1=tt[:, 0:N], op0=ALU.mult, op1=ALU.subtract), reads=[p2b, tb], writes=[tb])
            em.op("act", lambda e, tt=tt: e.activation(out=RSTD[:, 0:N], in_=tt[:, 0:N], func=AF.Sqrt, bias=epsc[:, 0:1], scale=1.0), reads=[tb, b_const], writes=[b_RSTD])
            em.op("dve", lambda e: e.reciprocal(out=RSTD[:, 0:N], in_=RSTD[:, 0:N]), reads=[b_RSTD], writes=[b_RSTD])
            for j in range(KC):
                tt, tb = tmpA()
                em.op("dve", lambda e, j=j, tt=tt: e.tensor_tensor(out=tt[:, 0:N], in0=Y[:, j, 0:N], in1=MEAN[:, 0:N], op=ALU.subtract), reads=[b_Y, b_MEAN], writes=[tb])
                em.op("dve", lambda e, tt=tt: e.tensor_tensor(out=tt[:, 0:N], in0=tt[:, 0:N], in1=RSTD[:, 0:N], op=ALU.mult), reads=[tb, b_RSTD], writes=[tb])
                em.op("act", lambda e, j=j, tt=tt: e.activation(out=ZS[:, j, 0:N], in_=tt[:, 0:N], func=AF.Silu, bias=vcol(R_LNB + l, j), scale=vcol(R_LNG + l, j)), reads=[tb, b_vec], writes=[b_ZS])
            w2 = w_pw2[l].rearrange("(kc p) c -> p kc c", p=128)
            for blk in range(4):
                wt, wb = wload([(0, KC, 256, w2[:, :, blk * 256:(blk + 1) * 256])])
                wv = wview(wt, 0, KC, 256)
                for c2 in range(2):
                    j = blk * 2 + c2
                    pt, pb = ps_next()
                    for kc in range(KC):
                        em.op("pe", lambda e, kc=kc, c2=c2, pt=pt, wv=wv: e.matmul(pt[:, 0:N], lhsT=wv[:, kc, c2 * 128:(c2 + 1) * 128], rhs=ZS[:, kc, 0:N], start=(kc == 0), stop=(kc == KC - 1)), reads=[wb, b_ZS], writes=[pb])
                    for si, seq in enumerate(G.seqs):
                        c0, c1 = si * n, (si + 1) * n
                        tt, tb = tmpA()
                        em.op("dve", lambda e, j=j, c0=c0, c1=c1, seq=seq, pt=pt, tt=tt: e.tensor_scalar(out=tt[:, c0:c1], in0=pt[:, c0:c1], scalar1=vcol(R_BPW2 + l, j), scalar2=modG[li][:, j, seq:seq + 1], op0=ALU.add, op1=ALU.mult), reads=[pb, b_vec, b_mod[li]], writes=[tb])
                        em.op("dve", lambda e, j=j, c0=c0, c1=c1, tt=tt: e.tensor_tensor(out=X[:, j, c0:c1], in0=X[:, j, c0:c1], in1=tt[:, c0:c1], op=ALU.add), reads=[tb, b_X], writes=[b_X])

        def kv_proj(G, out_rows, gi):
            N = G.N
            norm_mod(G, lambda kc, seq: vcol(R_GKV, kc), None, [b_vec])
            wk = w_kv.rearrange("(kc p) c -> p kc c", p=128)
            wts = []
            for cb in range(3):
                wt, wb = wload([(0, KC, 512, wk[:, :, cb * 512:(cb + 1) * 512])])
                wts.append((wview(wt, 0, KC, 512), wb))
            for t0 in range(0, N, 128):
                tn = min(128, N - t0)
                rt, rb = bigt()
                for cb in range(3):
                    wv, wb = wts[cb]
                    pt, pb = ps_next()
                    for kc in range(KC):
                        em.op("pe", lambda e, kc=kc, t0=t0, tn=tn, pt=pt, wv=wv: e.matmul(pt[0:tn, :], lhsT=H[:, kc, t0:t0 + tn], rhs=wv[:, kc, :], start=(kc == 0), stop=(kc == KC - 1)), reads=[wb, b_H], writes=[pb])
                    em.op("act", lambda e, cb=cb, tn=tn, pt=pt, rt=rt: e.copy(out=rt[0:tn, cb * 512:(cb + 1) * 512], in_=pt[0:tn, :]), reads=[pb], writes=[rb])
                if out_rows is not None:
                    em.dma("sp", lambda e, t0=t0, tn=tn, rt=rt: e.dma_start(out=out_rows[t0:t0 + tn, :], in_=rt[0:tn, :]), reads=[rb])
                if gi is None:
                    continue
                T = gi * 4 + t0 // 128
                tl = t0 // 128
                em.op("act", lambda e, rt=rt, T=T: e.copy(out=SV[:, T, :], in_=rt[:, 768:1024]), reads=[rb], writes=[b_SV])
                if T >= 12:
                    em.op("act", lambda e, rt=rt, T=T: e.copy(out=WV[:, T - 12, :], in_=rt[:, 1280:1536]), reads=[rb], writes=[b_WV])
                for which, cbase, p0 in ((0, 512, 0), (1, 1024, 64)):
                    pt, pb = ps_next()
                    for g in range(4):
                        em.op("pe", lambda e, g=g, rt=rt, pt=pt, cbase=cbase: e.transpose(pt[0:64, g * 128:(g + 1) * 128], rt[:, cbase + g * 64:cbase + (g + 1) * 64], ident[:]), reads=[rb, b_const], writes=[pb])
                    em.op("dve", lambda e, pt=pt, p0=p0, T=T: e.tensor_copy(out=KT[p0:p0 + 64, :, T * 128:(T + 1) * 128], in_=pt[0:64, :].rearrange("p (g t) -> p g t", g=4)), reads=[pb], writes=[b_KT])
                pt, pb = ps_next()
                for e_ in range(2):
                    for gp in range(2):
                        ix = e_ * 2 + gp
                        em.op("pe", lambda e, ix=ix, e_=e_, gp=gp, rt=rt, pt=pt: e.transpose(pt[:, ix * 128:(ix + 1) * 128], rt[:, e_ * 256 + gp * 128:e_ * 256 + (gp + 1) * 128], ident[:]), reads=[rb, b_const], writes=[pb])
                for e_ in range(2):
                    for gp in range(2):
                        ix = e_ * 2 + gp
                        pe_b = VEC2[:, e_ * 32:(e_ + 1) * 32].unsqueeze(1).broadcast_to([128, 4, 32])
                        em.op("dve", lambda e, ix=ix, e_=e_, gp=gp, pt=pt, tl=tl, pe_b=pe_b: e.tensor_tensor(out=CT[:, e_, gp, tl * 128:(tl + 1) * 128].rearrange("p (n s) -> p n s", s=32), in0=pt[:, ix * 128:(ix + 1) * 128].rearrange("p (n s) -> p n s", s=32), in1=pe_b, op=ALU.add),
                              reads=[pb, b_vec], writes=[b_CT])
            if gi is None:
                return
            for e_ in range(2):
                wi = w_rr[0]
                w_rr[0] = (wi + 1) % NW
                src = w_phi1[e_].rearrange("(s d) h -> d s h", d=64)
                for half in range(2):
                    em.dma("pool", lambda e, wi=wi, half=half, src=src: e.dma_start(out=wsl[wi][half * 64:(half + 1) * 64, 0:4096].rearrange("p (s h) -> p s h", s=32), in_=src), writes=[wsb[wi]])
                w1v = wsl[wi][:, 0:4096].rearrange("p (s h) -> p s h", s=32)
                w1b = wsb[wi]
                for g in range(4):
                    gp, base = g // 2, (g % 2) * 64
                    pt, pb = ps_next()
                    ctv = CT[base:base + 64, e_, gp, :].rearrange("p (n s) -> p n s", s=32)
                    for s_ in range(32):
                        em.op("pe", lambda e, s_=s_, base=base, pt=pt, ctv=ctv, w1v=w1v: e.matmul(pt[:, 0:16], lhsT=w1v[base:base + 64, s_, :], rhs=ctv[:, :, s_], start=(s_ == 0), stop=(s_ == 31)), reads=[b_CT, w1b], writes=[pb])
                    em.op("act", lambda e, e_=e_, g=g, pt=pt: e.activation(out=HIDC[:, e_, g, gi * 16:(gi + 1) * 16], in_=pt[:, 0:16], func=AF.Silu, bias=VEC2[:, 64 + e_:65 + e_], scale=1.0), reads=[pb, b_vec], writes=[b_HIDC])

        GP = Grp(1, GN, [0])
        GS = Grp(NS, ST, [1, 2, 3, 4])

        def run_group(G, gi, src_rows, out_rows):
            load_xT(G, src_rows)
            if gi is not None:
                em.dma("sp", lambda e: e.dma_start(out=VALID[:], in_=validt[:, gi * GN:(gi + 1) * GN].broadcast_to([128, GN])), writes=[b_VALID])
            for l in range(2):
                conf_layer(G, l, gi)
                ffn(G, l)
            kv_proj(G, out_rows, gi)
            if gi is not None and gi >= NG // 2:
                q = gi - NG // 2
                em.dma("sp", lambda e, q=q: e.dma_start(out=XS[:, :, q * GN:(q + 1) * GN], in_=X[:]), reads=[b_X], writes=[b_XS])

        if stage >= 1:
            for gi in range(NG):
                orow = rows_p[(gi - NG // 2) * GN:(gi - NG // 2 + 1) * GN, :] if gi >= NG // 2 else None
                run_group(GP, gi, xp[gi * GN:(gi + 1) * GN, :], orow)
            pt, pb = ps_next()
            em.op("pe", lambda e, pt=pt: e.matmul(pt[0:64, :], lhsT=W2[:, 0, :], rhs=HIDC[:, 0, :, :].rearrange("p g n -> p (g n)"), start=True, stop=True), reads=[b_HIDC, b_W1T], writes=[pb])
            em.op("act", lambda e, pt=pt: e.activation(out=CKT[:].rearrange("p g n -> p (g n)"), in_=pt[0:64, :], func=AF.Identity, bias=VEC2[0:64, 66:67], scale=1.0), reads=[pb, b_vec], writes=[b_CK])
            pt, pb = ps_next()
            for g in range(4):
                em.op("pe", lambda e, g=g, pt=pt: e.matmul(pt[:, g * 64:(g + 1) * 64], lhsT=HIDC[:, 1, g, :], rhs=W2[:, 1, :], start=True, stop=True), reads=[b_HIDC, b_W1T], writes=[pb])
            em.op("dve", lambda e, pt=pt: e.tensor_tensor(out=CV[:], in0=pt[:, 0:256].rearrange("p (g d) -> p g d", g=4), in1=B2V[:].unsqueeze(1).broadcast_to([128, 4, 64]), op=ALU.add), reads=[pb, b_W1T], writes=[b_CK])
            dbg("CKT", CKT[:], [b_CK])
            dbg("CV", CV[:], [b_CK])
            dbg("KT0", KT[:, 0, 0:512], [b_KT])
            run_group(GS, None, xs, rows_s)
            em.op("act", lambda e: e.copy(out=XSS[:], in_=X[:, :, 0:NS * ST]), reads=[b_X], writes=[b_XSS])
        em.barrier()
        stA.close()

        ps_nrot[0] = 6
        accs = [(pst[6], psb[6]), (pst[7], psb[7])]
        acc_rr = [0]

        def acc_next():
            i = acc_rr[0]
            acc_rr[0] = 1 - i
            return accs[i]

        wall_tiles = [BIG[:, 5632:5632 + 1664], sb("wall1", [128, 1664])]
        wall_bufs = [em.buf("wall0"), em.buf("wall1")]
        wall_rr = [0]
        wall_first = [True]

        def wall():
            i = wall_rr[0]
            wall_rr[0] = 1 - i
            return wall_tiles[i], wall_bufs[i]
        Pt_own = rot("Pt", [128, 512], BF16, 4)
        pt_pool = list(zip(Pt_own.tiles, Pt_own.bufs)) + list(zip(sqt.tiles, sqt.bufs))
        pt_c = [0]

        def Pt():
            i = pt_c[0]
            pt_c[0] = (i + 1) % len(pt_pool)
            return pt_pool[i]
        PT = rot("PTt", [128, 512], BF16, 3)
        sm = rot("sm", [128, 4], F32, 8)
        GATE = sb("GATE", [128, 4, 48])
        IMP = sb("IMP", [128, 256])
        IMPB = sb("IMPB", [128, 128])
        SCORE = sb("SCORE", [128, 128])
        SC2 = sb("SC2", [128, 128])
        MASK = sb("MASK", [128, 128])
        T8 = sb("T8", [128, 16])
        b_GATE, b_IMP, b_SCORE, b_MASK = [em.buf(n) for n in ("GATE", "IMP", "SCORE", "MASK")]
        evac_rr = [0]
        dbg_once = []

        def evac(out, in_, reads, writes):
            evac_rr[0] ^= 1
            if evac_rr[0]:
                em.op("act", lambda e: e.copy(out=out, in_=in_), reads=reads, writes=writes)
            else:
                em.op("dve", lambda e: e.tensor_copy(out=out, in_=in_), reads=reads, writes=writes)

        def pipeline(units, lagB=2, lagC=4):
            n = len(units)
            for t in range(n + lagC):
                if t < n:
                    units[t][0]()
                if 0 <= t - lagB < n:
                    units[t - lagB][1]()
                if 0 <= t - lagC < n:
                    units[t - lagC][2]()

        def transposeP(u, wu):
            nq = u["nq"]
            tp, tpb = ps_next()
            tpv = tp[:].bitcast(BF16)
            p_, p_b = u["p_"], u["p_b"]
            off = 0
            j = 0
            while off < wu:
                w1 = min(128, wu - off)
                em.op("pe", lambda e, j=j, off=off, w1=w1, p_=p_, tpv=tpv: e.transpose(tpv[0:w1, j * 128:j * 128 + nq], p_[0:nq, off:off + w1], identb[0:nq, 0:nq]), reads=[p_b, b_const], writes=[tpb])
                off += w1
                j += 1
            pT, pTb = PT()
            if wu >= 128:
                evac(pT[:, 0:j * 128], tpv[:, 0:j * 128], [tpb], [pTb])
            else:
                evac(pT[0:wu, 0:128], tpv[0:wu, 0:128], [tpb], [pTb])
            u["pT"], u["pTb"] = pT, pTb

        def attn_core(nq, qb_cols, gate_ap, otm, otm_b, g, heads, q0, first_blk, cmpK, cmpV, b_cmp, n_lo, n_hi, vbc_ap, selK, selV_of, winK, winV_of, sel_units, win_units, need_vb, jb_cfg, QTv, b_QTv):
            units = []
            for r, h in enumerate(heads):
                u = {"nq": nq}

                def A(u=u, r=r, h=h):
                    ps, pb = ps_next()
                    em.op("pe", lambda e: e.matmul(ps[0:nq, 0:n_hi], lhsT=QTv[0:64, h, qb_cols], rhs=cmpK[0:64, g, 0:n_hi], start=True, stop=False), reads=[b_QTv, b_cmp], writes=[pb])
                    em.op("pe", lambda e: e.matmul(ps[0:nq, 0:n_hi], lhsT=onesb[0:1, 0:nq], rhs=vbc_ap[0:1, 0:n_hi], start=False, stop=True), reads=[b_const, b_bias], writes=[pb])
                    sc, scb = tmpA()
                    em.op("dve", lambda e: e.tensor_scalar(out=sc[0:nq, 0:n_lo], in0=ps[0:nq, 0:n_lo], scalar1=CH[0:nq, h:h + 1], scalar2=None, op0=ALU.add), reads=[pb, b_bias], writes=[scb])
                    em.op("dve", lambda e: e.tensor_tensor(out=sc[0:nq, n_lo:n_hi], in0=ps[0:nq, n_lo:n_hi], in1=BCALL[0:nq, h, 0:n_hi - n_lo], op=ALU.add), reads=[pb, b_bias], writes=[scb])
                    s4, s4b = sm()
                    em.op("act", lambda e: e.activation(out=sc[0:nq, 0:n_hi], in_=sc[0:nq, 0:n_hi], func=AF.Exp, bias=NEGM[0:nq, h:h + 1], scale=1.0), reads=[scb, b_bias], writes=[scb])
                    em.op("dve", lambda e: e.tensor_reduce(out=s4[0:nq, 0:1], in_=sc[0:nq, 0:n_hi], axis=AX.X, op=ALU.add), reads=[scb], writes=[s4b])
                    em.op("dve", lambda e: e.tensor_scalar(out=s4[0:nq, 1:2], in0=s4[0:nq, 0:1], scalar1=1e-30, scalar2=None, op0=ALU.max), reads=[s4b], writes=[s4b])
                    em.op("dve", lambda e: e.reciprocal(out=s4[0:nq, 2:3], in_=s4[0:nq, 1:2]), reads=[s4b], writes=[s4b])
                    if r == 0:
                        em.op("dve", lambda e: e.memset(IMP[:], 0.0), writes=[b_IMP])
                        em.op("dve", lambda e: e.tensor_scalar(out=IMP[0:nq, 0:n_hi], in0=sc[0:nq, 0:n_hi], scalar1=s4[0:nq, 2:3], scalar2=None, op0=ALU.mult), reads=[scb, s4b], writes=[b_IMP])
                    else:
                        em.op("dve", lambda e: e.scalar_tensor_tensor(out=IMP[0:nq, 0:n_hi], in0=sc[0:nq, 0:n_hi], scalar=s4[0:nq, 2:3], in1=IMP[0:nq, 0:n_hi], op0=ALU.mult, op1=ALU.add), reads=[scb, s4b, b_IMP], writes=[b_IMP])
                    em.op("dve", lambda e: e.tensor_tensor(out=s4[0:nq, 3:4], in0=s4[0:nq, 2:3], in1=gate_ap(h, 0), op=ALU.mult), reads=[s4b, b_GATE], writes=[s4b])
                    p_, p_b = Pt()
                    em.op("act", lambda e: e.copy(out=p_[0:nq, 0:n_hi], in_=sc[0:nq, 0:n_hi]), reads=[scb], writes=[p_b])
                    u.update(p_=p_, p_b=p_b, s4=s4, s4b=s4b)

                def B(u=u):
                    transposeP(u, n_hi)

                def C(u=u, h=h):
                    po, pob = ps_next()
                    pT, pTb = u["pT"], u["pTb"]
                    nt = (n_hi + 127) // 128
                    for j in range(nt):
                        w1 = min(128, n_hi - j * 128)
                        em.op("pe", lambda e, j=j, w1=w1: e.matmul(po[0:nq, 0:64], lhsT=pT[0:w1, j * 128:j * 128 + nq], rhs=cmpV(j, w1), start=(j == 0), stop=(j == nt - 1)), reads=[pTb, b_cmp], writes=[pob])
                    s4, s4b = u["s4"], u["s4b"]
                    em.op("dve", lambda e: e.tensor_scalar(out=otm[0:nq, h * 64:(h + 1) * 64], in0=po[0:nq, 0:64], scalar1=s4[0:nq, 3:4], scalar2=None, op0=ALU.mult), reads=[pob, s4b], writes=[otm_b])
                units.append((A, B, C))
            pipeline(units)
            jb_cfg()
            units = []
            for r, h in enumerate(heads):
                hs = {}
                for br in (1, 2):
                    ul = sel_units if br == 1 else win_units
                    ntile = sum((w_ + 127) // 128 for (_, w_, _) in ul)
                    bs = {"tcount": 0, "ntile": ntile}
                    for ui, (k0, wu, kind) in enumerate(ul):
                        u = {"nq": nq}

                        def A(u=u, h=h, br=br, ui=ui, k0=k0, wu=wu, kind=kind, hs=hs, bs=bs):
                            if "wl" not in hs:
                                wl, wlb = wall()
                                extra = [b_BIG] if wall_first[0] else []
                                wall_first[0] = False
                                em.dma("sp", lambda e: e.dma_start(out=wl[0:nq, :], in_=WS[h][0:nq, :]), reads=[b_ws] + extra, writes=[wlb])
                                hs["wl"], hs["wlb"] = wl, wlb
                            wl, wlb = hs["wl"], hs["wlb"]
                            ps, pb = ps_next()
                            p_, p_b = Pt()
                            if br == 1:
                                kap, kbuf = selK(k0, wu)
                                em.op("pe", lambda e: e.matmul(ps[0:nq, 0:wu], lhsT=QTv[0:64, h, qb_cols], rhs=kap, start=True, stop=True), reads=[b_QTv, kbuf], writes=[pb])
                                tt = None
                                if kind[0] == "near":
                                    cs, c0 = kind[1], kind[2]
                                    tt, tb = tmpA()
                                    if cs > 0:
                                        em.op("dve", lambda e: e.tensor_scalar(out=tt[0:nq, 0:cs], in0=ps[0:nq, 0:cs], scalar1=CH[0:nq, h:h + 1], scalar2=None, op0=ALU.add), reads=[pb, b_bias], writes=[tb])
                                    em.op("dve", lambda e: e.tensor_tensor(out=tt[0:nq, cs:wu], in0=ps[0:nq, cs:wu], in1=wl[0:nq, c0:c0 + wu - cs], op=ALU.add), reads=[pb, wlb], writes=[tb])
                                    pf, pfb = Pt()
                                    em.op("act", lambda e: e.activation(out=pf[0:nq, 0:wu], in_=tt[0:nq, 0:wu], func=AF.Exp, bias=NEGM[0:nq, h:h + 1], scale=1.0), reads=[tb, b_bias], writes=[pfb])
                                else:
                                    pf, pfb = Pt()
                                    em.op("act", lambda e: e.activation(out=pf[0:nq, 0:wu], in_=ps[0:nq, 0:wu], func=AF.Exp, bias=NEGMC[0:nq, h:h + 1], scale=1.0), reads=[pb, b_bias], writes=[pfb])
                                if kind[-1] == "nomask":
                                    p_, p_b = pf, pfb
                                else:
                                    nb = wu // 64
                                    mcol = kind[-1]
                                    mk = MASK[0:nq, mcol:mcol + nb].unsqueeze(2).broadcast_to([nq, nb, 64])
                                    em.op("dve", lambda e: e.scalar_tensor_tensor(out=p_[0:nq, 0:wu].rearrange("p (b k) -> p b k", k=64), in0=pf[0:nq, 0:wu].rearrange("p (b k) -> p b k", k=64), scalar=1.0, in1=mk, op0=ALU.mult, op1=ALU.mult),
                                          reads=[pfb, b_MASK], writes=[p_b])
                            else:
                                kap, kbuf = winK(k0, wu)
                                vb = need_vb(k0, wu)
                                em.op("pe", lambda e: e.matmul(ps[0:nq, 0:wu], lhsT=QTv[64:128, h, qb_cols], rhs=kap, start=True, stop=(vb is None)), reads=[b_QTv, kbuf], writes=[pb])
                                if vb is not None:
                                    em.op("pe", lambda e: e.matmul(ps[0:nq, 0:wu], lhsT=onesb[0:1, 0:nq], rhs=vb, start=False, stop=True), reads=[b_const, b_bias], writes=[pb])
                                c0 = kind[1]
                                tt, tb = tmpA()
                                em.op("dve", lambda e: e.tensor_tensor(out=tt[0:nq, 0:wu], in0=ps[0:nq, 0:wu], in1=wl[0:nq, c0:c0 + wu], op=ALU.add), reads=[pb, wlb], writes=[tb])
                                em.op("act", lambda e: e.activation(out=p_[0:nq, 0:wu], in_=tt[0:nq, 0:wu], func=AF.Exp, bias=NEGM[0:nq, h:h + 1], scale=1.0), reads=[tb, b_bias], writes=[p_b])
                            u.update(p_=p_, p_b=p_b)

                        def B(u=u, wu=wu):
                            transposeP(u, wu)

                        def C(u=u, h=h, br=br, ui=ui, k0=k0, wu=wu, bs=bs, nul=len(ul)):
                            if ui == 0:
                                bs["po"], bs["pob"] = acc_next()
                            po, pob = bs["po"], bs["pob"]
                            pT, pTb = u["pT"], u["pTb"]
                            nt = (wu + 127) // 128
                            for j in range(nt):
                                w1 = min(128, wu - j * 128)
                                vap, vbuf = (selV_of if br == 1 else winV_of)(k0 + j * 128, w1)
                                first = (bs["tcount"] == 0)
                                last = (bs["tcount"] == bs["ntile"] - 1)
                                em.op("pe", lambda e, j=j, w1=w1, vap=vap, first=first: e.matmul(po[0:nq, 0:64], lhsT=pT[0:w1, j * 128:j * 128 + nq], rhs=vap, start=first, stop=False), reads=[pTb, vbuf], writes=[pob])
                                em.op("pe", lambda e, j=j, w1=w1, last=last: e.matmul(po[0:nq, 64:65], lhsT=pT[0:w1, j * 128:j * 128 + nq], rhs=onesb[0:w1, 0:1], start=False, stop=last), reads=[pTb, b_const], writes=[pob])
                                bs["tcount"] += 1
                            if ui == nul - 1:
                                s4, s4b = sm()
                                em.op("dve", lambda e: e.reciprocal(out=s4[0:nq, 2:3], in_=po[0:nq, 64:65]), reads=[pob], writes=[s4b])
                                em.op("dve", lambda e: e.tensor_tensor(out=s4[0:nq, 3:4], in0=s4[0:nq, 2:3], in1=gate_ap(h, br), op=ALU.mult), reads=[s4b, b_GATE], writes=[s4b])
                                em.op("dve", lambda e: e.scalar_tensor_tensor(out=otm[0:nq, h * 64:(h + 1) * 64], in0=po[0:nq, 0:64], scalar=s4[0:nq, 3:4], in1=otm[0:nq, h * 64:(h + 1) * 64], op0=ALU.mult, op1=ALU.add), reads=[pob, s4b, otm_b], writes=[otm_b])
                        units.append((A, B, C))
            pipeline(units)

        def select_blocks(nq, memsets, kth, ncol=64, use_data=True):
            R_ = slice(0, nq)
            C_ = slice(0, ncol)
            em.op("dve", lambda e: e.tensor_tensor(out=SCORE[R_, C_], in0=IMP[R_, 0:2 * ncol:2], in1=IMP[R_, 1:2 * ncol:2], op=ALU.add), reads=[b_IMP], writes=[b_SCORE])
            for (r0, r1, c0, c1, val) in memsets:
                r1 = min(r1, nq)
                if r0 >= r1:
                    continue
                em.op("dve", lambda e, r0=r0, r1=r1, c0=c0, c1=c1, val=val: e.memset(SCORE[r0:r1, c0:c1], val), reads=[b_SCORE], writes=[b_SCORE])
            if use_data:
                em.op("dve", lambda e: e.tensor_tensor(out=SCORE[R_, C_], in0=SCORE[R_, C_], in1=F0[R_, C_], op=ALU.max), reads=[b_SCORE, b_bias], writes=[b_SCORE])
                em.op("dve", lambda e: e.tensor_tensor(out=SCORE[R_, C_], in0=SCORE[R_, C_], in1=VLIM[R_, C_], op=ALU.min), reads=[b_SCORE, b_bias], writes=[b_SCORE])
            em.op("dve", lambda e: e.max(out=T8[R_, 0:8], in_=SCORE[R_, C_]), reads=[b_SCORE], writes=[b_SCORE])
            em.op("dve", lambda e: e.match_replace(out=SC2[R_, C_], in_to_replace=T8[R_, 0:8], in_values=SCORE[R_, C_], imm_value=-1e30), reads=[b_SCORE], writes=[b_SCORE])
            em.op("dve", lambda e: e.max(out=T8[R_, 8:16], in_=SC2[R_, C_]), reads=[b_SCORE], writes=[b_SCORE])
            em.op("dve", lambda e: e.tensor_scalar(out=SC2[R_, C_], in0=SCORE[R_, C_], scalar1=T8[R_, kth - 1:kth], scalar2=None, op0=ALU.is_ge), reads=[b_SCORE], writes=[b_SCORE])
            em.op("dve", lambda e: e.scalar_tensor_tensor(out=MASK[R_, C_], in0=SCORE[R_, C_], scalar=0.0, in1=SC2[R_, C_], op0=ALU.is_ge, op1=ALU.mult), reads=[b_SCORE], writes=[b_MASK])

        def qg_proj(G, lb, gate_rows):
            N = G.N
            wq = w_qg[lb].rearrange("(kc p) c -> p kc c", p=128)
            for hb in range(4):
                wt, wb = wload([(0, KC, 256, wq[:, :, hb * 256:(hb + 1) * 256])])
                wv = wview(wt, 0, KC, 256)
                for hp in range(2):
                    pt, pb = ps_next()
                    for kc in range(KC):
                        em.op("pe", lambda e, kc=kc, hp=hp, pt=pt, wv=wv: e.matmul(pt[:, 0:N], lhsT=wv[:, kc, hp * 128:(hp + 1) * 128], rhs=H[:, kc, 0:N], start=(kc == 0), stop=(kc == KC - 1)), reads=[wb, b_H], writes=[pb])
                    h0 = hb * 4 + hp * 2
                    for (src0, hh) in ((0, h0), (64, h0 + 1)):
                        for dst0 in (0, 64):
                            if (src0 + dst0) % 128 == 0 and False:
                                pass
                            em.op("act", lambda e, src0=src0, dst0=dst0, hh=hh, pt=pt: e.activation(out=QT[dst0:dst0 + 64, hh, 0:N], in_=pt[src0:src0 + 64, 0:N], func=AF.Copy, scale=0.125), reads=[pb], writes=[b_QT])
            wt, wb = wload([(0, KC, 48, wq[:, :, 1024:1072])])
            wgv = wview(wt, 0, KC, 48)
            for qb, (t0, tn) in enumerate(gate_rows):
                pt, pb = ps_next()
                for kc in range(KC):
                    em.op("pe", lambda e, kc=kc, t0=t0, tn=tn, pt=pt, wgv=wgv: e.matmul(pt[0:tn, 0:48], lhsT=H[:, kc, t0:t0 + tn], rhs=wgv[:, kc, :], start=(kc == 0), stop=(kc == KC - 1)), reads=[wb, b_H], writes=[pb])
                em.op("act", lambda e, qb=qb, tn=tn, pt=pt: e.activation(out=GATE[0:tn, qb, :], in_=pt[0:tn, 0:48], func=AF.Sigmoid), reads=[pb], writes=[b_GATE])

        def wo_proj(G, lb, li):
            N = G.N
            wo = w_o[lb].rearrange("(kc p) c -> p kc c", p=128)
            for blk in range(4):
                wt, wb = wload([(0, KC, 256, wo[:, :, blk * 256:(blk + 1) * 256])])
                wv = wview(wt, 0, KC, 256)
                for c2 in range(2):
                    j = blk * 2 + c2
                    pt, pb = ps_next()
                    for kc in range(KC):
                        em.op("pe", lambda e, kc=kc, c2=c2, pt=pt, wv=wv: e.matmul(pt[:, 0:N], lhsT=wv[:, kc, c2 * 128:(c2 + 1) * 128], rhs=ZS[:, kc, 0:N], start=(kc == 0), stop=(kc == KC - 1)), reads=[wb, b_ZS], writes=[pb])
                    for si, seq in enumerate(G.seqs):
                        c0, c1 = si * G.n, (si + 1) * G.n
                        em.op("dve", lambda e, j=j, c0=c0, c1=c1, seq=seq, pt=pt: e.scalar_tensor_tensor(out=X[:, j, c0:c1], in0=pt[:, c0:c1], scalar=modG[li][:, j, seq:seq + 1], in1=X[:, j, c0:c1], op0=ALU.mult, op1=ALU.add), reads=[pb, b_X, b_mod[li]], writes=[b_X])

        def otm_to_ZS(nq, otm, otm_b, cols):
            for k0 in range(0, KC, 4):
                pt, pb = ps_next()
                for kk in range(4):
                    em.op("pe", lambda e, kk=kk, k0=k0, pt=pt: e.transpose(pt[:, kk * 128:kk * 128 + nq], otm[0:nq, (k0 + kk) * 128:(k0 + kk + 1) * 128], ident[0:nq, 0:nq]), reads=[otm_b, b_const], writes=[pb])
                evac(ZS[:, k0:k0 + 4, cols], pt[:].rearrange("p (k t) -> p k t", k=4)[:, :, 0:nq], [pb], [b_ZS])

        def attn_prompt(gq, lb):
            li = (2 + lb) * 2
            norm_mod(GP, lambda kc, seq: modA[li][:, kc, seq:seq + 1], lambda kc, seq: modB[li][:, kc, seq:seq + 1], [b_mod[li]])
            qg_proj(GP, lb, [(qb * 128, 128) for qb in range(4)])
            for qb in range(4):
                i = 4 * gq + qb
                q0 = QH0 + 128 * i
                qc = slice(qb * 128, (qb + 1) * 128)
                otm, otm_b = bigt()
                jb = q0 // 64
                n_hi = q0 // 32 + 4
                n_lo = q0 // 32 - 28
                sel_units = []
                k0 = 0
                while k0 < q0 + 128:
                    wu = min(512, q0 + 128 - k0)
                    if k0 + wu > q0 - 896:
                        cs = max(k0, q0 - 896) - k0
                        sel_units.append((k0, wu, ("near", cs, (k0 + cs) - (q0 - 896), k0 // 64)))
                    else:
                        sel_units.append((k0, wu, ("far", k0 // 64)))
                    k0 += 512
                win_units = [(q0 - 512, 512, ("win", 1024)), (q0, 128, ("win", 1024 + 512))]
                memsets = []
                if jb + 2 < 64:
                    memsets.append((0, 128, jb + 2, 64, -1.0))
                memsets += [(0, 64, jb + 1, jb + 2, -1.0), (64, 128, jb + 1, jb + 2, 1e4), (0, 128, jb, jb + 1, 1e4), (0, 64, jb - 1, jb, 1e4)]
                for g in range(4):
                    attn_core(
                        128, qc, lambda h, br, qb=qb: GATE[:, qb, h * 3 + br:h * 3 + br + 1], otm, otm_b, g, [4 * g + r for r in range(4)], q0, None,
                        CKT, lambda j, w1, g=g: CV[0:w1, g, :], b_CK, n_lo, n_hi, VBC,
                        lambda k0, wu, g=g: (KT[0:64, g, k0:k0 + wu], b_KT), lambda k, w1, g=g: (SV[0:w1, k // 128, g * 64:(g + 1) * 64], b_SV),
                        lambda k0, wu, g=g: (KT[64:128, g, k0:k0 + wu], b_KT), lambda k, w1, g=g: (WV[0:w1, k // 128 - 12, g * 64:(g + 1) * 64], b_WV),
                        sel_units, win_units,
                        (lambda k0, wu, i=i: VBW[0:1, k0 - 1536:k0 - 1536 + wu] if i < 4 else None),
                        lambda memsets=memsets: select_blocks(128, memsets, 16), QT, b_QT)
                if gq == 0 and lb == 0:
                    dbg("otm_q%d" % qb, otm[:, 0:D], [otm_b])
                otm_to_ZS(128, otm, otm_b, qc)
            if gq == 0 and lb == 0:
                dbg("BCALL", BCALL[:], [b_bias])
                dbg("CH", CH[:], [b_bias])
                dbg("WS", WS, [b_ws])
                dbg("IMP", IMP[:, 0:128], [b_IMP])
                dbg("SCORE", SCORE[:, 0:64], [b_SCORE])
                dbg("SC2", SC2[:, 0:64], [b_SCORE])
                dbg("T8", T8[:], [b_SCORE])
                dbg("MASK", MASK[:, 0:64], [b_MASK])
                dbg("QT", QT[:, :, 0:128], [b_QT])
                dbg("GATE", GATE[:], [b_GATE])
                dbg("ZS", ZS[:, :, 0:128], [b_ZS])
            wo_proj(GP, lb, li)

        def final_out(G, out_rows):
            N = G.N
            Yf = BIG[:, 0:KC * GN].rearrange("p (k t) -> p k t", k=KC)
            norm_mod(G, lambda kc, seq: vcol(R_FG, kc), None, [b_vec], out=Yf, out_buf=b_Y)
            for t0 in range(0, N, 128):
                tn = min(128, N - t0)
                yt, ytb = bigt()
                for k0 in range(0, KC, 4):
                    pt, pb = ps_next()
                    for kk in range(4):
                        em.op("pe", lambda e, kk=kk, k0=k0, t0=t0, tn=tn, pt=pt: e.transpose(pt[0:tn, kk * 128:(kk + 1) * 128], Yf[:, k0 + kk, t0:t0 + tn], ident[:]), reads=[b_Y, b_const], writes=[pb])
                    evac(yt[0:tn, k0 * 128:(k0 + 4) * 128], pt[0:tn, :], [pb], [ytb])
                em.dma("sp", lambda e, t0=t0, tn=tn, yt=yt: e.dma_start(out=out_rows[t0:t0 + tn, :], in_=yt[0:tn, 0:D]), reads=[ytb])

        if stage >= 3:
            ngq = 4 if stage >= 4 else 1
            for gq in range(ngq):
                em.dma("sp", lambda e, gq=gq: e.dma_start(out=X[:], in_=XS[:, :, gq * GN:(gq + 1) * GN]), reads=[b_XS], writes=[b_X])
                for lb in range(2):
                    attn_prompt(gq, lb)
                    dbg("xa%d_%d" % (gq, lb), X[:, :, 0:16], [b_X])
                    ffn(GP, 2 + lb)
                    dbg("xb%d_%d" % (gq, lb), X[:, :, 0:16], [b_X])
                final_out(GP, y_p[gq * GN:(gq + 1) * GN, :])

        if stage >= 5:
            em.barrier()
            q0s = 8192
            KTflat = KT[:].rearrange("p g k -> p (g k)")
            KTf32 = KTflat.bitcast(F32)
            PG = KTf32[:, 0:2048].rearrange("p (q c) -> p q c", q=4)
            WVs = KTflat[:, 4096:4096 + 1280].rearrange("p (t c) -> p t c", t=5)
            KN = KTflat[:, 5632:5632 + 16].rearrange("p (g k) -> p g k", g=4)
            WKN = KTflat[:, 5696:5696 + 16].rearrange("p (g k) -> p g k", g=4)
            VN = KTflat[:, 5760:5760 + 256]
            KU = KTflat[:, 8192:12288].rearrange("p (b g k) -> p b g k", b=2, g=4)
            VU = KTflat[:, 12288:14336].rearrange("p (b q c) -> p b q c", b=2, q=4)
            WKs = KTflat[:, 14336:16384].rearrange("p (g k) -> p g k", g=4)
            SVflat = SV[:].rearrange("p t c -> p (t c)")
            CKTs = SVflat[:, 0:4096].rearrange("p (s g n) -> p s g n", s=4, g=4)
            CVs = SVflat[:, 4096:6144].rearrange("p (s j g d) -> p s j g d", s=4, j=2, g=4)
            HIDCs = SVflat[:, 6144:8192].rearrange("p (e g n) -> p e g n", e=2, g=4)
            WVflat = WV[:].rearrange("p t c -> p (t c)")
            CTs = WVflat[:, 0:4096].rearrange("p (e a n) -> p e a n", e=2, a=2)
            IDXf = WVflat[:, 4096:4608].bitcast(F32)
            IDX = WVflat[:, 4608:5120].bitcast(I32)
            PIO = KTflat[:, 6016:6018].bitcast(F32)
            OACC = KTflat[0:4, 6080:6080 + 2080].bitcast(F32).rearrange("p (h d) -> p h d", h=16)
            RSs = KTflat[0:4, 14336:14336 + 640].bitcast(F32).rearrange("p (h u) -> p h u", h=16)
            MASKs = KTflat[0:4, 14336 + 640:14336 + 640 + 1024].bitcast(F32).rearrange("p (g n) -> p g n", g=4)
            ZB = KTflat[0:1, 14336 + 1664:14336 + 1664 + 256]
            b_PG, b_WVs, b_KN, b_KU, b_VU, b_WKs, b_scmp, b_HIDCs, b_CTs, b_IDX, b_MASKs, b_OACC, b_RSs = [em.buf(n) for n in
                ("PG", "WVs", "KN", "KU", "VU", "WKs", "scmp", "HIDCs", "CTs", "IDX", "MASKs", "OACC", "RSs")]
            b_KUb = [em.buf("KU0"), em.buf("KU1")]
            b_VUb = [em.buf("VU0"), em.buf("VU1")]
            em.op("dve", lambda e: e.memset(ZB[:], 0.0), writes=[b_IDX])
            em.dma("sp", lambda e: e.dma_start(out=IDX[:], in_=ptab.rearrange("s j -> (s j)").unsqueeze(0).broadcast_to([128, NS * 64])), writes=[b_IDX])
            em.op("pool", lambda e: e.iota(PIO[:], [[1, 1]], base=0, channel_multiplier=1, allow_small_or_imprecise_dtypes=True), writes=[b_IDX])
            em.op("dve", lambda e: e.tensor_copy(out=IDXf[:], in_=IDX[:]), reads=[b_IDX], writes=[b_IDX])
            em.op("dve", lambda e: e.tensor_scalar(out=IDXf[:], in0=IDXf[:], scalar1=128.0, scalar2=PIO[:, 0:1], op0=ALU.mult, op1=ALU.add), reads=[b_IDX], writes=[b_IDX])
            em.op("dve", lambda e: e.tensor_copy(out=IDX[:], in_=IDXf[:]), reads=[b_IDX], writes=[b_IDX])

            def gather4(cache, s, pg0):
                for q in range(4):
                    col = s * 64 + pg0 + q
                    em.dma("pool", lambda e, q=q, col=col: e.indirect_dma_start(out=PG[:, q, :], out_offset=None, in_=cache, in_offset=bass.IndirectOffsetOnAxis(ap=IDX[:, col:col + 1], axis=0)), reads=[b_IDX], writes=[b_PG])

            for s in range(NS):
                em.dma("sp", lambda e, s=s: e.dma_start(out=win_s[s, 0:508, :], in_=cache_win[s, 4:512, :]))
                em.dma("sp", lambda e, s=s: e.dma_start(out=win_s[s, 508:512, :], in_=rows_s[s * ST:(s + 1) * ST, 1024:1536]))

            for s in range(NS):
                for bt in range(8):
                    for sub in range(2):
                        gather4(cache_cmp, s, bt * 8 + sub * 4)
                        for q in range(4):
                            tl = sub * 4 + q
                            pt, pb = ps_next()
                            for e_ in range(2):
                                for gp in range(2):
                                    ix = e_ * 2 + gp
                                    em.op("pe", lambda e, ix=ix, e_=e_, gp=gp, q=q, pt=pt: e.transpose(pt[:, ix * 128:(ix + 1) * 128], PG[:, q, e_ * 256 + gp * 128:e_ * 256 + (gp + 1) * 128], ident[:]), reads=[b_PG, b_const], writes=[pb])
                            for e_ in range(2):
                                for gp in range(2):
                                    ix = e_ * 2 + gp
                                    pe_b = VEC2[:, e_ * 32:(e_ + 1) * 32].unsqueeze(1).broadcast_to([128, 4, 32])
                                    em.op("dve", lambda e, ix=ix, e_=e_, gp=gp, pt=pt, tl=tl, pe_b=pe_b: e.tensor_tensor(out=CTs[:, e_, gp, tl * 128:(tl + 1) * 128].rearrange("p (n s) -> p n s", s=32), in0=pt[:, ix * 128:(ix + 1) * 128].rearrange("p (n s) -> p n s", s=32), in1=pe_b, op=ALU.add),
                                          reads=[pb, b_vec], writes=[b_CTs])
                    for e_ in range(2):
                        wi = w_rr[0]
                        w_rr[0] = (wi + 1) % NW
                        src = w_phi1[e_].rearrange("(s d) h -> d s h", d=64)
                        for half in range(2):
                            em.dma("pool", lambda e, wi=wi, half=half, src=src: e.dma_start(out=wsl[wi][half * 64:(half + 1) * 64, 0:4096].rearrange("p (s h) -> p s h", s=32), in_=src), writes=[wsb[wi]])
                        w1v = wsl[wi][:, 0:4096].rearrange("p (s h) -> p s h", s=32)
                        w1b = wsb[wi]
                        for g in range(4):
                            gp, base = g // 2, (g % 2) * 64
                            pt, pb = ps_next()
                            ctv = CTs[base:base + 64, e_, gp, :].rearrange("p (n s) -> p n s", s=32)
                            for s_ in range(32):
                                em.op("pe", lambda e, s_=s_, base=base, pt=pt, ctv=ctv, w1v=w1v: e.matmul(pt[:, 0:32], lhsT=w1v[base:base + 64, s_, :], rhs=ctv[:, :, s_], start=(s_ == 0), stop=(s_ == 31)), reads=[b_CTs, w1b], writes=[pb])
                            em.op("act", lambda e, e_=e_, g=g, pt=pt, bt=bt: e.activation(out=HIDCs[:, e_, g, bt * 32:(bt + 1) * 32], in_=pt[:, 0:32], func=AF.Silu, bias=VEC2[:, 64 + e_:65 + e_], scale=1.0), reads=[pb, b_vec], writes=[b_HIDCs])
                for half in range(2):
                    pt, pb = ps_next()
                    em.op("pe", lambda e, pt=pt, half=half: e.matmul(pt[0:64, :], lhsT=W2[:, 0, :], rhs=HIDCs[:, 0, half * 2:half * 2 + 2, :].rearrange("p g n -> p (g n)"), start=True, stop=True), reads=[b_HIDCs, b_W1T], writes=[pb])
                    em.op("act", lambda e, pt=pt, half=half, s=s: e.activation(out=CKTs[0:64, s, half * 2:half * 2 + 2, :].rearrange("p g n -> p (g n)"), in_=pt[0:64, :], func=AF.Identity, bias=VEC2[0:64, 66:67], scale=1.0), reads=[pb, b_vec], writes=[b_scmp])
                for j in range(2):
                    pt, pb = ps_next()
                    for g in range(4):
                        em.op("pe", lambda e, g=g, j=j, pt=pt: e.matmul(pt[:, g * 64:(g + 1) * 64], lhsT=HIDCs[:, 1, g, j * 128:(j + 1) * 128], rhs=W2[:, 1, :], start=True, stop=True), reads=[b_HIDCs, b_W1T], writes=[pb])
                    em.op("dve", lambda e, pt=pt, j=j, s=s: e.tensor_tensor(out=CVs[:, s, j, :, :], in0=pt[:, 0:256].rearrange("p (g d) -> p g d", g=4), in1=B2V[:].unsqueeze(1).broadcast_to([128, 4, 64]), op=ALU.add), reads=[pb, b_W1T], writes=[b_scmp])

            def build_unit(cache_rows_tile_of, buf):
                for g in range(4):
                    pt, pb = ps_next()
                    for q in range(4):
                        em.op("pe", lambda e, g=g, q=q, pt=pt: e.transpose(pt[0:64, q * 128:(q + 1) * 128], PG[:, q, g * 64:(g + 1) * 64], ident[:]), reads=[b_PG, b_const], writes=[pb])
                    evac(KU[0:64, buf, g, :], pt[0:64, :], [pb], [b_KUb[buf]])
                em.op("act", lambda e: e.copy(out=VU[:, buf, :, :], in_=PG[:, :, 256:512]), reads=[b_PG], writes=[b_VUb[buf]])

            def sample_attn(lb):
                li = (2 + lb) * 2
                norm_mod(GS, lambda kc, seq: modA[li][:, kc, seq:seq + 1], lambda kc, seq: modB[li][:, kc, seq:seq + 1], [b_mod[li]])
                qg_proj(GS, lb, [(s * ST, ST) for s in range(NS)])
                for s in range(NS):
                    qc = slice(s * ST, (s + 1) * ST)
                    otm, otm_b = bigt()
                    nr, nrb = bigt()
                    em.dma("sp", lambda e, nr=nr, s=s: e.dma_start(out=nr[0:ST, :], in_=rows_s[s * ST:(s + 1) * ST, :]), writes=[nrb])
                    pt, pb = ps_next()
                    for g in range(4):
                        em.op("pe", lambda e, g=g, nr=nr, pt=pt: e.transpose(pt[0:64, g * 4:g * 4 + 4], nr[0:ST, 512 + g * 64:512 + (g + 1) * 64], ident[0:ST, 0:ST]), reads=[nrb, b_const], writes=[pb])
                        em.op("pe", lambda e, g=g, nr=nr, pt=pt: e.transpose(pt[0:64, 16 + g * 4:16 + g * 4 + 4], nr[0:ST, 1024 + g * 64:1024 + (g + 1) * 64], ident[0:ST, 0:ST]), reads=[nrb, b_const], writes=[pb])
                    em.op("dve", lambda e, pt=pt: e.tensor_copy(out=KN[0:64, :, :], in_=pt[0:64, 0:16].rearrange("p (g k) -> p g k", g=4)), reads=[pb], writes=[b_KN])
                    em.op("dve", lambda e, pt=pt: e.tensor_copy(out=WKN[64:128, :, :], in_=pt[0:64, 16:32].rearrange("p (g k) -> p g k", g=4)), reads=[pb], writes=[b_KN])
                    em.op("act", lambda e, nr=nr: e.copy(out=VN[0:ST, :], in_=nr[0:ST, 768:1024]), reads=[nrb], writes=[b_KN])
                    em.op("act", lambda e, nr=nr: e.copy(out=WVs[0:ST, 4, :], in_=nr[0:ST, 1280:1536]), reads=[nrb], writes=[b_WVs])
                    for q in range(4):
                        em.dma("sp", lambda e, q=q, s=s: e.dma_start(out=PG[:, q, :], in_=cache_win[s, q * 128:(q + 1) * 128, :]), writes=[b_PG])
                    for g in range(4):
                        pt, pb = ps_next()
                        for q in range(4):
                            em.op("pe", lambda e, g=g, q=q, pt=pt: e.transpose(pt[0:64, q * 128:(q + 1) * 128], PG[:, q, g * 64:(g + 1) * 64], ident[:]), reads=[b_PG, b_const], writes=[pb])
                        evac(WKs[64:128, g, :], pt[0:64, :], [pb], [b_WKs])
                    em.op("act", lambda e: e.copy(out=WVs[:, 0:4, :], in_=PG[:, :, 256:512]), reads=[b_PG], writes=[b_WVs])
                    win_units = [(0, 512, ("win", 1024)), (512, ST, ("win", 1024 + 512))]
                    for g in range(4):
                        def sel_cb(g=g):
                            select_blocks(ST, [(0, 128, 0, 1, 1e4), (0, 128, 127, 128, 1e4)], 15, ncol=128, use_data=False)
                            em.op("dve", lambda e: e.tensor_copy(out=MASKs[0:ST, g, :], in_=MASK[0:ST, 0:128]), reads=[b_MASK], writes=[b_MASKs])
                        attn_core(
                            ST, qc, lambda h, br, s=s: GATE[0:ST, s, h * 3 + br:h * 3 + br + 1], otm, otm_b, g, [4 * g + r for r in range(4)], q0s, None,
                            CKTs[:, s], lambda j, w1, g=g, s=s: CVs[0:w1, s, j, g, :], b_scmp, 228, 256, ZB,
                            None, None,
                            lambda k0, wu, g=g: ((WKs[64:128, g, k0:k0 + wu], b_WKs) if k0 < 512 else (WKN[64:128, g, 0:wu], b_KN)),
                            lambda k, w1, g=g: (WVs[0:w1, k // 128, g * 64:(g + 1) * 64], b_WVs),
                            [], win_units, (lambda k0, wu: None), sel_cb, QT, b_QT)
                    if s == 0 and lb == 0:
                        dbg("s_otm_cw", otm[0:ST, 0:D], [otm_b])
                        dbg("s_masks", MASKs[0:ST, :, :], [b_MASKs])
                        dbg("s_gate", GATE[0:ST, 0, :], [b_GATE])
                        dbg("s_qt", QT[0:64, :, 0:ST], [b_QT])
                    units = []
                    for u_ in range(17):
                        for h in range(16):
                            g = h // 4
                            u = {"nq": ST}

                            def A(u=u, u_=u_, h=h, g=g, s=s, qc=qc):
                                buf = u_ % 2
                                if h == 0 and u_ < 16:
                                    gather4(cache_sel, s, u_ * 4)
                                    build_unit(None, buf)
                                if u_ >= 14 and "wl" not in u:
                                    wl, wlb = wall()
                                    em.dma("sp", lambda e: e.dma_start(out=wl[0:ST, 0:1024], in_=WS[h][0:ST, 0:1024]), reads=[b_ws], writes=[wlb])
                                    u["wl"], u["wlb"] = wl, wlb
                                wu = 512 if u_ < 16 else ST
                                k0 = u_ * 512
                                ps, pb = ps_next()
                                if u_ < 16:
                                    em.op("pe", lambda e: e.matmul(ps[0:ST, 0:wu], lhsT=QT[0:64, h, qc], rhs=KU[0:64, buf, g, :], start=True, stop=True), reads=[b_QT, b_KUb[buf]], writes=[pb])
                                else:
                                    em.op("pe", lambda e: e.matmul(ps[0:ST, 0:wu], lhsT=QT[0:64, h, qc], rhs=KN[0:64, g, :], start=True, stop=True), reads=[b_QT, b_KN], writes=[pb])
                                pf, pfb = Pt()
                                if u_ >= 14:
                                    wl, wlb = u["wl"], u["wlb"]
                                    cs = max(k0, q0s - 896) - k0
                                    c0 = (k0 + cs) - (q0s - 896)
                                    tt, tb = tmpA()
                                    if cs > 0:
                                        em.op("dve", lambda e: e.tensor_scalar(out=tt[0:ST, 0:cs], in0=ps[0:ST, 0:cs], scalar1=CH[0:ST, h:h + 1], scalar2=None, op0=ALU.add), reads=[pb, b_bias], writes=[tb])
                                    em.op("dve", lambda e: e.tensor_tensor(out=tt[0:ST, cs:wu], in0=ps[0:ST, cs:wu], in1=wl[0:ST, c0:c0 + wu - cs], op=ALU.add), reads=[pb, wlb], writes=[tb])
                                    em.op("act", lambda e: e.activation(out=pf[0:ST, 0:wu], in_=tt[0:ST, 0:wu], func=AF.Exp, bias=NEGM[0:ST, h:h + 1], scale=1.0), reads=[tb, b_bias], writes=[pfb])
                                else:
                                    em.op("act", lambda e: e.activation(out=pf[0:ST, 0:wu], in_=ps[0:ST, 0:wu], func=AF.Exp, bias=NEGMC[0:ST, h:h + 1], scale=1.0), reads=[pb, b_bias], writes=[pfb])
                                p_, p_b = Pt()
                                if u_ < 16:
                                    mk = MASKs[0:ST, g, u_ * 8:u_ * 8 + 8].unsqueeze(2).broadcast_to([ST, 8, 64])
                                    em.op("dve", lambda e: e.scalar_tensor_tensor(out=p_[0:ST, 0:wu].rearrange("p (b k) -> p b k", k=64), in0=pf[0:ST, 0:wu].rearrange("p (b k) -> p b k", k=64), scalar=1.0, in1=mk, op0=ALU.mult, op1=ALU.mult),
                                          reads=[pfb, b_MASKs], writes=[p_b])
                                else:
                                    p_, p_b = pf, pfb
                                u.update(p_=p_, p_b=p_b, wu=wu)

                            def B(u=u):
                                transposeP(u, u["wu"])

                            def C(u=u, u_=u_, h=h, g=g, s=s, otm=otm, otm_b=otm_b):
                                buf = u_ % 2
                                po, pob = ps_next()
                                pT, pTb = u["pT"], u["pTb"]
                                if u_ < 16:
                                    for j in range(4):
                                        em.op("pe", lambda e, j=j: e.matmul(po[0:ST, 0:64], lhsT=pT[:, j * 128:j * 128 + ST], rhs=VU[:, buf, j, g * 64:(g + 1) * 64], start=(j == 0), stop=False), reads=[pTb, b_VUb[buf]], writes=[pob])
                                        em.op("pe", lambda e, j=j: e.matmul(po[0:ST, 64:65], lhsT=pT[:, j * 128:j * 128 + ST], rhs=onesb[:, 0:1], start=False, stop=(j == 3)), reads=[pTb, b_const], writes=[pob])
                                else:
                                    em.op("pe", lambda e: e.matmul(po[0:ST, 0:64], lhsT=pT[0:ST, 0:ST], rhs=VN[0:ST, g * 64:(g + 1) * 64], start=True, stop=False), reads=[pTb, b_KN], writes=[pob])
                                    em.op("pe", lambda e: e.matmul(po[0:ST, 64:65], lhsT=pT[0:ST, 0:ST], rhs=onesb[0:ST, 0:1], start=False, stop=True), reads=[pTb, b_const], writes=[pob])
                                if u_ == 0:
                                    em.op("dve", lambda e: e.tensor_copy(out=OACC[0:ST, h, :], in_=po[0:ST, 0:65]), reads=[pob], writes=[b_OACC])
                                else:
                                    em.op("dve", lambda e: e.tensor_tensor(out=OACC[0:ST, h, :], in0=OACC[0:ST, h, :], in1=po[0:ST, 0:65], op=ALU.add), reads=[pob, b_OACC], writes=[b_OACC])
                                if u_ == 16:
                                    s4, s4b = sm()
                                    em.op("dve", lambda e: e.reciprocal(out=s4[0:ST, 2:3], in_=OACC[0:ST, h, 64:65]), reads=[b_OACC], writes=[s4b])
                                    em.op("dve", lambda e: e.tensor_tensor(out=s4[0:ST, 3:4], in0=s4[0:ST, 2:3], in1=GATE[0:ST, s, h * 3 + 1:h * 3 + 2], op=ALU.mult), reads=[s4b, b_GATE], writes=[s4b])
                                    em.op("dve", lambda e: e.scalar_tensor_tensor(out=otm[0:ST, h * 64:(h + 1) * 64], in0=OACC[0:ST, h, 0:64], scalar=s4[0:ST, 3:4], in1=otm[0:ST, h * 64:(h + 1) * 64], op0=ALU.mult, op1=ALU.add), reads=[b_OACC, s4b, otm_b], writes=[otm_b])
                            units.append((A, B, C))
                    pipeline(units)
                    if s == 0 and lb == 0:
                        dbg("s_otm_all", otm[0:ST, 0:D], [otm_b])
                    otm_to_ZS(ST, otm, otm_b, qc)
                wo_proj(GS, lb, li)

            em.op("act", lambda e: e.copy(out=X[:, :, 0:NS * ST], in_=XSS[:]), reads=[b_XSS], writes=[b_X])
            for lb in range(2):
                sample_attn(lb)
                ffn(GS, 2 + lb)
            final_out(GS, y_s)
        em.finish()
    return nc


_CACHE = {}


def _prep_inputs(inp, c):
    b, half = c // 2, c % 2
    f = np.float32
    xp = np.zeros((TV, D), f)
    valid = np.zeros((1, TV), f)
    vbc = np.zeros((1, 128), f)
    vbw = np.zeros((1, TV), f)
    vlim = np.full((1, 64), 1e30, f)
    f0 = np.full((1, 64), -1.0, f)
    if half == 1:
        xp[:] = inp["x_prompt"][b]
        valid[:] = 1.0
        f0[0, 0] = 1e4
    else:
        xp[2048:] = inp["x_prompt"][b, :2048]
        valid[0, 2048:] = 1.0
        vbc[0, :64] = NEGV
        vbw[0, :2048] = NEGV
        vlim[0, :32] = -1.0
        f0[0, 32] = 1e4
    vec = np.zeros((NVEC, D), f)
    vec[R_BADA:R_BADA + 24] = np.asarray(inp["b_ada"]).reshape(24, D)
    vec[R_NG:R_NG + 8] = np.asarray(inp["norm_g"]).reshape(8, D)
    vec[R_BPW1:R_BPW1 + 4] = np.asarray(inp["b_pw1"]).reshape(4, D)
    vec[R_WDW:R_WDW + 62] = np.asarray(inp["w_dw"]).reshape(62, D)
    vec[R_BDW:R_BDW + 2] = inp["b_dw"]
    vec[R_LNG:R_LNG + 2] = inp["ln_g"]
    vec[R_LNB:R_LNB + 2] = inp["ln_b"]
    vec[R_BPW2:R_BPW2 + 2] = inp["b_pw2"]
    vec[R_GKV] = inp["g_kv"]
    vec[R_FG] = inp["final_g"]
    v2 = np.zeros((67, 128), f)
    for e in range(2):
        v2[e * 32:(e + 1) * 32] = np.concatenate([inp["pe_cmp"][e], inp["pe_cmp"][e]], axis=1)
    v2[64:66] = inp["b_phi1"]
    v2[66] = np.concatenate([inp["b_phi2"][0], inp["b_phi2"][1]])
    m = {
        "xp": xp, "validt": valid, "vecs": vec, "vec2": v2, "vbc": vbc, "vbw": vbw, "vlim": vlim, "f0": f0,
        "xs": np.ascontiguousarray(inp["x_sample"][NS * c:NS * (c + 1)].reshape(NS * ST, D)),
        "cvec": np.ascontiguousarray(np.concatenate([inp["c_prompt"][b:b + 1], inp["c_sample"][NS * c:NS * (c + 1)]], 0)),
        "stconv": np.ascontiguousarray(inp["state_conv"][:, NS * c:NS * (c + 1)]),
    }
    m["cache_cmp"] = np.asarray(inp["cache_kv_cmp"]).reshape(2560 * 128, 512)
    m["cache_sel"] = np.asarray(inp["cache_kv_sel"]).reshape(2560 * 128, 512)
    m["cache_win"] = np.ascontiguousarray(np.asarray(inp["cache_kv_win"])[NS * c:NS * (c + 1)].reshape(NS, 512, 512))
    m["ptab"] = np.ascontiguousarray(np.asarray(inp["page_table"])[NS * c:NS * (c + 1)]).astype(np.int32)
    for k in ("w_ada", "w_pw1", "w_pw2", "w_kv", "w_gate", "w_up", "w_down", "w_phi1", "w_phi2", "b_phi2", "rel_table", "w_qg", "w_o"):
        m[k] = np.asarray(inp[k])
    return m


def kernel(stage=9, cores=None, **inp):
    inp = {k: np.asarray(v) for k, v in inp.items()}
    if stage not in _CACHE:
        _CACHE[stage] = build(stage)
    nc = _CACHE[stage]
    if cores is not None:
        in_maps = [_prep_inputs(inp, c) for c in cores]
        res = run_bass_kernel_spmd(nc, in_maps, core_ids=list(range(len(cores))))
        kernel.raw = res.results
        return None
    in_maps = [_prep_inputs(inp, c) for c in range(8)]
    res = run_bass_kernel_spmd(nc, in_maps, core_ids=list(range(8)))
    R = res.results
    kernel.raw = R
    f = np.float32
    B, T = 4, 4096
    y_p = np.zeros((B, T, D), f)
    y_s = np.zeros((32, ST, D), f)
    rows = np.zeros((B, T, 1536), f)
    rows_s = np.zeros((32, ST, 1536), f)
    conv_p = np.zeros((2, B, 30, D), f)
    conv_s = np.zeros((2, 32, 30, D), f)
    win_s = np.zeros((32, 512, 2, 4, 64), f)
    for c in range(8):
        b, half = c // 2, c % 2
        rows[b, half * 2048:(half + 1) * 2048] = R[c]["rows_p"]
        y_p[b, half * 2048:(half + 1) * 2048] = R[c]["y_p"]
        y_s[NS * c:NS * (c + 1)] = R[c]["y_s"].reshape(NS, ST, D)
        rows_s[NS * c:NS * (c + 1)] = R[c]["rows_s"].reshape(NS, ST, 1536)
        if half == 1:
            conv_p[:, b] = R[c]["conv_p"]
        conv_s[:, NS * c:NS * (c + 1)] = R[c]["conv_s"]
        win_s[NS * c:NS * (c + 1)] = R[c]["win_s"].reshape(NS, 512, 2, 4, 64)
    rows = rows.reshape(B, T, 3, 2, 4, 64)
    rows_s = rows_s.reshape(32, ST, 3, 2, 4, 64)
    return (y_p, y_s, np.ascontiguousarray(rows[:, :, 0]), np.ascontiguousarray(rows_s[:, :, 0]),
            np.ascontiguousarray(rows[:, :, 1]), np.ascontiguousarray(rows_s[:, :, 1]),
            np.ascontiguousarray(rows[:, -512:, 2]), win_s, conv_p, conv_s)
```
